# Optimizing a Trainium2 kernel written in Bass

```python
import math
import jax, jax.numpy as jnp
from jax import lax
import numpy as np

D_MODEL = 1024
BATCH = 2
SEQ = 8192
DEPTH = 2

CHUNK = 64
Q_BLOCK = 128
HEAD_DIM = 64
ROT_FRAC = 4
ROPE_THETA = 500000.0
NORM_EPS = 1e-6
NEG_INF = -1e30

A_HEADS = 8
IDX_HEADS = 8
IDX_DIM = 32
TOPK_MAX = 256
B_HEADS = 4
B_V_DIM = 2 * HEAD_DIM
C_HEADS = D_MODEL // HEAD_DIM
D_FF = 2816
CONV_WIDTH = 3

A_WIDTH = A_HEADS * HEAD_DIM
B_QK_WIDTH = B_HEADS * 2 * HEAD_DIM
B_WIDTH = B_HEADS * B_V_DIM
EVEN_SIZES = (A_WIDTH, A_WIDTH, A_WIDTH, IDX_HEADS * IDX_DIM, IDX_DIM, IDX_HEADS,
              B_QK_WIDTH, B_QK_WIDTH, B_WIDTH)
EVEN_IN = sum(EVEN_SIZES)
ODD_SIZES = (D_MODEL, D_MODEL, D_MODEL, C_HEADS)
ODD_IN = sum(ODD_SIZES)
N_EVEN = (DEPTH + 1) // 2
N_ODD = DEPTH // 2

kernel_name = 'hybrid_dsa_diff_fox_convffn'

F32 = jnp.float32


def _rms(x, g):
    xf = x.astype(F32)
    y = xf * lax.rsqrt(jnp.mean(xf * xf, axis=-1, keepdims=True) + NORM_EPS)
    return (y * g.astype(F32)).astype(x.dtype)


def _split(h, sizes):
    cuts = [int(c) for c in np.cumsum(sizes)[:-1]]
    return jnp.split(h, cuts, axis=-1)


def _rope(x, pos):
    rot = x.shape[-1] // ROT_FRAC
    half = rot // 2
    inv = ROPE_THETA ** (-jnp.arange(half, dtype=F32) / half)
    ang = pos.astype(F32)[:, None] * inv[None, :]
    cos = jnp.cos(ang)[:, None, :]
    sin = jnp.sin(ang)[:, None, :]
    xr = x[..., :rot].astype(F32)
    x1, x2 = xr[..., :half], xr[..., half:]
    r = jnp.concatenate([x1 * cos - x2 * sin, x2 * cos + x1 * sin], axis=-1)
    return jnp.concatenate([r.astype(x.dtype), x[..., rot:]], axis=-1)


def _to_blocks(a):
    b, s = a.shape[:2]
    a = a.reshape((b, s // Q_BLOCK, Q_BLOCK) + a.shape[2:])
    return jnp.moveaxis(a, 1, 0)


def _from_blocks(a):
    a = jnp.moveaxis(a, 0, 1)
    return a.reshape((a.shape[0], -1) + a.shape[3:])


def _dsa_attention(q, k, v, qi, ki, wi, pos):
    seq = q.shape[1]
    topk = min(TOPK_MAX, seq // 4)
    key_chunk = pos // CHUNK
    scale = HEAD_DIM ** -0.5

    def block(args):
        qb, qib, wib, pb = args
        qc = pb // CHUNK
        adm = key_chunk[None, :] <= qc[:, None]
        idx_logit = jnp.einsum('bqhd,bsd->bqhs', qib, ki, preferred_element_type=F32)
        score = jnp.einsum('bqhs,bqh->bqs', jax.nn.relu(idx_logit), wib.astype(F32))
        score = jnp.where(adm[None], score, NEG_INF)
        _, sel = lax.top_k(score, topk)
        valid = key_chunk[sel] <= qc[None, :, None]
        kg = jax.vmap(lambda kb, ib: kb[ib])(k, sel)
        vg = jax.vmap(lambda vb, ib: vb[ib])(v, sel)
        logit = jnp.einsum('bqhd,bqkhd->bhqk', qb, kg, preferred_element_type=F32) * scale
        logit = jnp.where(valid[:, None], logit, NEG_INF)
        p = jax.nn.softmax(logit, axis=-1).astype(v.dtype)
        return jnp.einsum('bhqk,bqkhd->bqhd', p, vg)

    out = lax.map(block, (_to_blocks(q), _to_blocks(qi), _to_blocks(wi),
                          pos.reshape(-1, Q_BLOCK)))
    return _from_blocks(out)


def _diff_attention(q, k, v, lam, pos):
    key_chunk = pos // CHUNK
    scale = HEAD_DIM ** -0.5

    def block(args):
        qb, pb = args
        adm = key_chunk[None, :] <= (pb // CHUNK)[:, None]
        logit = jnp.einsum('bqhmd,bshmd->bhmqs', qb, k, preferred_element_type=F32) * scale
        logit = jnp.where(adm, logit, NEG_INF)
        p = jax.nn.softmax(logit, axis=-1)
        w = p[:, :, 0] - lam * p[:, :, 1]
        return jnp.einsum('bhqs,bshd->bqhd', w.astype(v.dtype), v)

    out = lax.map(block, (_to_blocks(q), pos.reshape(-1, Q_BLOCK)))
    return _from_blocks(out)


def _forgetting_attention(q, k, v, log_f, pos):
    scale = HEAD_DIM ** -0.5
    c = jnp.cumsum(log_f, axis=1)
    c_keys = jnp.moveaxis(c, 1, 2)

    def block(args):
        qb, cb, pb = args
        causal = pos[None, :] <= pb[:, None]
        bias = jnp.moveaxis(cb, 1, 2)[..., None] - c_keys[:, :, None, :]
        logit = jnp.einsum('bqhd,bshd->bhqs', qb, k, preferred_element_type=F32) * scale + bias
        logit = jnp.where(causal, logit, NEG_INF)
        p = jax.nn.softmax(logit, axis=-1).astype(v.dtype)
        return jnp.einsum('bhqs,bshd->bqhd', p, v)

    out = lax.map(block, (_to_blocks(q), _to_blocks(c), pos.reshape(-1, Q_BLOCK)))
    return _from_blocks(out)


def _even_mixer(h, pos, layer, w_in, w_out, a_qn, a_kn, idx_kn, b_qn, b_kn,
                lq1, lk1, lq2, lk2, b_subln):
    b, s, _ = h.shape
    aq, ak, av, iq, ik, iw, bq, bk, bv = _split(h @ w_in, EVEN_SIZES)
    aq = _rope(_rms(aq.reshape(b, s, A_HEADS, HEAD_DIM), a_qn), pos)
    ak = _rope(_rms(ak.reshape(b, s, A_HEADS, HEAD_DIM), a_kn), pos)
    av = av.reshape(b, s, A_HEADS, HEAD_DIM)
    iq = _rope(iq.reshape(b, s, IDX_HEADS, IDX_DIM), pos)
    ik = _rope(_rms(ik, idx_kn)[:, :, None, :], pos)[:, :, 0]
    iw = iw * (IDX_HEADS ** -0.5 * IDX_DIM ** -0.5)
    a_out = _dsa_attention(aq, ak, av, iq, ik, iw, pos).reshape(b, s, A_WIDTH)
    lam_init = 0.8 - 0.6 * math.exp(-0.3 * layer)
    lam = (jnp.exp(jnp.sum(lq1.astype(F32) * lk1.astype(F32)))
           - jnp.exp(jnp.sum(lq2.astype(F32) * lk2.astype(F32))) + lam_init)
    bq = _rope(_rms(bq.reshape(b, s, 2 * B_HEADS, HEAD_DIM), b_qn), pos).reshape(b, s, B_HEADS, 2, HEAD_DIM)
    bk = _rope(_rms(bk.reshape(b, s, 2 * B_HEADS, HEAD_DIM), b_kn), pos).reshape(b, s, B_HEADS, 2, HEAD_DIM)
    bv = bv.reshape(b, s, B_HEADS, B_V_DIM)
    b_out = _rms(_diff_attention(bq, bk, bv, lam, pos), b_subln) * (1.0 - lam_init)
    b_out = b_out.reshape(b, s, B_WIDTH)
    return jnp.concatenate([a_out, b_out], axis=-1) @ w_out


def _odd_mixer(h, pos, w_in, b_f, w_out, c_qn, c_kn):
    b, s, _ = h.shape
    q, k, v, fg = _split(h @ w_in, ODD_SIZES)
    q = _rms(q.reshape(b, s, C_HEADS, HEAD_DIM), c_qn)
    k = _rms(k.reshape(b, s, C_HEADS, HEAD_DIM), c_kn)
    v = v.reshape(b, s, C_HEADS, HEAD_DIM)
    log_f = jax.nn.log_sigmoid((fg + b_f).astype(F32))
    out = _forgetting_attention(q, k, v, log_f, pos)
    return out.reshape(b, s, D_MODEL) @ w_out


def _conv_ffn(h, w_up, w_conv, b_conv, w_down):
    u = h @ w_up
    s = u.shape[1]
    up = jnp.pad(u, ((0, 0), (CONV_WIDTH - 1, 0), (0, 0)))
    conv = b_conv
    for j in range(CONV_WIDTH):
        conv = conv + up[:, j:j + s] * w_conv[j]
    g, val = jnp.split(conv, 2, axis=-1)
    return (jax.nn.silu(g) * val) @ w_down


def setup_inputs(seed: int = 0) -> dict:
    key = jax.random.key(seed)
    ks = iter(jax.random.split(key, 32))

    def nrm(shape, scale):
        return jax.random.normal(next(ks), shape, F32) * scale

    def gain(shape):
        return 1.0 + nrm(shape, 0.05)

    return {
        'x': nrm((BATCH, SEQ, D_MODEL), 1.0),
        'ln_mix': gain((DEPTH, D_MODEL)),
        'ln_ffn': gain((DEPTH, D_MODEL)),
        'ev_w_in': nrm((N_EVEN, D_MODEL, EVEN_IN), D_MODEL ** -0.5),
        'ev_w_out': nrm((N_EVEN, A_WIDTH + B_WIDTH, D_MODEL), (A_WIDTH + B_WIDTH) ** -0.5),
        'ev_a_qnorm': gain((N_EVEN, HEAD_DIM)),
        'ev_a_knorm': gain((N_EVEN, HEAD_DIM)),
        'ev_idx_knorm': gain((N_EVEN, IDX_DIM)),
        'ev_b_qnorm': gain((N_EVEN, HEAD_DIM)),
        'ev_b_knorm': gain((N_EVEN, HEAD_DIM)),
        'ev_lam_q1': nrm((N_EVEN, HEAD_DIM), 0.1),
        'ev_lam_k1': nrm((N_EVEN, HEAD_DIM), 0.1),
        'ev_lam_q2': nrm((N_EVEN, HEAD_DIM), 0.1),
        'ev_lam_k2': nrm((N_EVEN, HEAD_DIM), 0.1),
        'ev_b_subln': gain((N_EVEN, B_V_DIM)),
        'od_w_in': nrm((N_ODD, D_MODEL, ODD_IN), D_MODEL ** -0.5),
        'od_b_f': 3.0 + nrm((N_ODD, C_HEADS), 0.5),
        'od_w_out': nrm((N_ODD, D_MODEL, D_MODEL), D_MODEL ** -0.5),
        'od_c_qnorm': gain((N_ODD, HEAD_DIM)),
        'od_c_knorm': gain((N_ODD, HEAD_DIM)),
        'ffn_up': nrm((DEPTH, D_MODEL, 2 * D_FF), D_MODEL ** -0.5),
        'ffn_conv': nrm((DEPTH, CONV_WIDTH, 2 * D_FF), CONV_WIDTH ** -0.5),
        'ffn_conv_b': nrm((DEPTH, 2 * D_FF), 0.01),
        'ffn_down': nrm((DEPTH, D_FF, D_MODEL), D_FF ** -0.5),
    }


def reference(x, ln_mix, ln_ffn, ev_w_in, ev_w_out, ev_a_qnorm, ev_a_knorm, ev_idx_knorm,
              ev_b_qnorm, ev_b_knorm, ev_lam_q1, ev_lam_k1, ev_lam_q2, ev_lam_k2, ev_b_subln,
              od_w_in, od_b_f, od_w_out, od_c_qnorm, od_c_knorm,
              ffn_up, ffn_conv, ffn_conv_b, ffn_down):
    seq = x.shape[1]
    pos = jnp.arange(seq, dtype=jnp.int32)
    for i in range(DEPTH):
        j = i // 2
        h = _rms(x, ln_mix[i])
        if i % 2 == 0:
            mix = _even_mixer(h, pos, i, ev_w_in[j], ev_w_out[j], ev_a_qnorm[j], ev_a_knorm[j],
                              ev_idx_knorm[j], ev_b_qnorm[j], ev_b_knorm[j], ev_lam_q1[j],
                              ev_lam_k1[j], ev_lam_q2[j], ev_lam_k2[j], ev_b_subln[j])
        else:
            mix = _odd_mixer(h, pos, od_w_in[j], od_b_f[j], od_w_out[j], od_c_qnorm[j], od_c_knorm[j])
        x = x + mix
        x = x + _conv_ffn(_rms(x, ln_ffn[i]), ffn_up[i], ffn_conv[i], ffn_conv_b[i], ffn_down[i])
    return x
```

```python
import math
from contextlib import ExitStack
import numpy as np
import concourse.bass as bass
import concourse.mybir as mybir
from concourse.bass_utils import run_bass_kernel_spmd


F32 = mybir.dt.float32
BF16 = mybir.dt.bfloat16
I32 = mybir.dt.int32
FP8 = mybir.dt.float8e4
ALU = mybir.AluOpType
AF = mybir.ActivationFunctionType
AX = mybir.AxisListType

ENGS = ["pe", "act", "dve", "pool", "sp"]
N_DMA_SEMS = 32
N_SW_SEMS = 8


class Prog:
    def __init__(self, nc, es, same_engine_sync=True):
        self.nc = nc
        self.es = es
        self.same = same_engine_sync
        self.sems = {}
        for e in ENGS:
            self.sems["e_" + e] = es.enter_context(nc.semaphore("se_" + e))
        self.cnt = {e: 0 for e in ENGS}
        self.known = {e: {} for e in ENGS}
        self.ops = {e: [] for e in ENGS}
        self.reg = {}
        self.dsem = []
        for i in range(N_DMA_SEMS):
            nm = "d_%d" % i
            self.sems[nm] = es.enter_context(nc.semaphore("sd_%d" % i))
            self.dsem.append([nm, 0])
        self.drr = 0
        self.drr_sw = 0
        self.n_inst = 0
        self.cur_es = es
        self.csem = []
        self.n_coll = 0

    def phase(self):
        prog = self

        class _Ph:
            def __enter__(self_):
                self_.st = ExitStack()
                self_.prev = prog.cur_es
                prog.cur_es = self_.st
                return self_

            def __exit__(self_, *a):
                prog.barrier()
                prog.cur_es = self_.prev
                self_.st.close()
                return False
        return _Ph()

    def barrier(self):
        targets = {}
        for e in ENGS:
            if self.cnt[e] > 0:
                targets["e_" + e] = self.cnt[e]
        for nm, v in self.dsem:
            if v > 0:
                targets[nm] = v
        for e in ENGS:
            waits = []
            for s_, v in targets.items():
                if s_ == "e_" + e:
                    continue
                if self.known[e].get(s_, 0) < v:
                    waits.append((s_, v))
                    self.known[e][s_] = v
            if waits:
                self.ops[e].append((waits, None, None))

    def coll(self, src_t, dst_t, groups, reads=(), writes=()):
        waits = self._deps("pool", reads, writes)
        nm = "c_%d" % self.n_coll
        self.n_coll += 1
        self.sems[nm] = self.es.enter_context(self.nc.semaphore("sc_%d" % (self.n_coll - 1)))
        self.csem.append(nm)
        tag = (nm, 1)
        self._mark(reads, writes, tag)
        kw = dict(replica_groups=groups, ins=[src_t.ap().opt()], outs=[dst_t.ap().opt()])
        self.ops["pool"].append((waits, ("collective_compute", ("AllGather", ALU.bypass), kw), (nm, 1)))
        self.n_inst += 1

    def sb(self, name, shape, dt):
        return self.cur_es.enter_context(self.nc.sbuf_tensor(name, list(shape), dt))

    def ps(self, name, shape, dt):
        return self.cur_es.enter_context(self.nc.psum_tensor(name, list(shape), dt))

    def _deps(self, eng, reads, writes):
        deps = {}

        def add(sv):
            if sv is None:
                return
            s, v = sv
            if deps.get(s, 0) < v:
                deps[s] = v

        for k in reads:
            st = self.reg.get(k)
            if st is not None:
                add(st[0])
        for k in writes:
            st = self.reg.get(k)
            if st is not None:
                add(st[0])
                for s, v in st[1].items():
                    add((s, v))
        waits = []
        me = "e_" + eng
        for s, v in deps.items():
            if s == me and (eng == "pe" or not self.same):
                continue
            if self.known[eng].get(s, 0) < v:
                waits.append((s, v))
                self.known[eng][s] = v
        return waits

    def _mark(self, reads, writes, tag):
        for k in writes:
            self.reg[k] = [tag, {}]
        for k in reads:
            st = self.reg.setdefault(k, [None, {}])
            s, v = tag
            if st[1].get(s, 0) < v:
                st[1][s] = v

    def op(self, eng, meth, *args, reads=(), writes=(), **kw):
        fn = (meth, args, kw)
        waits = self._deps(eng, reads, writes)
        self.cnt[eng] += 1
        tag = ("e_" + eng, self.cnt[eng])
        self._mark(reads, writes, tag)
        self.ops[eng].append((waits, fn, ("e_" + eng, 1)))
        self.n_inst += 1

    def dma(self, q, out, in_, reads=(), writes=(), **kw):
        waits = self._deps(q, reads, writes)
        if q == "pool":
            ent = self.dsem[N_DMA_SEMS - N_SW_SEMS + self.drr_sw]
            self.drr_sw = (self.drr_sw + 1) % N_SW_SEMS
        else:
            ent = self.dsem[self.drr]
            self.drr = (self.drr + 1) % (N_DMA_SEMS - N_SW_SEMS)
        nm, prev = ent
        if prev > 0 and self.known[q].get(nm, 0) < prev:
            waits.append((nm, prev))
            self.known[q][nm] = prev
        ent[1] = prev + 16
        tag = (nm, prev + 16)
        self._mark(reads, writes, tag)
        kw = dict(kw); kw["out"] = out; kw["in_"] = in_
        self.ops[q].append((waits, ("dma_start", (), kw), (nm, 16)))
        self.n_inst += 1

    def finish(self):
        waits = []
        for nm, v in self.dsem:
            if v > 0 and self.known["sp"].get(nm, 0) < v:
                waits.append((nm, v))
        self.ops["sp"].append((waits, None, None))

    def emit(self):
        nc = self.nc
        block = self.es.enter_context(nc.Block())
        names = {"pe": "tensor", "act": "scalar", "dve": "vector", "pool": "gpsimd", "sp": "sync"}

        def mk(e):
            def body(engine):
                for waits, fn, inc in self.ops[e]:
                    for s, v in waits:
                        engine.wait_ge(self.sems[s], v)
                    if fn is not None:
                        inst = getattr(engine, fn[0])(*fn[1], **fn[2])
                        if inc is not None:
                            inst.then_inc(self.sems[inc[0]], inc[1])
            return body

        for e in ENGS:
            getattr(block, names[e])(mk(e))


D = 1024
EPS = 1e-6
THETA = 500000.0
TWO_PI = 2.0 * math.pi
C1 = 6.28125
C2 = TWO_PI - 6.28125


def build_consts(P):
    C = {}
    it = P.sb("c_iota_i", [128, 128], I32)
    itf = P.sb("c_iota_f", [128, 128], F32)
    ident = P.sb("c_ident", [128, 128], BF16)
    identf = P.sb("c_identf", [128, 128], F32)
    P.op("pool", "iota", it[:], pattern=[[1, 128]], base=0, channel_multiplier=-1, writes=["c_iota_i"])
    P.op("dve", "tensor_copy", itf[:], it[:], reads=["c_iota_i"], writes=["c_iota_f"])
    P.op("dve", "tensor_single_scalar", ident[:], itf[:], 0.0, op=ALU.is_equal, reads=["c_iota_f"], writes=["c_ident"])
    P.op("dve", "tensor_single_scalar", identf[:], itf[:], 0.0, op=ALU.is_equal, reads=["c_iota_f"], writes=["c_identf"])
    eps = P.sb("c_eps", [128, 1], F32)
    one = P.sb("c_one", [128, 1], F32)
    P.op("dve", "memset", eps[:], EPS, writes=["c_eps"])
    P.op("dve", "memset", one[:], 1.0, writes=["c_one"])
    C["ident"] = ident
    C["identf"] = identf
    C["eps"] = eps
    C["one"] = one
    return C


def rope_tables(P, posf, kp, NT, half, name):
    cos = P.sb(name + "_cos", [128, NT, half], F32)
    sin = P.sb(name + "_sin", [128, NT, half], F32)
    ang = P.sb(name + "_ang", [128, NT, half], F32)
    kf = P.sb(name + "_kf", [128, NT, half], F32)
    ki = P.sb(name + "_ki", [128, NT, half], I32)
    r = P.sb(name + "_r", [128, NT, half], F32)
    m = P.sb(name + "_m", [128, NT, half], F32)
    ka, kk, kr, km = name + "_ang", name + "_kf", name + "_r", name + "_m"
    for i in range(half):
        inv = float(np.float32(THETA ** (-float(i) / half)))
        P.op("dve", "tensor_scalar", ang[:, :, i], posf[:], inv, None, op0=ALU.mult, reads=[kp], writes=[ka])
    for which, dst, shift in (("s", sin, 0.0), ("c", cos, math.pi / 2)):
        kd = name + ("_sin" if which == "s" else "_cos")
        P.op("dve", "tensor_scalar", kf[:], ang[:], shift, 1.0 / TWO_PI, op0=ALU.add, op1=ALU.mult, reads=[ka], writes=[kk])
        P.op("dve", "tensor_copy", ki[:], kf[:], reads=[kk], writes=[name + "_ki"])
        P.op("dve", "tensor_copy", kf[:], ki[:], reads=[name + "_ki"], writes=[kk])
        P.op("dve", "scalar_tensor_tensor", out=r[:], in0=kf[:], scalar=-C1, in1=ang[:], op0=ALU.mult, op1=ALU.add, reads=[kk, ka], writes=[kr])
        P.op("dve", "scalar_tensor_tensor", out=r[:], in0=kf[:], scalar=-C2, in1=r[:], op0=ALU.mult, op1=ALU.add, reads=[kk, kr], writes=[kr])
        if shift != 0.0:
            P.op("dve", "tensor_scalar", r[:], r[:], shift, None, op0=ALU.add, reads=[kr], writes=[kr])
        P.op("dve", "tensor_scalar", m[:], r[:], math.pi, -TWO_PI, op0=ALU.is_gt, op1=ALU.mult, reads=[kr], writes=[km])
        P.op("dve", "tensor_tensor", out=r[:], in0=r[:], in1=m[:], op=ALU.add, reads=[kr, km], writes=[kr])
        P.op("dve", "tensor_scalar", m[:], r[:], -math.pi, TWO_PI, op0=ALU.is_lt, op1=ALU.mult, reads=[kr], writes=[km])
        P.op("dve", "tensor_tensor", out=r[:], in0=r[:], in1=m[:], op=ALU.add, reads=[kr, km], writes=[kr])
        P.op("dve", "tensor_scalar", r[:], r[:], -3.14159, 3.14159, op0=ALU.max, op1=ALU.min, reads=[kr], writes=[kr])
        P.op("act", "activation", dst[:], r[:], AF.Sin, reads=[kr], writes=[kd])
    return cos, sin


def load_weight_bf16(P, w_dram, ncol, name, stg=None, stg_keys=None, kchunks=8, engines=None):
    wb = P.sb(name, [128, kchunks, ncol], BF16)
    wv = w_dram.rearrange("(kc p) n -> p kc n", p=128)
    n0 = 0
    while n0 < ncol:
        n1 = min(n0 + 512, ncol)
        P.dma("pool", wb[:, :, n0:n1], wv[:, :, n0:n1], writes=[(name, n0)])
        n0 = n1
    return wb


def wkeys(name, n0, n1):
    return [(name, c) for c in range((n0 // 512) * 512, n1, 512)]


def emit_proj(P, C, layer, io, NT):
    nc = P.nc
    NCOL = 3368 if layer == 0 else 3088
    NG = NT // 4
    ident, identf, eps, one = C["ident"], C["identf"], C["eps"], C["one"]
    pfx = "p%d_" % layer

    wb = load_weight_bf16(P, io["w_in"], NCOL, pfx + "w")
    lnb = P.sb(pfx + "lnb", [128, D], F32)
    P.dma("sp", lnb[:], io["ln"].partition_broadcast(128), writes=[pfx + "lnb"])
    G = P.sb(pfx + "G", [128, 32, 64], F32)
    G4 = P.sb(pfx + "G4", [128, 4, 64], F32)
    gl = io["gains"]
    for i, g in enumerate(gl):
        P.dma("sp", G4[:, i, :], g.partition_broadcast(128), writes=[(pfx + "G4", i)])
    per = 32 // len(gl)
    for i in range(len(gl)):
        P.op("dve", "tensor_copy", G[:, i * per:(i + 1) * per, :], G4[:, i, :].unsqueeze(1).to_broadcast([128, per, 64]), reads=[(pfx + "G4", i)], writes=[(pfx + "G", i)])
    Gkeys = [(pfx + "G", i) for i in range(len(gl))]
    posf = P.sb(pfx + "posf", [128, NT], F32)
    P.dma("sp", posf[:], io["pos"], writes=[pfx + "posf"])
    if layer == 0:
        cos8, sin8 = rope_tables(P, posf, pfx + "posf", NT, 8, pfx + "r8")
        cos4, sin4 = rope_tables(P, posf, pfx + "posf", NT, 4, pfx + "r4")
        gik = P.sb(pfx + "gik", [128, 32], F32)
        P.dma("sp", gik[:], io["g_ik"].partition_broadcast(128), writes=[pfx + "gik"])
        iw_all = P.sb(pfx + "iw_all", [128, NT, 8], F32)
    else:
        bfb = P.sb(pfx + "bfb", [128, 16], F32)
        P.dma("sp", bfb[:], io["b_f"].partition_broadcast(128), writes=[pfx + "bfb"])
        lfT = P.sb(pfx + "lfT", [16, NT * 128], F32)

    xt = [P.sb(pfx + "x%d" % i, [128, D], F32) for i in range(2)]
    junk = P.sb(pfx + "junk", [128, D], BF16)
    hb = [P.sb(pfx + "h%d" % i, [128, D], BF16) for i in range(2)]
    hT = [P.sb(pfx + "hT%d" % i, [128, 8, 128], BF16) for i in range(2)]
    pj = [P.sb(pfx + "pj%d" % i, [128, NCOL], F32) for i in range(2)]
    st_l = [P.sb(pfx + "st%d" % i, [128, 8], F32) for i in range(2)]
    tmpA_l = [P.sb(pfx + "tmpA%d" % i, [128, 32, 64], F32) for i in range(2)]
    hst_l = [P.sb(pfx + "hst%d" % i, [128, 3, 32], F32) for i in range(2)]
    rt_l = [P.sb(pfx + "rt%d" % i, [128, 4, 32, 8], F32) for i in range(2)]
    nr = [P.sb(pfx + "nr%d" % i, [128, 32, 64], BF16) for i in range(2)]
    vt = [P.sb(pfx + "vt%d" % i, [128, 1024], BF16) for i in range(2)]
    qkT_g = P.sb(pfx + "qkTg", [64, 32, 512], BF16)
    tp = [P.ps(pfx + "tp%d" % i, [128, 8, 128], BF16) for i in range(2)]
    mm = [P.ps(pfx + "mm%d" % i, [128, 512], F32) for i in range(3)]
    if layer == 0:
        tI = P.sb(pfx + "tI", [128, 9, 32], F32)
        rti_l = [P.sb(pfx + "rti%d" % i, [128, 4, 9, 4], F32) for i in range(2)]
        ihi = P.sb(pfx + "ihi", [128, 9, 32], BF16)
        ilo = P.sb(pfx + "ilo", [128, 9, 32], BF16)
        i3 = [P.sb(pfx + "i3_%d" % i, [128, 9, 96], BF16) for i in range(2)]
        iqT_g = P.sb(pfx + "iqTg", [96, 8, 512], BF16)
        ikT_g = P.sb(pfx + "ikTg", [96, 512], BF16)
    else:
        lft = P.sb(pfx + "lft", [128, 4, 16], F32)
        tpf = P.ps(pfx + "tpf", [16, 128], F32)

    def K(n, i=None):
        return (pfx + n) if i is None else (pfx + n, i)

    tpc = 0

    def tile_body(t):
        nonlocal tpc
        b = t % 2
        g, tl = t // 4, t % 4
        st, tmpA, hst, rt = st_l[b], tmpA_l[b], hst_l[b], rt_l[b]

        def KK(n, i=None, b=b):
            return K(n + "_b%d" % b, i)
        x = xt[b]
        kx = K("x", b)
        P.dma("sp", x[:], io["x"][t * 128:(t + 1) * 128, :], writes=[kx])
        P.op("act", "activation", junk[:], x[:], AF.Square, accum_out=st[:, 0:1], reads=[kx], writes=[K("junk"), KK("st", 0)])
        P.op("act", "activation", st[:, 1:2], st[:, 0:1], AF.Sqrt, bias=eps[:], scale=1.0 / D, reads=[KK("st", 0), "c_eps"], writes=[KK("st", 1)])
        P.op("dve", "reciprocal", st[:, 2:3], st[:, 1:2], reads=[KK("st", 1)], writes=[KK("st", 2)])
        h = hb[b]
        P.op("dve", "scalar_tensor_tensor", out=h[:], in0=x[:], scalar=st[:, 2:3], in1=lnb[:], op0=ALU.mult, op1=ALU.mult, reads=[kx, KK("st", 2), K("lnb")], writes=[K("h", b)])
        tpp = tp[tpc % 2]
        ktp = K("tp", tpc % 2)
        tpc += 1
        for kc in range(8):
            P.op("pe", "transpose", tpp[:, kc, :], h[:, kc * 128:(kc + 1) * 128], ident[:], reads=[K("h", b), "c_ident"], writes=[ktp])
        hTt = hT[b]
        P.op("act", "copy", hTt[:], tpp[:], reads=[ktp], writes=[K("hT", b)])
        yield "A"
        pjt = pj[b]
        n0 = 0
        gi = 0
        while n0 < NCOL:
            n1 = min(n0 + 512, NCOL)
            m = mm[gi % 3]
            km = K("mm", gi % 3)
            for kc in range(8):
                P.op("pe", "matmul", m[:, 0:n1 - n0], lhsT=hTt[:, kc, :], rhs=wb[:, kc, n0:n1], start=(kc == 0), stop=(kc == 7), reads=[K("hT", b)] + wkeys(pfx + "w", n0, n1), writes=[km])
            if gi % 2 == 0:
                P.op("act", "copy", pjt[:, n0:n1], m[:, 0:n1 - n0], reads=[km], writes=[K("pj", b)])
            else:
                P.op("dve", "tensor_copy", pjt[:, n0:n1], m[:, 0:n1 - n0], reads=[km], writes=[K("pj", b)])
            n0 = n1
            gi += 1
            yield "B"
        yield "Bend"
        kpj = K("pj", b)
        nrt = nr[b]
        knr = K("nr", b)

        def normrope(src, H, h0, rope_half, cos, sin, Gsl, Gk):
            ta = tmpA[:, h0:h0 + H, :]
            kta = KK("tmpA", h0)
            P.op("act", "activation", ta, src, AF.Square, reads=[kpj], writes=[kta])
            yield "C"
            P.op("dve", "tensor_reduce", out=hst[:, 0, h0:h0 + H], in_=ta, axis=AX.X, op=ALU.add, reads=[kta], writes=[KK("hst", h0)])
            yield "C"
            P.op("act", "activation", hst[:, 1, h0:h0 + H], hst[:, 0, h0:h0 + H], AF.Sqrt, bias=eps[:], scale=1.0 / 64, reads=[KK("hst", h0), "c_eps"], writes=[KK("hst1", h0)])
            yield "C"
            P.op("dve", "reciprocal", hst[:, 2, h0:h0 + H], hst[:, 1, h0:h0 + H], reads=[KK("hst1", h0)], writes=[KK("hst2", h0)])
            yield "C"
            P.op("dve", "tensor_tensor", out=ta, in0=src, in1=hst[:, 2, h0:h0 + H].unsqueeze(2).to_broadcast([128, H, 64]), op=ALU.mult, reads=[kpj, KK("hst2", h0)], writes=[kta])
            yield "C"
            P.op("dve", "tensor_tensor", out=ta, in0=ta, in1=Gsl, op=ALU.mult, reads=[kta] + Gk, writes=[kta])
            yield "C"
            if rope_half:
                hf = rope_half
                x1 = tmpA[:, h0:h0 + H, 0:hf]
                x2 = tmpA[:, h0:h0 + H, hf:2 * hf]
                cb = cos[:, t, :].unsqueeze(1).to_broadcast([128, H, hf])
                sb_ = sin[:, t, :].unsqueeze(1).to_broadcast([128, H, hf])
                r = [rt[:, i, h0:h0 + H, :] for i in range(4)]
                krt = KK("rt", h0)
                P.op("dve", "tensor_tensor", out=r[0], in0=x1, in1=cb, op=ALU.mult, reads=[kta], writes=[(krt, 0)])
                yield "C"
                P.op("dve", "tensor_tensor", out=r[1], in0=x2, in1=sb_, op=ALU.mult, reads=[kta], writes=[(krt, 1)])
                yield "C"
                P.op("dve", "tensor_tensor", out=r[2], in0=x2, in1=cb, op=ALU.mult, reads=[kta], writes=[(krt, 2)])
                yield "C"
                P.op("dve", "tensor_tensor", out=r[3], in0=x1, in1=sb_, op=ALU.mult, reads=[kta], writes=[(krt, 3)])
                yield "C"
                P.op("dve", "tensor_tensor", out=nrt[:, h0:h0 + H, 0:hf], in0=r[0], in1=r[1], op=ALU.subtract, reads=[(krt, 0), (krt, 1)], writes=[(knr, h0, 0)])
                yield "C"
                P.op("dve", "tensor_tensor", out=nrt[:, h0:h0 + H, hf:2 * hf], in0=r[2], in1=r[3], op=ALU.add, reads=[(krt, 2), (krt, 3)], writes=[(knr, h0, 1)])
                yield "C"
                P.op("act", "copy", nrt[:, h0:h0 + H, 2 * hf:64], tmpA[:, h0:h0 + H, 2 * hf:64], reads=[kta], writes=[(knr, h0, 2)])
                yield "C"
                return [(knr, h0, 0), (knr, h0, 1), (knr, h0, 2)]
            else:
                P.op("act", "copy", nrt[:, h0:h0 + H, :], ta, reads=[kta], writes=[(knr, h0, 2)])
                yield "C"
                return [(knr, h0, 2)]

        if layer == 0:
            k1 = yield from normrope(pjt[:, 0:1024].rearrange("p (h d) -> p h d", d=64), 16, 0, 8, cos8, sin8, G[:, 0:16, :], Gkeys[0:2])
            k2 = yield from normrope(pjt[:, 1832:2856].rearrange("p (h d) -> p h d", d=64), 16, 16, 8, cos8, sin8, G[:, 16:32, :], Gkeys[2:4])
            nrkeys = {0: k1, 16: k2}
        else:
            k1 = yield from normrope(pjt[:, 0:1024].rearrange("p (h d) -> p h d", d=64), 16, 0, 0, None, None, G[:, 0:16, :], Gkeys[0:1])
            k2 = yield from normrope(pjt[:, 1024:2048].rearrange("p (h d) -> p h d", d=64), 16, 16, 0, None, None, G[:, 16:32, :], Gkeys[1:2])
            nrkeys = {0: k1, 16: k2}
        for q8 in range(4):
            tpp = tp[tpc % 2]
            ktp = K("tp", tpc % 2)
            tpc += 1
            for j in range(8):
                hh = q8 * 8 + j
                P.op("pe", "transpose", tpp[0:64, j, :], nrt[:, hh, :], ident[:], reads=nrkeys[(hh // 16) * 16] + ["c_ident"], writes=[ktp])
                yield "C"
            eng = "act" if q8 % 2 == 0 else "dve"
            if eng == "act":
                P.op("act", "copy", qkT_g[:, q8 * 8:(q8 + 1) * 8, tl * 128:(tl + 1) * 128], tpp[0:64, :, :], reads=[ktp], writes=[K("qkTg", q8)])
                yield "C"
            else:
                P.op("dve", "tensor_copy", qkT_g[:, q8 * 8:(q8 + 1) * 8, tl * 128:(tl + 1) * 128], tpp[0:64, :, :], reads=[ktp], writes=[K("qkTg", q8)])
                yield "C"
        vtt = vt[b]
        if layer == 0:
            P.op("dve", "tensor_copy", vtt[:, 0:512], pjt[:, 1024:1536], reads=[kpj], writes=[K("vt", b)])
            yield "C"
            P.op("dve", "tensor_copy", vtt[:, 512:1024], pjt[:, 2856:3368], reads=[kpj], writes=[K("vt", b)])
            yield "C"
        else:
            P.op("dve", "tensor_copy", vtt[:], pjt[:, 2048:3072], reads=[kpj], writes=[K("vt", b)])
            yield "C"
        gkeys = io.setdefault("grp_keys", {}).setdefault(g, [])
        if layer == 0:
            P.dma("act", io["va_p"][g][:, :, tl, :], vtt[:, 0:512].rearrange("p (h d) -> p h d", d=64), reads=[K("vt", b)], writes=[("dram_va", t)])
            yield "C"
            P.dma("act", io["vb_p"][g][:, :, tl, :], vtt[:, 512:1024].rearrange("p (h d) -> p h d", d=128), reads=[K("vt", b)], writes=[("dram_vb", t)])
            yield "C"
            gkeys += [("dram_va", t), ("dram_vb", t)]
        else:
            P.dma("act", io["vc_p"][g][:, :, tl, :], vtt[:, :].rearrange("p (h d) -> p h d", d=64), reads=[K("vt", b)], writes=[("dram_vc", t)])
            yield "C"
            gkeys += [("dram_vc", t)]
        if layer == 0:
            P.op("dve", "tensor_copy", tI[:, 0:8, :], pjt[:, 1536:1792].rearrange("p (h d) -> p h d", d=32), reads=[kpj], writes=[K("tI", 0)])
            yield "C"
            P.op("act", "activation", junk[:, 0:32], pjt[:, 1792:1824], AF.Square, accum_out=st[:, 3:4], reads=[kpj], writes=[K("junk"), KK("st", 3)])
            yield "C"
            P.op("act", "activation", st[:, 4:5], st[:, 3:4], AF.Sqrt, bias=eps[:], scale=1.0 / 32, reads=[KK("st", 3), "c_eps"], writes=[KK("st", 4)])
            yield "C"
            P.op("dve", "reciprocal", st[:, 5:6], st[:, 4:5], reads=[KK("st", 4)], writes=[KK("st", 5)])
            yield "C"
            P.op("dve", "scalar_tensor_tensor", out=tI[:, 8, :], in0=pjt[:, 1792:1824], scalar=st[:, 5:6], in1=gik[:], op0=ALU.mult, op1=ALU.mult, reads=[kpj, KK("st", 5), K("gik")], writes=[K("tI", 1)])
            yield "C"
            kti = [K("tI", 0), K("tI", 1)]
            x1 = tI[:, :, 0:4]
            x2 = tI[:, :, 4:8]
            cb = cos4[:, t, :].unsqueeze(1).to_broadcast([128, 9, 4])
            sb_ = sin4[:, t, :].unsqueeze(1).to_broadcast([128, 9, 4])
            rr = [rti_l[b][:, i, :, :] for i in range(4)]
            kr = KK("rti")
            P.op("dve", "tensor_tensor", out=rr[0], in0=x1, in1=cb, op=ALU.mult, reads=kti, writes=[(kr, 0)])
            yield "C"
            P.op("dve", "tensor_tensor", out=rr[1], in0=x2, in1=sb_, op=ALU.mult, reads=kti, writes=[(kr, 1)])
            yield "C"
            P.op("dve", "tensor_tensor", out=rr[2], in0=x2, in1=cb, op=ALU.mult, reads=kti, writes=[(kr, 2)])
            yield "C"
            P.op("dve", "tensor_tensor", out=rr[3], in0=x1, in1=sb_, op=ALU.mult, reads=kti, writes=[(kr, 3)])
            yield "C"
            P.op("dve", "tensor_tensor", out=tI[:, :, 0:4], in0=rr[0], in1=rr[1], op=ALU.subtract, reads=[(kr, 0), (kr, 1)], writes=kti)
            yield "C"
            P.op("dve", "tensor_tensor", out=tI[:, :, 4:8], in0=rr[2], in1=rr[3], op=ALU.add, reads=[(kr, 2), (kr, 3)], writes=kti)
            yield "C"
            P.op("dve", "tensor_copy", ihi[:], tI[:], reads=kti, writes=[K("ihi")])
            yield "C"
            P.op("dve", "tensor_tensor", out=ilo[:], in0=tI[:], in1=ihi[:], op=ALU.subtract, reads=kti + [K("ihi")], writes=[K("ilo")])
            yield "C"
            i3t = i3[b]
            ki3 = K("i3", b)
            P.op("dve", "tensor_copy", i3t[:, 0:8, 0:32], ihi[:, 0:8, :], reads=[K("ihi")], writes=[(ki3, 0)])
            yield "C"
            P.op("dve", "tensor_copy", i3t[:, 0:8, 32:64], ihi[:, 0:8, :], reads=[K("ihi")], writes=[(ki3, 1)])
            yield "C"
            P.op("dve", "tensor_copy", i3t[:, 0:8, 64:96], ilo[:, 0:8, :], reads=[K("ilo")], writes=[(ki3, 2)])
            yield "C"
            P.op("dve", "tensor_copy", i3t[:, 8, 0:32], ihi[:, 8, :], reads=[K("ihi")], writes=[(ki3, 3)])
            yield "C"
            P.op("dve", "tensor_copy", i3t[:, 8, 32:64], ilo[:, 8, :], reads=[K("ilo")], writes=[(ki3, 4)])
            yield "C"
            P.op("dve", "tensor_copy", i3t[:, 8, 64:96], ihi[:, 8, :], reads=[K("ihi")], writes=[(ki3, 5)])
            yield "C"
            i3keys = [(ki3, i) for i in range(6)]
            tpp = tp[tpc % 2]
            ktp = K("tp", tpc % 2)
            tpc += 1
            for j in range(8):
                P.op("pe", "transpose", tpp[0:96, j, :], i3t[:, j, :], ident[:], reads=i3keys + ["c_ident"], writes=[ktp])
                yield "C"
            P.op("act", "copy", iqT_g[:, :, tl * 128:(tl + 1) * 128], tpp[0:96, :, :], reads=[ktp], writes=[K("iqTg")])
            yield "C"
            tpp = tp[tpc % 2]
            ktp = K("tp", tpc % 2)
            tpc += 1
            P.op("pe", "transpose", tpp[0:96, 0, :], i3t[:, 8, :], ident[:], reads=i3keys + ["c_ident"], writes=[ktp])
            yield "C"
            P.op("dve", "tensor_copy", ikT_g[:, tl * 128:(tl + 1) * 128], tpp[0:96, 0, :], reads=[ktp], writes=[K("ikTg")])
            yield "C"
            P.op("dve", "tensor_scalar", iw_all[:, t, :], pjt[:, 1824:1832], 1.0 / 16.0, None, op0=ALU.mult, reads=[kpj], writes=[K("iw_all", t)])
            yield "C"
        else:
            z, a_, e_, m_ = lft[:, 0, :], lft[:, 1, :], lft[:, 2, :], lft[:, 3, :]
            P.op("dve", "tensor_tensor", out=z, in0=pjt[:, 3072:3088], in1=bfb[:], op=ALU.add, reads=[kpj, K("bfb")], writes=[K("lft", 0)])
            yield "C"
            P.op("act", "activation", a_, z, AF.Abs, reads=[K("lft", 0)], writes=[K("lft", 1)])
            yield "C"
            P.op("act", "activation", e_, a_, AF.Exp, scale=-1.0, reads=[K("lft", 1)], writes=[K("lft", 2)])
            yield "C"
            P.op("act", "activation", e_, e_, AF.Ln, bias=one[:], scale=1.0, reads=[K("lft", 2), "c_one"], writes=[K("lft", 2)])
            yield "C"
            P.op("dve", "tensor_scalar", m_, z, 0.0, None, op0=ALU.min, reads=[K("lft", 0)], writes=[K("lft", 3)])
            yield "C"
            P.op("dve", "tensor_tensor", out=m_, in0=m_, in1=e_, op=ALU.subtract, reads=[K("lft", 3), K("lft", 2)], writes=[K("lft", 3)])
            yield "C"
            P.op("pe", "transpose", tpf[:, :], m_, identf[:], reads=[K("lft", 3), "c_identf"], writes=[K("tpf")])
            yield "C"
            P.op("dve", "tensor_copy", lfT[:, t * 128:(t + 1) * 128], tpf[:, :], reads=[K("tpf")], writes=[K("lfT", t)])
            yield "C"
        if layer == 1 and t == NT - 1:
            P.dma("sp", io["logfT"], lfT[:], reads=[K("lfT", t_) for t_ in range(NT)], writes=["dram_logfT"])
            yield "C"
            if "on_lf" in io:
                io["on_lf"](["dram_logfT"])
        if tl == 3:
            for q8 in range(4):
                if layer == 0:
                    dst = (io["qT"], None, io["qT"], None)[q8]
                    h0 = (0, 0, 8, 8)[q8]
                else:
                    dst = (io["qT"], io["qT"], None, None)[q8]
                    h0 = (0, 8, 0, 8)[q8]
                if dst is io["qT"]:
                    P.dma("act", dst[:, h0:h0 + 8, g * 512:(g + 1) * 512], qkT_g[:, q8 * 8:(q8 + 1) * 8, :], reads=[K("qkTg", q8)], writes=[("dram_qkT", g, q8)])
                    yield "C"
                else:
                    P.dma("act", io["kT_p"][g][:, h0:h0 + 8, :], qkT_g[:, q8 * 8:(q8 + 1) * 8, :], reads=[K("qkTg", q8)], writes=[("dram_qkT", g, q8)])
                    gkeys.append(("dram_qkT", g, q8))
                    yield "C"
            if layer == 0:
                P.dma("act", io["iqT"][:, :, g * 512:(g + 1) * 512], iqT_g[:], reads=[K("iqTg")], writes=[("dram_iqT", g)])
                yield "C"
                P.dma("act", io["ikT_p"][g], ikT_g[:], reads=[K("ikTg")], writes=[("dram_ikT", g)])
                gkeys.append(("dram_ikT", g))
                yield "C"
            if "on_group" in io:
                io["on_group"](g, list(gkeys))

    gens = [tile_body(t) for t in range(NT)]

    def adv(gen, stops):
        while True:
            try:
                tag = next(gen)
            except StopIteration:
                return None
            if tag in stops:
                return tag

    adv(gens[0], ("Bend",))
    for t in range(NT):
        cur = gens[t]
        if t + 1 < NT:
            nxt = gens[t + 1]
            adv(nxt, ("A",))
            alive = True
            while True:
                tag = adv(nxt, ("B", "Bend"))
                for _ in range(7):
                    if alive and adv(cur, ("C",)) is None:
                        alive = False
                if tag != "B":
                    break
        adv(cur, ())
    if layer == 0:
        P.dma("sp", io["iw"].rearrange("(t p) h -> p t h", p=128), iw_all[:], reads=[K("iw_all", t) for t in range(NT)], writes=["dram_iw"])


TOPK = 256
NITER = 20
BLO = -64.0
BW = 128.0
NEG = -1.0e30
MASK_DT = FP8
SAT = {"saturate": False}


def emit_attn(P, C, layer, io, NG=4, LA=2):
    ident, eps = C["ident"], C["eps"]
    pfx = "a%d_" % layer
    GK = io.get("gk", {})
    L0 = (layer == 0)
    NT = NG * 4
    Kd = 70

    def K(n, i=None):
        return (pfx + n) if i is None else (pfx + n, i)

    n_s = 3
    ps_s = [P.ps(pfx + "pss%d" % i, [128, 512], F32) for i in range(n_s)]
    n_os = 1 if L0 else 2
    ps_o_raw = [[P.ps(pfx + "pso%d_%d" % (s, b), [128, 512], F32) for b in range(2)] for s in range(n_os)]
    def oview_psum(os_, b, vd):
        return ps_o_raw[os_][b][:, 0:2 * (vd + 1)].rearrange("p (a b) -> p a b", b=vd + 1)
    n_x = 8 - n_s - 2 * n_os
    ps_x = [P.ps(pfx + "psx%d" % i, [128, 512], F32) for i in range(n_x)]
    xc = [0]

    def next_x():
        i = xc[0] % n_x
        xc[0] += 1
        return ps_x[i], K("psx", i)

    wout = load_weight_bf16(P, io["w_out"], 1024, pfx + "wo")
    if L0:
        score = P.sb(pfx + "score", [128, 8192], F32)
    kcol_i = P.sb(pfx + "kcol_i", [128, 64], I32)
    kcol = P.sb(pfx + "kcol", [128, 64], F32)
    qrow = P.sb(pfx + "qrow", [128, NT * 128], F32)
    P.dma("sp", qrow[:], io["qkey_row"].partition_broadcast(128), writes=[K("qrow")])
    if L0:
        qrow_b = P.sb(pfx + "qrow_b", [128, NT * 128], BF16)
        P.op("pool", "tensor_copy", qrow_b[:], qrow[:], reads=[K("qrow")], writes=[K("qrow_b")])
    if L0:
        P.op("pool", "iota", kcol_i[0:64, :], pattern=[[2, 64]], base=0, channel_multiplier=0, writes=[K("kcol_i")])
        P.op("pool", "iota", kcol_i[64:128, :], pattern=[[2, 64]], base=1, channel_multiplier=0, writes=[K("kcol_i")])
    else:
        P.op("pool", "iota", kcol_i[:, :], pattern=[[128, 64]], base=0, channel_multiplier=1, writes=[K("kcol_i")])
    P.op("dve", "tensor_copy", kcol[:], kcol_i[:], reads=[K("kcol_i")], writes=[K("kcol")])

    if L0:
        ikT = P.sb(pfx + "ikT", [96, 8192], BF16)
        for i_ in range(4):
            P.dma("sp", ikT[:, :].rearrange("d (i j q) -> d i j q", i=4, j=4, q=512)[:, i_, :, :], io["ikT_g"][i_], reads=GK.get(("ikT", i_), []), writes=[K("ikT", i_)])
        iw = P.sb(pfx + "iw", [128, NT, 8], F32)
        P.dma("sp", iw[:], io["iw"], writes=[K("iw")])
        qcrel = P.sb(pfx + "qcrel", [128, NT], F32)
        P.dma("sp", qcrel[:], io["qchk_col"], writes=[K("qcrel")])
        for i in range(NG):
            if i > 0:
                P.op("dve", "tensor_scalar", qcrel[:, 4 * i:4 * i + 4], qcrel[:, 4 * i:4 * i + 4], -32.0 * i, None, op0=ALU.add, reads=[K("qcrel")], writes=[K("qcrel")])
        relk_i = score[:, 4096:6144].bitcast(I32).rearrange("p (a b) -> p a b", b=64)
        relk = P.sb(pfx + "relk", [128, 2048], BF16)
        P.op("pool", "iota", relk_i, pattern=[[1, 32], [0, 64]], base=0, channel_multiplier=0, writes=[K("relk_i"), K("score")])
        P.op("dve", "tensor_copy", relk[:], relk_i.rearrange("p a b -> p (a b)"), reads=[K("relk_i"), K("score")], writes=[K("relk")])
        maskT = [P.sb(pfx + "maskT0", [128, 16 * (NG - 1 if NG > 1 else 1), 512], MASK_DT), P.sb(pfx + "maskT1", [128, 16 * NG, 512], MASK_DT)]
        mkb = [P.sb(pfx + "mk%d" % i, [128, 2048], BF16) for i in range(2)]
        rbuf = [P.sb(pfx + "r%d" % i, [128, 512], F32) for i in range(2)]
        bs = P.sb(pfx + "bs", [128, 8], F32)
        junkA = P.sb(pfx + "junkA", [128, 2048], FP8)
        iqb = [P.sb(pfx + "iqb%d" % i_, [96, 8, 128], BF16) for i_ in range(2)]
        ident2 = P.sb(pfx + "ident2", [128, 128], BF16)
        P.op("dve", "tensor_scalar", ident2[:], ident[:], 2.0, None, op0=ALU.mult, reads=["c_ident"], writes=[K("ident2")])
        L4 = P.sb(pfx + "L4", [128, 4, 64], F32)
        for i, nm in enumerate(["lq1", "lk1", "lq2", "lk2"]):
            P.dma("sp", L4[:, i, :], io[nm].partition_broadcast(128), writes=[K("L4", i)])
        lm = P.sb(pfx + "lm", [128, 8], F32)
        lj = P.sb(pfx + "lj", [128, 64], F32)
        for j in range(2):
            P.op("dve", "tensor_tensor", out=lj[:], in0=L4[:, 2 * j, :], in1=L4[:, 2 * j + 1, :], op=ALU.mult, reads=[K("L4", 2 * j), K("L4", 2 * j + 1)], writes=[K("lj")])
            P.op("dve", "tensor_reduce", out=lm[:, j:j + 1], in_=lj[:], axis=AX.X, op=ALU.add, reads=[K("lj")], writes=[K("lm", j)])
            P.op("act", "activation", lm[:, 2 + j:3 + j], lm[:, j:j + 1], AF.Exp, reads=[K("lm", j)], writes=[K("lm", 2 + j)])
        lam_init = 0.8 - 0.6 * math.exp(-0.3 * layer)
        P.op("dve", "tensor_tensor", out=lm[:, 4:5], in0=lm[:, 3:4], in1=lm[:, 2:3], op=ALU.subtract, reads=[K("lm", 2), K("lm", 3)], writes=[K("lm", 4)])
        P.op("dve", "tensor_scalar", lm[:, 5:6], lm[:, 4:5], -lam_init, None, op0=ALU.add, reads=[K("lm", 4)], writes=[K("neglam")])
        gsub = P.sb(pfx + "gsub", [128, 128], F32)
        P.dma("sp", gsub[:], io["subln"].partition_broadcast(128), writes=[K("gsub")])
        P.op("dve", "tensor_scalar", gsub[:], gsub[:], 1.0 - lam_init, None, op0=ALU.mult, reads=[K("gsub")], writes=[K("gsub")])
        o1 = P.sb(pfx + "o1", [128, 4, 128], F32)
        o2 = P.sb(pfx + "o2", [128, 4, 128], F32)
        sst = P.sb(pfx + "sst", [128, 3, 4], F32)

    NRING = 3
    Kc = [P.sb(pfx + "Kc%d" % i, [Kd, 2048], BF16) for i in range(NRING)]
    vdmax = 128 if L0 else 64
    Vc = [P.sb(pfx + "Vc%d" % i, [128, 16, vdmax + 1], BF16) for i in range(NRING)]
    for i in range(NRING):
        P.op("pool", "memset", Vc[i][:, :, 0:1], 1.0, writes=[K("Vone", i)])
        if L0:
            P.op("pool", "memset", Kc[i][64:70, :], 0.0, writes=[K("Kc6", i)])
    if L0:
        QTj = [P.sb(pfx + "QTj%d" % i_, [70, 512], BF16) for i_ in range(3)]
        for i_ in range(3):
            P.op("pool", "memset", QTj[i_][64:70, :], 0.0, writes=[K("QTj6", i_)])
    else:
        QTg = P.sb(pfx + "QTg", [Kd, 16, 512], BF16)
    NPT = 4
    PT = [P.sb(pfx + "pt%d" % i, [128, 512], BF16) for i in range(NPT)]
    ao = P.sb(pfx + "ao", [128, 4, 1024], BF16)
    aT = P.sb(pfx + "aT", [128, 8, 128], BF16)
    if L0:
        xt = [qrow[:, 0:1024], qrow[:, 1024:2048]]
    else:
        xt = [P.sb(pfx + "x%d" % i, [128, 1024], F32) for i in range(2)]
    rc = P.sb(pfx + "rc", [128, 4], F32)
    if L0:
        ocp = P.sb(pfx + "ocp", [128, 2, 258], F32)
    if not L0:
        negm = P.sb(pfx + "negm", [128, 16, 512], BF16)


    def prepass_steps(i):
        steps = []
        nch = i + 1
        nkg = 4 * nch
        q0 = i * 512
        mT = maskT[i % 2]
        mb = i % 2

        def s_load(qt):
            t = 4 * i + qt
            P.dma("pool", iqb[t % 2][:], io["iqT"][:, :, q0 + qt * 128:q0 + (qt + 1) * 128], writes=[K("iqb", t % 2)])

        def s_score(qt, kg, h):
            t = 4 * i + qt
            sk = K("score", kg)
            px, kpx = next_x()
            P.op("pe", "matmul", px[:, :], lhsT=iqb[t % 2][:, h, :], rhs=ikT[:, kg * 512:(kg + 1) * 512], start=True, stop=True,
                 reads=[K("iqb", t % 2), K("ikT", kg // 4)], writes=[kpx])
            r = rbuf[(kg * 8 + h) % 2]
            kr = K("r", (kg * 8 + h) % 2)
            P.op("act", "activation", r[:], px[:, :], AF.Relu, reads=[kr], writes=[kpx, kr])
            sc = score[:, kg * 512:(kg + 1) * 512]
            if h == 0:
                P.op("dve", "tensor_scalar", sc, r[:], iw[:, t, 0:1], None, op0=ALU.mult, reads=[kr, K("iw")], writes=[sk] + ([K("score")] if first_score[0] else []))
                first_score[0] = False
            else:
                P.op("dve", "scalar_tensor_tensor", out=sc, in0=r[:], scalar=iw[:, t, h:h + 1], in1=sc, op0=ALU.mult, op1=ALU.add, reads=[kr, K("iw"), sk], writes=[sk])

        def s_pen(qt, kg):
            t = 4 * i + qt
            sk = K("score", kg)
            a = kg - 4 * i
            r = rbuf[0]
            P.op("dve", "tensor_scalar", r[:], relk[:, a * 512:(a + 1) * 512], qcrel[:, t:t + 1], NEG, op0=ALU.is_gt, op1=ALU.mult,
                 reads=[K("relk"), K("qcrel")], writes=[K("r", 0)])
            P.op("dve", "tensor_tensor", out=score[:, kg * 512:(kg + 1) * 512], in0=score[:, kg * 512:(kg + 1) * 512], in1=r[:], op=ALU.add,
                 reads=[K("r", 0), sk], writes=[sk])

        skeys = [K("score", kg) for kg in range(nkg)]

        def s_binit():
            P.op("dve", "memset", bs[:, 1:2], BLO + BW / 2, writes=[K("bs", 1)])

        n_act = nch // 2
        n_d = nch - n_act
        BIG = 1.0e5

        def s_bis(it):
            w = BW / (2.0 ** (it + 1))
            if n_act:
                P.op("dve", "tensor_scalar", bs[:, 5:6], bs[:, 1:2], BIG, None, op0=ALU.mult, reads=[K("bs", 1)], writes=[K("bs", 5)])
            for c in range(n_d):
                P.op("dve", "tensor_scalar", mkb[0][:], score[:, c * 2048:(c + 1) * 2048], bs[:, 1:2], (bs[:, 2:3] if c > 0 else None),
                     op0=ALU.is_ge, op1=ALU.add, accum_out=bs[:, 2:3],
                     reads=skeys[4 * c:4 * c + 4] + [K("bs", 1)] + ([K("bs", 2)] if c > 0 else []), writes=[K("mk", 0), K("bs", 2)])
            for a_ in range(n_act):
                c = n_d + a_
                P.op("act", "activation", junkA[:], score[:, c * 2048:(c + 1) * 2048], AF.Tanh, bias=bs[:, 5:6], scale=-BIG, accum_out=bs[:, 6 + a_:7 + a_],
                     reads=skeys[4 * c:4 * c + 4] + [K("bs", 5)], writes=[K("junkA"), K("bs", 6 + a_)], saturate=False)
            thr = float(TOPK) - 0.5
            if n_act:
                P.op("dve", "scalar_tensor_tensor", out=bs[:, 2:3], in0=bs[:, 2:3], scalar=2.0, in1=bs[:, 6:7], op0=ALU.mult, op1=ALU.subtract,
                     reads=[K("bs", 2), K("bs", 6)], writes=[K("bs", 2)])
                if n_act == 2:
                    P.op("dve", "tensor_tensor", out=bs[:, 2:3], in0=bs[:, 2:3], in1=bs[:, 7:8], op=ALU.subtract, reads=[K("bs", 2), K("bs", 7)], writes=[K("bs", 2)])
                thr = 2.0 * thr - n_act * 2048.0
            P.op("dve", "tensor_scalar", bs[:, 3:4], bs[:, 2:3], thr, w, op0=ALU.is_ge, op1=ALU.mult, reads=[K("bs", 2)], writes=[K("bs", 3)])
            P.op("dve", "scalar_tensor_tensor", out=bs[:, 1:2], in0=bs[:, 3:4], scalar=-w / 2.0, in1=bs[:, 1:2], op0=ALU.add, op1=ALU.add,
                 reads=[K("bs", 3), K("bs", 1)], writes=[K("bs", 1)])

        def s_bfin():
            P.op("dve", "tensor_scalar", bs[:, 0:1], bs[:, 1:2], -BW / (2.0 ** (NITER + 1)), None, op0=ALU.add, reads=[K("bs", 1)], writes=[K("bs", 0)])

        def s_mask(qt, c):
            mk = mkb[c % 2]
            kmk = K("mk", c % 2)
            P.op("dve", "tensor_scalar", mk[:], score[:, c * 2048:(c + 1) * 2048], bs[:, 0:1], -240.0, op0=ALU.is_lt, op1=ALU.mult,
                 reads=skeys[4 * c:4 * c + 4] + [K("bs", 0)], writes=[kmk])

        def s_tr(qt, c, hf):
            mk = mkb[c % 2]
            kmk = K("mk", c % 2)
            px, kpx = next_x()
            pxb = px[:, :].bitcast(BF16).rearrange("p (a b) -> p a b", b=128)
            for j in range(8):
                kt = hf * 8 + j
                P.op("pe", "transpose", pxb[:, j, :], mk[:, kt * 128:(kt + 1) * 128], ident[:], reads=[kmk, "c_ident"], writes=[kpx])
            dst = mT[:, c * 16 + hf * 8:c * 16 + hf * 8 + 8, qt * 128:(qt + 1) * 128]
            if hf == 0:
                P.op("act", "copy", dst, pxb[:, 0:8, :], writes=[kpx, K("maskT", (mb, c * 2 + hf, qt))], **SAT)
            else:
                P.op("act", "copy", dst, pxb[:, 0:8, :], writes=[kpx, K("maskT", (mb, c * 2 + hf, qt))], **SAT)

        for qt in range(4):
            steps.append(lambda qt=qt: s_load(qt))
            for kg in range(nkg):
                for h in range(8):
                    steps.append(lambda qt=qt, kg=kg, h=h: s_score(qt, kg, h))
                if kg >= 4 * i:
                    steps.append(lambda qt=qt, kg=kg: s_pen(qt, kg))
            steps.append(s_binit)
            for it in range(NITER):
                steps.append(lambda it=it: s_bis(it))
            steps.append(s_bfin)
            for c in range(nch):
                steps.append(lambda qt=qt, c=c: s_mask(qt, c))
                for hf in range(2):
                    steps.append(lambda qt=qt, c=c, hf=hf: s_tr(qt, c, hf))
        return steps

    slot_base = [0]
    order = ([1, 2, 3, 0] if (L0 and NG == 4) else list(range(NG)))
    first_score = [True]
    for oi, i in enumerate(order):
        nch = i + 1
        q0 = i * 512
        if not L0:
            P.dma("sp", QTg[0:64, :, :], io["qT"][:, :, q0:q0 + 512], writes=[K("QTg")])
            P.dma("sp", QTg[64:70, :, :], io["caug_q"][:, :, q0:q0 + 512], writes=[K("QTg6")])
        qkeys = [] if L0 else [K("QTg"), K("QTg6")]
        if not L0:
            for kt in range(16):
                ktg = i * 16 + kt
                P.op("dve", "tensor_scalar", negm[:, kt, :], qrow[:, q0:q0 + 512], kcol[:, ktg:ktg + 1], -30000.0, op0=ALU.is_lt, op1=ALU.mult,
                     reads=[K("qrow"), K("kcol")], writes=[K("negm", kt)])

        if L0 and oi == 0:
            for st_ in prepass_steps(i):
                st_()
        nxt_steps = prepass_steps(order[oi + 1]) if (L0 and oi + 1 < NG) else []

        if L0:
            jobs = [dict(mode="dsa", vd=64, qh=h, kh=h, vname="va_g", vh=h) for h in range(8)]
            jobs += [dict(mode="chunk", vd=128, qh=8 + j, kh=8 + j, vname="vb_g", vh=j // 2, pair=j) for j in range(8)]
        else:
            jobs = [dict(mode="causal", vd=64, qh=h, kh=h, vname="vc_g", vh=h) for h in range(16)]
        steps = [(ji, c) for ji in range(len(jobs)) for c in range(nch)]
        units = [(ji, c, kt) for (ji, c) in steps for kt in range(16)]

        def slot_of(si):
            return (slot_base[0] + si) % NRING

        def emit_load2(si):
            ji, c = steps[si]
            jb = jobs[ji]
            sl = slot_of(si)
            if L0 and c == 0:
                P.dma("sp", QTj[ji % 3][0:64, :], io["qT"][:, jb["qh"], q0:q0 + 512], writes=[K("QTj", ji % 3)])
            P.dma("sp", Kc[sl][0:64, :].rearrange("d (r q) -> d r q", r=4), io["kT_g"][c][:, :, jb["kh"], :], reads=GK.get(("kT", c), []), writes=[K("Kc", sl)])
            if not L0:
                P.dma("sp", Kc[sl][64:70, :], io["caug_k"][jb["kh"], :, c * 2048:(c + 1) * 2048], writes=[K("Kc6", sl)])
            vd = jb["vd"]
            for r in range(4):
                P.dma("sp", Vc[sl][:, 4 * r:4 * r + 4, 1:1 + vd], io[jb["vname"]][c][jb["vh"]][:, r, :, :], reads=GK.get((jb["vname"], c), []), writes=[K("Vc", sl)])

        def ksl(si):
            sl = slot_of(si)
            return [K("Kc", sl), K("Kc6", sl)]

        def emit_qk(u):
            ji, c, kt = units[u]
            jb = jobs[ji]
            si = ji * nch + c
            sl = slot_of(si)
            ps = ps_s[u % n_s]
            kps = K("pss", u % n_s)
            addm = (jb["mode"] == "causal" and c == i) or jb["mode"] == "dsa"
            P.op("pe", "matmul", ps[:, :], lhsT=Kc[sl][0:Kd, kt * 128:(kt + 1) * 128], rhs=(QTj[ji % 3][:, :] if L0 else QTg[0:Kd, jb["qh"], :]), start=True, stop=(not addm),
                 reads=ksl(si) + qkeys + ([K("QTj", ji % 3), K("QTj6", ji % 3)] if L0 else []), writes=[kps])
            if jb["mode"] == "dsa":
                ktg_ = c * 16 + kt
                P.op("pe", "matmul", ps[:, :], lhsT=ident2[:, :], rhs=maskT[i % 2][:, ktg_, :], start=False, stop=True,
                     reads=[K("ident2")] + [K("maskT", (i % 2, ktg_ // 8, qt)) for qt in range(4)], writes=[kps])
            elif addm:
                P.op("pe", "matmul", ps[:, :], lhsT=ident[:, :], rhs=negm[:, kt, :], start=False, stop=True,
                     reads=["c_ident", K("negm", kt)], writes=[kps])
            pt = PT[u % NPT]
            kpt = K("pt", u % NPT)
            P.op("act", "activation", pt[:], ps[:, :], AF.Exp, scale=0.125, reads=[kps], writes=[kpt])
            ktg = c * 16 + kt
            if c == i and jb["mode"] == "chunk":
                P.op("dve", "scalar_tensor_tensor", out=pt[:], in0=qrow_b[:, q0:q0 + 512], scalar=kcol[:, ktg:ktg + 1], in1=pt[:], op0=ALU.is_ge, op1=ALU.mult,
                     reads=[kpt, K("qrow_b"), K("kcol")], writes=[kpt])

        def emit_pv(u):
            ji, c, kt = units[u]
            jb = jobs[ji]
            si = ji * nch + c
            sl = slot_of(si)
            vd = jb["vd"]
            os_ = ji % n_os
            pt = PT[u % NPT]
            kpt = K("pt", u % NPT)
            first = (c == 0 and kt == 0)
            last = (c == nch - 1 and kt == 15)
            for s in range(4):
                ob = oview_psum(os_, s // 2, vd)
                P.op("pe", "matmul", ob[:, s % 2, 0:vd + 1], lhsT=pt[:, s * 128:(s + 1) * 128], rhs=Vc[sl][:, kt, 0:vd + 1],
                     start=(first and s % 2 == 0), stop=last, skip_group_check=True,
                     reads=[kpt, K("Vc", sl), K("Vone", sl)], writes=[K("pso", (os_, s // 2))])
            if last:
                emit_norm(ji)

        def emit_norm(ji):
            jb = jobs[ji]
            vd = jb["vd"]
            os_ = ji % n_os
            okeys = [K("pso", (os_, 0)), K("pso", (os_, 1))]
            if L0:
                P.op("act", "copy", ocp[:, 0, 0:2 * (vd + 1)], ps_o_raw[os_][0][:, 0:2 * (vd + 1)], writes=[okeys[0], K("ocp", 0)])
                P.op("dve", "tensor_copy", ocp[:, 1, 0:2 * (vd + 1)], ps_o_raw[os_][1][:, 0:2 * (vd + 1)], writes=[okeys[1], K("ocp", 1)])
                okeys = [K("ocp", 0), K("ocp", 1)]

                def oview(os__, b_, vd_):
                    return ocp[:, b_, 0:2 * (vd_ + 1)].rearrange("p (a b) -> p a b", b=vd_ + 1)
            else:
                oview = oview_psum
            for s in range(4):
                ob = oview(os_, s // 2, vd)
                P.op("dve", "reciprocal", rc[:, s:s + 1], ob[:, s % 2, 0:1], writes=[okeys[s // 2], K("rc", s)])
            if jb["mode"] != "chunk":
                h = jb["qh"]
                for s in range(4):
                    ob = oview(os_, s // 2, vd)
                    if s // 2 == 0:
                        P.op("act", "activation", ao[:, s, h * 64:(h + 1) * 64], ob[:, s % 2, 1:1 + vd], AF.Copy, scale=rc[:, s:s + 1],
                             reads=[K("rc", s)], writes=[okeys[s // 2], K("ao", (s, h))])
                    else:
                        P.op("dve", "tensor_scalar", ao[:, s, h * 64:(h + 1) * 64], ob[:, s % 2, 1:1 + vd], rc[:, s:s + 1], None, op0=ALU.mult,
                             reads=[K("rc", s)], writes=[okeys[s // 2], K("ao", (s, h))])
            else:
                j = jb["pair"]
                hb, m = j // 2, j % 2
                dst = o1 if m == 0 else o2
                for s in range(4):
                    ob = oview(os_, s // 2, vd)
                    P.op("dve", "tensor_scalar", dst[:, s, :], ob[:, s % 2, 1:1 + vd], rc[:, s:s + 1], None, op0=ALU.mult,
                         reads=[K("rc", s)], writes=[okeys[s // 2], K("o12", (m, s))])
                if m == 1:
                    o12 = [K("o12", (mm_, s)) for mm_ in range(2) for s in range(4)]
                    P.op("dve", "scalar_tensor_tensor", out=o1[:], in0=o2[:], scalar=lm[:, 5:6], in1=o1[:], op0=ALU.mult, op1=ALU.add,
                         reads=o12 + [K("neglam")], writes=o12)
                    P.op("dve", "tensor_tensor", out=o2[:], in0=o1[:], in1=o1[:], op=ALU.mult, reads=o12, writes=o12)
                    P.op("dve", "tensor_reduce", out=sst[:, 0, :], in_=o2[:], axis=AX.X, op=ALU.add, reads=o12, writes=[K("sst", 0)])
                    P.op("act", "activation", sst[:, 1, :], sst[:, 0, :], AF.Sqrt, bias=eps[:], scale=1.0 / 128, reads=[K("sst", 0), "c_eps"], writes=[K("sst", 1)])
                    P.op("dve", "reciprocal", sst[:, 2, :], sst[:, 1, :], reads=[K("sst", 1)], writes=[K("sst", 2)])
                    P.op("dve", "tensor_tensor", out=o1[:], in0=o1[:], in1=sst[:, 2, :].unsqueeze(2).to_broadcast([128, 4, 128]), op=ALU.mult,
                         reads=o12 + [K("sst", 2)], writes=o12)
                    P.op("dve", "tensor_tensor", out=ao[:, :, 512 + hb * 128:512 + (hb + 1) * 128], in0=o1[:], in1=gsub[:].unsqueeze(1).to_broadcast([128, 4, 128]), op=ALU.mult,
                         reads=o12 + [K("gsub")], writes=[K("ao", (s, 8 + 2 * hb + e)) for s in range(4) for e in range(2)])

        nst = len(steps)
        nxt_done = [0]
        emit_load2(0)
        if nst > 1:
            emit_load2(1)
        nu = len(units)
        for idx in range(nu + LA):
            if idx < nu:
                emit_qk(idx)
            if idx - LA >= 0:
                emit_pv(idx - LA)
            if idx < nu:
                ji, c, kt = units[idx]
                si = ji * nch + c
                if kt == LA and si + 2 < nst:
                    emit_load2(si + 2)
                want = ((idx + 1) * len(nxt_steps)) // nu
                while nxt_done[0] < want:
                    nxt_steps[nxt_done[0]]()
                    nxt_done[0] += 1
        slot_base[0] = (slot_base[0] + nst) % NRING
        while nxt_done[0] < len(nxt_steps):
            nxt_steps[nxt_done[0]]()
            nxt_done[0] += 1

        aokeys = lambda s: [K("ao", (s, h)) for h in range(16)]
        for s in range(4):
            t = 4 * i + s
            px, kpx = next_x()
            pxb = px[:, :].bitcast(BF16).rearrange("p (a b) -> p a b", b=128)
            for kc in range(8):
                P.op("pe", "transpose", pxb[:, kc, :], ao[:, s, kc * 128:(kc + 1) * 128], ident[:], reads=aokeys(s) + ["c_ident"], writes=[kpx])
            P.op("act", "copy", aT[:], pxb[:, 0:8, :], reads=[kpx], writes=[K("aT")])
            x = xt[t % 2]
            kx = K("x", t % 2)
            P.dma("sp", x[:], io["x"][t * 128:(t + 1) * 128, :], writes=[kx] + ([K("qrow")] if L0 else []))
            for n in range(2):
                px, kpx = next_x()
                for kc in range(8):
                    P.op("pe", "matmul", px[:, :], lhsT=aT[:, kc, :], rhs=wout[:, kc, n * 512:(n + 1) * 512], start=(kc == 0), stop=(kc == 7),
                         reads=[K("aT")] + wkeys(pfx + "wo", n * 512, (n + 1) * 512), writes=[kpx])
                P.op("dve", "tensor_tensor", out=x[:, n * 512:(n + 1) * 512], in0=x[:, n * 512:(n + 1) * 512], in1=px[:, :], op=ALU.add, reads=[kx, kpx], writes=[kx])
            P.dma("sp", io["out"][t * 128:(t + 1) * 128, :], x[:], reads=[kx], writes=[("dram_out", t)])
            if s == 3:
                P.dma("sp", io["tails"][2 * i:2 * i + 2, :], x[126:128, :], reads=[kx], writes=[("dram_tails", i)])


DFF = 2816
NFC = 44
NGC = 22


def emit_ffn(P, C, io, NG=4, pfx="f_"):
    ident, eps = C["ident"], C["eps"]

    def K(n, i=None):
        return (pfx + n) if i is None else (pfx + n, i)

    ps_u = [P.ps(pfx + "psu%d" % i, [128, 512], F32) for i in range(3)]
    ps_h = [P.ps(pfx + "psh%d" % i, [128, 512], F32) for i in range(2)]
    ps_d = [P.ps(pfx + "psd%d" % i, [128, 512], F32) for i in range(2)]
    ps_t = P.ps(pfx + "pst", [128, 8, 128], BF16)

    xg = P.sb(pfx + "xg", [128, 4, 1024], F32)
    wup = load_weight_bf16(P, io["w_up"], 2 * DFF, pfx + "wup")
    wdn = P.sb(pfx + "wdn", [128, NGC, 1024], BF16)
    wdv = io["w_down"].rearrange("(c p) n -> p c n", p=128)
    for c0 in range(0, NGC, 4):
        c1 = min(c0 + 4, NGC)
        P.dma("pool", wdn[:, c0:c1, :], wdv[:, c0:c1, :], writes=[K("wdn", c0)])
    wdn_keys = [K("wdn", c0) for c0 in range(0, NGC, 4)]
    lnb = P.sb(pfx + "lnb", [128, 1024], F32)
    P.dma("sp", lnb[:], io["ln"].partition_broadcast(128), writes=[K("lnb")])
    cw = P.sb(pfx + "cw", [128, 4, NFC], F32)
    P.dma("sp", cw[:], io["cw"], writes=[K("cw")])

    xh = P.sb(pfx + "xh", [2, 1024], F32)
    e1h = P.sb(pfx + "e1h", [2, 4], F32)
    P.dma("sp", e1h[:], io["e1h"][0:2, :], writes=[K("e1h")])
    st = P.sb(pfx + "st", [128, 4], F32)
    hb = [P.sb(pfx + "h%d" % i, [128, 1024], BF16) for i in range(2)]
    hT = P.sb(pfx + "hT", [128, 8, 514], BF16)
    U = [P.sb(pfx + "U%d" % i, [128, 514], F32) for i in range(2)]
    cv = [P.sb(pfx + "cv%d" % i, [128, 512], F32) for i in range(3)]
    sg = [P.sb(pfx + "sg%d" % i, [128, 512], F32) for i in range(2)]
    actT = P.sb(pfx + "actT", [128, NGC, 512], BF16)
    cand = actT[0:2, 0:4, :].rearrange("p a b -> p (a b)").bitcast(F32)

    for g in range(NG):
        ck = [K("cand")] + [K("actT", fc) for fc in range(4)]
        for jp in range(4):
            if jp == 0 and g == 0:
                P.op("pool", "memset", cand, 0.0, writes=ck)
            else:
                r_src = (jp - 1) % 4
                g_src = g - (1 if jp == 0 else 0)
                P.dma("sp", cand, io["tails_g"][r_src, 2 * g_src:2 * g_src + 2, :], reads=io.get("gk_tails", []), writes=ck)
            if jp == 0:
                P.op("dve", "tensor_scalar", xh[:], cand, e1h[:, 0:1], None, op0=ALU.mult, reads=ck + [K("e1h")], writes=[K("xh")])
            else:
                P.op("dve", "scalar_tensor_tensor", out=xh[:], in0=cand, scalar=e1h[:, jp:jp + 1], in1=xh[:], op0=ALU.mult, op1=ALU.add,
                     reads=ck + [K("e1h"), K("xh")], writes=[K("xh")])
        P.dma("sp", xg[:], io["x"][g * 512:(g + 1) * 512, :].rearrange("(t p) d -> p t d", p=128), writes=[K("xg")])
        for tl in range(-1, 4):
            npart = 2 if tl < 0 else 128
            src = xh[:, :] if tl < 0 else xg[:, tl, :]
            ksrc = K("xh") if tl < 0 else K("xg")
            b = (tl + 1) % 2
            P.op("act", "activation", hb[b][0:npart, :], src, AF.Square, accum_out=st[0:npart, 0:1], reads=[ksrc], writes=[K("h", b), K("st", 0)])
            P.op("act", "activation", st[0:npart, 1:2], st[0:npart, 0:1], AF.Sqrt, bias=eps[0:npart, :], scale=1.0 / 1024, reads=[K("st", 0), "c_eps"], writes=[K("st", 1)])
            P.op("dve", "reciprocal", st[0:npart, 2:3], st[0:npart, 1:2], reads=[K("st", 1)], writes=[K("st", 2)])
            h = hb[b]
            P.op("dve", "scalar_tensor_tensor", out=h[0:npart, :], in0=src, scalar=st[0:npart, 2:3], in1=lnb[0:npart, :], op0=ALU.mult, op1=ALU.mult,
                 reads=[ksrc, K("st", 2), K("lnb")], writes=[K("h", b)])
            if tl < 0:
                for kc in range(8):
                    P.op("pe", "transpose", ps_t[:, kc, 0:2], h[0:2, kc * 128:(kc + 1) * 128], ident[0:2, 0:2], reads=[K("h", b), "c_ident"], writes=[K("pst")])
                P.op("act", "copy", hT[:, :, 0:2], ps_t[:, :, 0:2], writes=[K("pst"), K("hT", -1)])
            else:
                for kc in range(8):
                    P.op("pe", "transpose", ps_t[:, kc, :], h[:, kc * 128:(kc + 1) * 128], ident[:], reads=[K("h", b), "c_ident"], writes=[K("pst")])
                P.op("act", "copy", hT[:, :, 2 + tl * 128:2 + (tl + 1) * 128], ps_t[:, :, :], writes=[K("pst"), K("hT", tl)])
        hTkeys = [K("hT", tl) for tl in range(-1, 4)]

        def up_chunk(fc, n):
            pu = ps_u[n % 3]
            kpu = K("psu", n % 3)
            ph = ps_h[n % 2]
            kph = K("psh", n % 2)
            for kc in range(8):
                P.op("pe", "matmul", pu[:, :], lhsT=wup[:, kc, fc * 128:(fc + 1) * 128], rhs=hT[:, kc, 2:514], start=(kc == 0), stop=(kc == 7),
                     reads=hTkeys + wkeys(pfx + "wup", fc * 128, (fc + 1) * 128), writes=[kpu])
            for kc in range(8):
                P.op("pe", "matmul", ph[:, 0:2], lhsT=wup[:, kc, fc * 128:(fc + 1) * 128], rhs=hT[:, kc, 0:2], start=(kc == 0), stop=(kc == 7),
                     reads=hTkeys + wkeys(pfx + "wup", fc * 128, (fc + 1) * 128), writes=[kph])
            u = U[n % 2]
            ku = K("U", n % 2)
            c = cv[n % 3]
            kc_ = K("cv", n % 3)
            P.op("act", "copy", u[:, 2:514], pu[:, :], writes=[kpu, (ku, 1)])
            P.op("act", "activation", c[:], pu[:, :], AF.Identity, bias=cw[:, 3, fc:fc + 1], scale=cw[:, 2, fc:fc + 1], reads=[K("cw")], writes=[kpu, kc_])
            P.op("act", "copy", u[:, 0:2], ph[:, 0:2], writes=[kph, (ku, 0)])
            P.op("dve", "scalar_tensor_tensor", out=c[:], in0=u[:, 1:513], scalar=cw[:, 1, fc:fc + 1], in1=c[:], op0=ALU.mult, op1=ALU.add, reads=[(ku, 0), (ku, 1), K("cw"), kc_], writes=[kc_])
            P.op("dve", "scalar_tensor_tensor", out=c[:], in0=u[:, 0:512], scalar=cw[:, 0, fc:fc + 1], in1=c[:], op0=ALU.mult, op1=ALU.add, reads=[(ku, 0), (ku, 1), K("cw"), kc_], writes=[kc_])
            return c, kc_

        n = 0
        for fc in range(NGC):
            cg, kcg = up_chunk(fc, n)
            n += 1
            cvl, kcv = up_chunk(fc + NGC, n)
            n += 1
            s = sg[fc % 2]
            ks = K("sg", fc % 2)
            P.op("act", "activation", s[:], cg[:], AF.Silu, reads=[kcg], writes=[ks])
            P.op("dve", "tensor_tensor", out=actT[:, fc, :], in0=s[:], in1=cvl[:], op=ALU.mult, reads=[ks, kcv], writes=[K("actT", fc)])
        akeys = [K("actT", fc) for fc in range(NGC)]

        for tl in range(4):
            t = g * 4 + tl
            for nh in range(2):
                pd = ps_d[(tl * 2 + nh) % 2]
                kpd = K("psd", (tl * 2 + nh) % 2)
                for fc in range(NGC):
                    P.op("pe", "matmul", pd[:, :], lhsT=actT[:, fc, tl * 128:(tl + 1) * 128], rhs=wdn[:, fc, nh * 512:(nh + 1) * 512], start=(fc == 0), stop=(fc == NGC - 1),
                         reads=akeys + wdn_keys, writes=[kpd])
                P.op("dve", "tensor_tensor", out=xg[:, tl, nh * 512:(nh + 1) * 512], in0=xg[:, tl, nh * 512:(nh + 1) * 512], in1=pd[:, :], op=ALU.add,
                     reads=[K("xg")], writes=[kpd, K("xo", (tl, nh))])
            P.dma("sp", io["out"][t * 128:(t + 1) * 128, :], xg[:, tl, :], reads=[K("xo", (tl, 0)), K("xo", (tl, 1)), K("xg")], writes=[("dram_out", t)])


def emit_cumsum(P, io):
    pfx = "cs_"

    def K(n, i=None):
        return (pfx + n) if i is None else (pfx + n, i)
    lfb = P.sb(pfx + "lfb", [16, 8192], F32)
    for j in range(4):
        P.dma("sp", lfb[:, :].rearrange("h (i j q) -> h j i q", i=4, j=4, q=512)[:, j, :, :],
              io["logf_g"][:, j, :].rearrange("h (i q) -> h i q", q=512), reads=io.get("gk_lf", []), writes=[K("lfb")])
    lfo = P.sb(pfx + "lfo", [16, 2048], F32)
    P.dma("sp", lfo[:], io["logf_own"], writes=[K("lfo")])
    e1h = P.sb(pfx + "e1h", [16, 4], F32)
    P.dma("sp", e1h[:], io["e1h"][0:16, :], writes=[K("e1h")])
    ones = P.sb(pfx + "ones", [16, 512], F32)
    P.op("dve", "memset", ones[:], 1.0, writes=[K("ones")])
    c = P.sb(pfx + "c", [16, 8192], F32)
    cq = P.sb(pfx + "cq", [16, 2048], F32)
    Pc = P.sb(pfx + "Pc", [16, 17], F32)
    P.op("dve", "memset", Pc[:, 0:1], 0.0, writes=[K("Pc", 0)])
    for g in range(16):
        P.op("dve", "tensor_tensor_scan", c[:, g * 512:(g + 1) * 512], ones[:], lfb[:, g * 512:(g + 1) * 512], Pc[:, g:g + 1], op0=ALU.mult, op1=ALU.add,
             reads=[K("ones"), K("lfb"), K("Pc", g)], writes=[K("c", g)])
        P.op("dve", "tensor_copy", Pc[:, g + 1:g + 2], c[:, (g + 1) * 512 - 1:(g + 1) * 512], reads=[K("c", g)], writes=[K("Pc", g + 1)])
    ini = P.sb(pfx + "ini", [16, 4], F32)
    for i in range(4):
        P.op("dve", "tensor_scalar", ini[:, i:i + 1], Pc[:, 4 * i:4 * i + 1], e1h[:, 0:1], None, op0=ALU.mult, reads=[K("Pc", 4 * i), K("e1h")], writes=[K("ini", i)])
        for j in range(1, 4):
            P.op("dve", "scalar_tensor_tensor", out=ini[:, i:i + 1], in0=Pc[:, 4 * i + j:4 * i + j + 1], scalar=e1h[:, j:j + 1], in1=ini[:, i:i + 1], op0=ALU.mult, op1=ALU.add,
                 reads=[K("Pc", 4 * i + j), K("e1h"), K("ini", i)], writes=[K("ini", i)])
        P.op("dve", "tensor_tensor_scan", cq[:, i * 512:(i + 1) * 512], ones[:], lfo[:, i * 512:(i + 1) * 512], ini[:, i:i + 1], op0=ALU.mult, op1=ALU.add,
             reads=[K("ones"), K("lfo"), K("ini", i)], writes=[K("cq", i)])
    c8 = P.sb(pfx + "c8", [16, 2048], F32)
    r = P.sb(pfx + "r", [16, 2048], F32)
    o = [P.sb(pfx + "o%d" % i, [16, 6, 2048], BF16) for i in range(2)]
    for n in range(5):
        b = n % 2
        src = cq[:, :] if n == 4 else c[:, n * 2048:(n + 1) * 2048]
        ksrc = [K("cq", i) for i in range(4)] if n == 4 else [K("c", g) for g in range(4 * n, 4 * n + 4)]
        ot = o[b]
        ko = K("o", b)
        P.op("dve", "tensor_scalar", c8[:], src, 8.0, None, op0=ALU.mult, reads=ksrc, writes=[K("c8")])
        P.op("dve", "tensor_copy", ot[:, 0, :], c8[:], reads=[K("c8")], writes=[(ko, 0)])
        P.op("dve", "tensor_tensor", out=r[:], in0=c8[:], in1=ot[:, 0, :], op=ALU.subtract, reads=[K("c8"), (ko, 0)], writes=[K("r")])
        P.op("dve", "tensor_copy", ot[:, 1, :], r[:], reads=[K("r")], writes=[(ko, 1)])
        P.op("dve", "tensor_tensor", out=r[:], in0=r[:], in1=ot[:, 1, :], op=ALU.subtract, reads=[K("r"), (ko, 1)], writes=[K("r")])
        P.op("dve", "tensor_copy", ot[:, 2, :], r[:], reads=[K("r")], writes=[(ko, 2)])
        hk = [(ko, 0), (ko, 1), (ko, 2)]
        if n == 4:
            P.op("dve", "memset", ot[:, 3:6, :], 1.0, writes=[(ko, 3)])
            P.dma("sp", io["caug_q"].rearrange("r h q -> h r q"), ot[:], reads=hk + [(ko, 3)], writes=["dram_cq"] + hk + [(ko, 3)])
        else:
            P.op("dve", "tensor_scalar", ot[:, 3:6, :], ot[:, 0:3, :], -1.0, None, op0=ALU.mult, reads=hk, writes=[(ko, 3)])
            P.op("dve", "memset", ot[:, 0:3, :], 1.0, reads=[(ko, 3)], writes=hk)
            P.dma("sp", io["caug_k"][:, :, n * 2048:(n + 1) * 2048], ot[:], reads=hk + [(ko, 3)], writes=[("dram_ck", n)] + hk + [(ko, 3)])


S_ = 8192
NCORES = 8
GROUPS = [[0, 1, 2, 3], [4, 5, 6, 7]]


def _toks(j, NG=4):
    return np.concatenate([np.arange(512) + 512 * (4 * i + j) for i in range(NG)])


def build_fused():
    nc = bass.Bass("TRN2", target_bir_lowering=False)
    NTOK = 2048

    def din(name, shape, dt=F32):
        return nc.dram_tensor(name, list(shape), dt, kind="ExternalInput").ap()

    def scr(name, shape, dt):
        return nc.dram_tensor(name, list(shape), dt)

    E = {}
    for name, shape in [("x", [NTOK, 1024]), ("pos", [128, 16]), ("qchk_row", [NTOK]), ("qpos_row", [NTOK]), ("qchk_col", [128, 16]), ("e1h", [16, 4]),
                        ("ln_mix", [2, 1024]), ("ln_ffn", [2, 1024]), ("ev_w_in", [1024, 3368]), ("ev_w_out", [1024, 1024]),
                        ("ev_a_qnorm", [64]), ("ev_a_knorm", [64]), ("ev_idx_knorm", [32]), ("ev_b_qnorm", [64]), ("ev_b_knorm", [64]),
                        ("ev_lam_q1", [64]), ("ev_lam_k1", [64]), ("ev_lam_q2", [64]), ("ev_lam_k2", [64]), ("ev_b_subln", [128]),
                        ("od_w_in", [1024, 3088]), ("od_b_f", [16]), ("od_w_out", [1024, 1024]), ("od_c_qnorm", [64]), ("od_c_knorm", [64]),
                        ("ffn_up", [2, 1024, 5632]), ("cw", [2, 128, 4, 44]), ("ffn_down", [2, 2816, 1024])]:
        E[name] = din(name, shape)
    out = nc.dram_tensor("out", [NTOK, 1024], F32, kind="ExternalOutput").ap()

    T = {}
    specs = [("qT0", [1024, NTOK], BF16), ("iqT", [768, NTOK], BF16), ("iw", [NTOK, 8], F32),
             ("x1", [NTOK, 1024], F32), ("tl1", [8, 1024], F32), ("tl1_g", [32, 1024], F32), ("x2", [NTOK, 1024], F32),
             ("qT1", [1024, NTOK], BF16), ("lf", [16, NTOK], F32), ("lf_g", [64, NTOK], F32),
             ("cak", [96, S_], BF16), ("caq", [96, NTOK], BF16),
             ("x3", [NTOK, 1024], F32), ("tl3", [8, 1024], F32), ("tl3_g", [32, 1024], F32)]
    for g in range(4):
        specs += [("kT0_%d" % g, [1024, 512], BF16), ("kT0_%d_g" % g, [4096, 512], BF16), ("kT1_%d" % g, [1024, 512], BF16), ("kT1_%d_g" % g, [4096, 512], BF16),
                  ("vc_%d" % g, [2048, 256], BF16), ("vc_%d_g" % g, [8192, 256], BF16),
                  ("va_%d" % g, [1024, 256], BF16), ("va_%d_g" % g, [4096, 256], BF16), ("vb_%d" % g, [512, 512], BF16), ("vb_%d_g" % g, [2048, 512], BF16),
                  ("ikT_%d" % g, [96, 512], BF16), ("ikT_%d_g" % g, [384, 512], BF16)]
    for name, shape, dt in specs:
        T[name] = scr("s_" + name, shape, dt)

    def A(name):
        return T[name].ap()

    with ExitStack() as es:
        P = Prog(nc, es)
        C = build_consts(P)
        with P.phase():
            io = {"x": E["x"], "pos": E["pos"], "ln": E["ln_mix"][0], "w_in": E["ev_w_in"],
                  "gains": [E["ev_a_qnorm"], E["ev_a_knorm"], E["ev_b_qnorm"], E["ev_b_knorm"]], "g_ik": E["ev_idx_knorm"],
                  "qT": A("qT0").rearrange("(d h) q -> d h q", h=16), "kT_p": [A("kT0_%d" % g).rearrange("(h d) q -> d h q", h=16) for g in range(4)],
                  "iqT": A("iqT").rearrange("(d h) q -> d h q", h=8), "ikT_p": [A("ikT_%d" % g) for g in range(4)], "iw": A("iw"),
                  "va_p": [A("va_%d" % g).rearrange("(h p) (t d) -> p h t d", p=128, d=64) for g in range(4)],
                  "vb_p": [A("vb_%d" % g).rearrange("(h p) (t d) -> p h t d", p=128, d=128) for g in range(4)]}

            def on_group0(g, keys):
                for nm in ["ikT_%d" % g, "kT0_%d" % g, "va_%d" % g, "vb_%d" % g]:
                    P.coll(T[nm], T[nm + "_g"], GROUPS, reads=keys, writes=[("g", nm)])
            io["on_group"] = on_group0
            emit_proj(P, C, 0, io, 16)
        gk0 = {}
        for g in range(4):
            gk0[("ikT", g)] = [("g", "ikT_%d" % g)]
            gk0[("kT", g)] = [("g", "kT0_%d" % g)]
            gk0[("va_g", g)] = [("g", "va_%d" % g)]
            gk0[("vb_g", g)] = [("g", "vb_%d" % g)]
        with P.phase():
            io = {"qT": A("qT0").rearrange("(d h) q -> d h q", h=16), "kT_g": [A("kT0_%d_g" % g).rearrange("(r h d) q -> d r h q", r=4, h=16) for g in range(4)],
                  "ikT_g": [A("ikT_%d_g" % g).rearrange("(j d) q -> d j q", j=4) for g in range(4)], "iqT": A("iqT").rearrange("(d h) q -> d h q", h=8),
                  "iw": A("iw").rearrange("(t p) h -> p t h", p=128), "qkey_row": E["qchk_row"], "qchk_col": E["qchk_col"],
                  "va_g": [A("va_%d_g" % g).rearrange("(r h p) (t d) -> h p r t d", r=4, p=128, d=64) for g in range(4)],
                  "vb_g": [A("vb_%d_g" % g).rearrange("(r h p) (t d) -> h p r t d", r=4, p=128, d=128) for g in range(4)],
                  "lq1": E["ev_lam_q1"], "lk1": E["ev_lam_k1"], "lq2": E["ev_lam_q2"], "lk2": E["ev_lam_k2"], "subln": E["ev_b_subln"],
                  "x": E["x"], "w_out": E["ev_w_out"], "out": A("x1"), "tails": A("tl1"), "gk": gk0}
            emit_attn(P, C, 0, io, NG=4)
        P.coll(T["tl1"], T["tl1_g"], GROUPS, writes=[("g", "tl1")])
        with P.phase():
            io = {"x": A("x1"), "tails_g": A("tl1_g").rearrange("(r g) d -> r g d", r=4), "e1h": E["e1h"], "ln": E["ln_ffn"][0], "w_up": E["ffn_up"][0],
                  "cw": E["cw"][0], "w_down": E["ffn_down"][0], "out": A("x2"), "gk_tails": [("g", "tl1")]}
            emit_ffn(P, C, io, NG=4, pfx="f0_")
        with P.phase():
            io = {"x": A("x2"), "pos": E["pos"], "ln": E["ln_mix"][1], "w_in": E["od_w_in"], "gains": [E["od_c_qnorm"], E["od_c_knorm"]], "b_f": E["od_b_f"],
                  "qT": A("qT1").rearrange("(d h) q -> d h q", h=16), "kT_p": [A("kT1_%d" % g).rearrange("(h d) q -> d h q", h=16) for g in range(4)],
                  "vc_p": [A("vc_%d" % g).rearrange("(h p) (t d) -> p h t d", p=128, d=64) for g in range(4)], "logfT": A("lf")}

            def on_group1(g, keys):
                for nm in ["kT1_%d" % g, "vc_%d" % g]:
                    P.coll(T[nm], T[nm + "_g"], GROUPS, reads=keys, writes=[("g", nm)])
            io["on_group"] = on_group1
            io["on_lf"] = lambda keys: P.coll(T["lf"], T["lf_g"], GROUPS, reads=keys, writes=[("g", "lf")])
            emit_proj(P, C, 1, io, 16)
        gk1 = {}
        for g in range(4):
            gk1[("kT", g)] = [("g", "kT1_%d" % g)]
            gk1[("vc_g", g)] = [("g", "vc_%d" % g)]
        with P.phase():
            io = {"logf_g": A("lf_g").rearrange("(j h) q -> h j q", j=4), "logf_own": A("lf"), "e1h": E["e1h"],
                  "caug_k": A("cak").rearrange("(h r) q -> h r q", r=6), "caug_q": A("caq").rearrange("(r h) q -> r h q", h=16), "gk_lf": [("g", "lf")]}
            emit_cumsum(P, io)
        with P.phase():
            io = {"qT": A("qT1").rearrange("(d h) q -> d h q", h=16), "kT_g": [A("kT1_%d_g" % g).rearrange("(r h d) q -> d r h q", r=4, h=16) for g in range(4)],
                  "qkey_row": E["qpos_row"], "vc_g": [A("vc_%d_g" % g).rearrange("(r h p) (t d) -> h p r t d", r=4, p=128, d=64) for g in range(4)],
                  "caug_k": A("cak").rearrange("(h r) q -> h r q", r=6), "caug_q": A("caq").rearrange("(r h) q -> r h q", h=16),
                  "x": A("x2"), "w_out": E["od_w_out"], "out": A("x3"), "tails": A("tl3"), "gk": gk1}
            emit_attn(P, C, 1, io, NG=4)
        P.coll(T["tl3"], T["tl3_g"], GROUPS, writes=[("g", "tl3")])
        with P.phase():
            io = {"x": A("x3"), "tails_g": A("tl3_g").rearrange("(r g) d -> r g d", r=4), "e1h": E["e1h"], "ln": E["ln_ffn"][1], "w_up": E["ffn_up"][1],
                  "cw": E["cw"][1], "w_down": E["ffn_down"][1], "out": out, "gk_tails": [("g", "tl3")]}
            emit_ffn(P, C, io, NG=4, pfx="f1_")
        P.finish()
        P.emit()
    return nc


def kernel(**inp):
    inp = {k: np.asarray(v) for k, v in inp.items()}
    x = np.ascontiguousarray(inp["x"], dtype=np.float32)
    nc = build_fused()
    cw = np.stack([np.concatenate([inp["ffn_conv"][l], inp["ffn_conv_b"][l][None]], 0).reshape(4, 44, 128).transpose(2, 0, 1) for l in range(2)], 0)
    shared = {"ln_mix": inp["ln_mix"], "ln_ffn": inp["ln_ffn"], "ev_w_in": inp["ev_w_in"][0], "ev_w_out": inp["ev_w_out"][0],
              "od_w_in": inp["od_w_in"][0], "od_b_f": inp["od_b_f"][0], "od_w_out": inp["od_w_out"][0],
              "ffn_up": inp["ffn_up"], "cw": np.ascontiguousarray(cw, dtype=np.float32), "ffn_down": inp["ffn_down"]}
    for k in ["ev_a_qnorm", "ev_a_knorm", "ev_idx_knorm", "ev_b_qnorm", "ev_b_knorm", "ev_lam_q1", "ev_lam_k1", "ev_lam_q2", "ev_lam_k2", "ev_b_subln",
              "od_c_qnorm", "od_c_knorm"]:
        shared[k] = inp[k][0]
    shared = {k: np.ascontiguousarray(v, dtype=np.float32) for k, v in shared.items()}
    maps = []
    for c in range(NCORES):
        b, j = c // 4, c % 4
        toks = _toks(j)
        m = dict(shared)
        e1h = np.zeros((16, 4), np.float32)
        e1h[:, j] = 1.0
        m.update(x=np.ascontiguousarray(x[b, toks]), pos=np.ascontiguousarray(toks.astype(np.float32).reshape(16, 128).T),
                 qchk_row=(toks // 64).astype(np.float32), qpos_row=toks.astype(np.float32),
                 qchk_col=np.ascontiguousarray((toks // 64).astype(np.float32).reshape(16, 128).T), e1h=e1h)
        maps.append(m)
    res = run_bass_kernel_spmd(nc, maps, core_ids=list(range(NCORES)))
    out = np.empty_like(x)
    for c in range(NCORES):
        out[c // 4, _toks(c % 4)] = np.asarray(res.results[c]["out"])
    return out.astype(np.float32)
```

```python
import math
from contextlib import ExitStack
import numpy as np
import concourse.bass as bass
import concourse.mybir as mybir
from concourse.bass_utils import run_bass_kernel_spmd


F32 = mybir.dt.float32
BF16 = mybir.dt.bfloat16
I32 = mybir.dt.int32
FP8 = mybir.dt.float8e4
ALU = mybir.AluOpType
AF = mybir.ActivationFunctionType
AX = mybir.AxisListType

ENGS = ["pe", "act", "dve", "pool", "sp"]
N_DMA_SEMS = 32
N_SW_SEMS = 8


class Prog:
    def __init__(self, nc, es, same_engine_sync=True):
        self.nc = nc
        self.es = es
        self.same = same_engine_sync
        self.sems = {}
        for e in ENGS:
            self.sems["e_" + e] = es.enter_context(nc.semaphore("se_" + e))
        self.cnt = {e: 0 for e in ENGS}
        self.known = {e: {} for e in ENGS}
        self.ops = {e: [] for e in ENGS}
        self.reg = {}
        self.dsem = []
        for i in range(N_DMA_SEMS):
            nm = "d_%d" % i
            self.sems[nm] = es.enter_context(nc.semaphore("sd_%d" % i))
            self.dsem.append([nm, 0])
        self.drr = 0
        self.drr_sw = 0
        self.n_inst = 0
        self.cur_es = es
        self.csem = []
        self.n_coll = 0

    def phase(self):
        prog = self

        class _Ph:
            def __enter__(self_):
                self_.st = ExitStack()
                self_.prev = prog.cur_es
                prog.cur_es = self_.st
                return self_

            def __exit__(self_, *a):
                prog.barrier()
                prog.cur_es = self_.prev
                self_.st.close()
                return False
        return _Ph()

    def barrier(self):
        targets = {}
        for e in ENGS:
            if self.cnt[e] > 0:
                targets["e_" + e] = self.cnt[e]
        for nm, v in self.dsem:
            if v > 0:
                targets[nm] = v
        for e in ENGS:
            waits = []
            for s_, v in targets.items():
                if s_ == "e_" + e:
                    continue
                if self.known[e].get(s_, 0) < v:
                    waits.append((s_, v))
                    self.known[e][s_] = v
            if waits:
                self.ops[e].append((waits, None, None))

    def coll(self, src_t, dst_t, groups, reads=(), writes=()):
        waits = self._deps("pool", reads, writes)
        nm = "c_%d" % self.n_coll
        self.n_coll += 1
        self.sems[nm] = self.es.enter_context(self.nc.semaphore("sc_%d" % (self.n_coll - 1)))
        self.csem.append(nm)
        tag = (nm, 1)
        self._mark(reads, writes, tag)
        kw = dict(replica_groups=groups, ins=[src_t.ap().opt()], outs=[dst_t.ap().opt()])
        self.ops["pool"].append((waits, ("collective_compute", ("AllGather", ALU.bypass), kw), (nm, 1)))
        self.n_inst += 1

    def sb(self, name, shape, dt):
        return self.cur_es.enter_context(self.nc.sbuf_tensor(name, list(shape), dt))

    def ps(self, name, shape, dt):
        return self.cur_es.enter_context(self.nc.psum_tensor(name, list(shape), dt))

    def _deps(self, eng, reads, writes):
        deps = {}

        def add(sv):
            if sv is None:
                return
            s, v = sv
            if deps.get(s, 0) < v:
                deps[s] = v

        for k in reads:
            st = self.reg.get(k)
            if st is not None:
                add(st[0])
        for k in writes:
            st = self.reg.get(k)
            if st is not None:
                add(st[0])
                for s, v in st[1].items():
                    add((s, v))
        waits = []
        me = "e_" + eng
        for s, v in deps.items():
            if s == me and (eng == "pe" or not self.same):
                continue
            if self.known[eng].get(s, 0) < v:
                waits.append((s, v))
                self.known[eng][s] = v
        return waits

    def _mark(self, reads, writes, tag):
        for k in writes:
            self.reg[k] = [tag, {}]
        for k in reads:
            st = self.reg.setdefault(k, [None, {}])
            s, v = tag
            if st[1].get(s, 0) < v:
                st[1][s] = v

    def op(self, eng, meth, *args, reads=(), writes=(), **kw):
        fn = (meth, args, kw)
        waits = self._deps(eng, reads, writes)
        self.cnt[eng] += 1
        tag = ("e_" + eng, self.cnt[eng])
        self._mark(reads, writes, tag)
        self.ops[eng].append((waits, fn, ("e_" + eng, 1)))
        self.n_inst += 1

    def dma(self, q, out, in_, reads=(), writes=(), **kw):
        waits = self._deps(q, reads, writes)
        if q == "pool":
            ent = self.dsem[N_DMA_SEMS - N_SW_SEMS + self.drr_sw]
            self.drr_sw = (self.drr_sw + 1) % N_SW_SEMS
        else:
            ent = self.dsem[self.drr]
            self.drr = (self.drr + 1) % (N_DMA_SEMS - N_SW_SEMS)
        nm, prev = ent
        if prev > 0 and self.known[q].get(nm, 0) < prev:
            waits.append((nm, prev))
            self.known[q][nm] = prev
        ent[1] = prev + 16
        tag = (nm, prev + 16)
        self._mark(reads, writes, tag)
        kw = dict(kw); kw["out"] = out; kw["in_"] = in_
        self.ops[q].append((waits, ("dma_start", (), kw), (nm, 16)))
        self.n_inst += 1

    def finish(self):
        waits = []
        for nm, v in self.dsem:
            if v > 0 and self.known["sp"].get(nm, 0) < v:
                waits.append((nm, v))
        self.ops["sp"].append((waits, None, None))

    def emit(self):
        nc = self.nc
        block = self.es.enter_context(nc.Block())
        names = {"pe": "tensor", "act": "scalar", "dve": "vector", "pool": "gpsimd", "sp": "sync"}

        def mk(e):
            def body(engine):
                for waits, fn, inc in self.ops[e]:
                    for s, v in waits:
                        engine.wait_ge(self.sems[s], v)
                    if fn is not None:
                        inst = getattr(engine, fn[0])(*fn[1], **fn[2])
                        if inc is not None:
                            inst.then_inc(self.sems[inc[0]], inc[1])
            return body

        for e in ENGS:
            getattr(block, names[e])(mk(e))


D = 1024
EPS = 1e-6
THETA = 500000.0
TWO_PI = 2.0 * math.pi
C1 = 6.28125
C2 = TWO_PI - 6.28125


def build_consts(P):
    C = {}
    it = P.sb("c_iota_i", [128, 128], I32)
    itf = P.sb("c_iota_f", [128, 128], F32)
    ident = P.sb("c_ident", [128, 128], BF16)
    identf = P.sb("c_identf", [128, 128], F32)
    P.op("pool", "iota", it[:], pattern=[[1, 128]], base=0, channel_multiplier=-1, writes=["c_iota_i"])
    P.op("dve", "tensor_copy", itf[:], it[:], reads=["c_iota_i"], writes=["c_iota_f"])
    P.op("dve", "tensor_single_scalar", ident[:], itf[:], 0.0, op=ALU.is_equal, reads=["c_iota_f"], writes=["c_ident"])
    P.op("dve", "tensor_single_scalar", identf[:], itf[:], 0.0, op=ALU.is_equal, reads=["c_iota_f"], writes=["c_identf"])
    eps = P.sb("c_eps", [128, 1], F32)
    one = P.sb("c_one", [128, 1], F32)
    P.op("dve", "memset", eps[:], EPS, writes=["c_eps"])
    P.op("dve", "memset", one[:], 1.0, writes=["c_one"])
    C["ident"] = ident
    C["identf"] = identf
    C["eps"] = eps
    C["one"] = one
    return C


def rope_tables(P, posf, kp, NT, half, name):
    cos = P.sb(name + "_cos", [128, NT, half], F32)
    sin = P.sb(name + "_sin", [128, NT, half], F32)
    ang = P.sb(name + "_ang", [128, NT, half], F32)
    kf = P.sb(name + "_kf", [128, NT, half], F32)
    ki = P.sb(name + "_ki", [128, NT, half], I32)
    r = P.sb(name + "_r", [128, NT, half], F32)
    m = P.sb(name + "_m", [128, NT, half], F32)
    ka, kk, kr, km = name + "_ang", name + "_kf", name + "_r", name + "_m"
    for i in range(half):
        inv = float(np.float32(THETA ** (-float(i) / half)))
        P.op("dve", "tensor_scalar", ang[:, :, i], posf[:], inv, None, op0=ALU.mult, reads=[kp], writes=[ka])
    for which, dst, shift in (("s", sin, 0.0), ("c", cos, math.pi / 2)):
        kd = name + ("_sin" if which == "s" else "_cos")
        P.op("dve", "tensor_scalar", kf[:], ang[:], shift, 1.0 / TWO_PI, op0=ALU.add, op1=ALU.mult, reads=[ka], writes=[kk])
        P.op("dve", "tensor_copy", ki[:], kf[:], reads=[kk], writes=[name + "_ki"])
        P.op("dve", "tensor_copy", kf[:], ki[:], reads=[name + "_ki"], writes=[kk])
        P.op("dve", "scalar_tensor_tensor", out=r[:], in0=kf[:], scalar=-C1, in1=ang[:], op0=ALU.mult, op1=ALU.add, reads=[kk, ka], writes=[kr])
        P.op("dve", "scalar_tensor_tensor", out=r[:], in0=kf[:], scalar=-C2, in1=r[:], op0=ALU.mult, op1=ALU.add, reads=[kk, kr], writes=[kr])
        if shift != 0.0:
            P.op("dve", "tensor_scalar", r[:], r[:], shift, None, op0=ALU.add, reads=[kr], writes=[kr])
        P.op("dve", "tensor_scalar", m[:], r[:], math.pi, -TWO_PI, op0=ALU.is_gt, op1=ALU.mult, reads=[kr], writes=[km])
        P.op("dve", "tensor_tensor", out=r[:], in0=r[:], in1=m[:], op=ALU.add, reads=[kr, km], writes=[kr])
        P.op("dve", "tensor_scalar", m[:], r[:], -math.pi, TWO_PI, op0=ALU.is_lt, op1=ALU.mult, reads=[kr], writes=[km])
        P.op("dve", "tensor_tensor", out=r[:], in0=r[:], in1=m[:], op=ALU.add, reads=[kr, km], writes=[kr])
        P.op("dve", "tensor_scalar", r[:], r[:], -3.14159, 3.14159, op0=ALU.max, op1=ALU.min, reads=[kr], writes=[kr])
        P.op("act", "activation", dst[:], r[:], AF.Sin, reads=[kr], writes=[kd])
    return cos, sin


def load_weight_bf16(P, w_dram, ncol, name, stg=None, stg_keys=None, kchunks=8, engines=None):
    wb = P.sb(name, [128, kchunks, ncol], BF16)
    wv = w_dram.rearrange("(kc p) n -> p kc n", p=128)
    n0 = 0
    while n0 < ncol:
        n1 = min(n0 + 512, ncol)
        P.dma("pool", wb[:, :, n0:n1], wv[:, :, n0:n1], writes=[(name, n0)])
        n0 = n1
    return wb


def wkeys(name, n0, n1):
    return [(name, c) for c in range((n0 // 512) * 512, n1, 512)]


def emit_proj(P, C, layer, io, NT):
    nc = P.nc
    NCOL = 3368 if layer == 0 else 3088
    NG = NT // 4
    ident, identf, eps, one = C["ident"], C["identf"], C["eps"], C["one"]
    pfx = "p%d_" % layer

    wb = load_weight_bf16(P, io["w_in"], NCOL, pfx + "w")
    lnb = P.sb(pfx + "lnb", [128, D], F32)
    P.dma("sp", lnb[:], io["ln"].partition_broadcast(128), writes=[pfx + "lnb"])
    G = P.sb(pfx + "G", [128, 32, 64], F32)
    G4 = P.sb(pfx + "G4", [128, 4, 64], F32)
    gl = io["gains"]
    for i, g in enumerate(gl):
        P.dma("sp", G4[:, i, :], g.partition_broadcast(128), writes=[(pfx + "G4", i)])
    per = 32 // len(gl)
    for i in range(len(gl)):
        P.op("dve", "tensor_copy", G[:, i * per:(i + 1) * per, :], G4[:, i, :].unsqueeze(1).to_broadcast([128, per, 64]), reads=[(pfx + "G4", i)], writes=[(pfx + "G", i)])
    Gkeys = [(pfx + "G", i) for i in range(len(gl))]
    posf = P.sb(pfx + "posf", [128, NT], F32)
    P.dma("sp", posf[:], io["pos"], writes=[pfx + "posf"])
    if layer == 0:
        cos8, sin8 = rope_tables(P, posf, pfx + "posf", NT, 8, pfx + "r8")
        cos4, sin4 = rope_tables(P, posf, pfx + "posf", NT, 4, pfx + "r4")
        gik = P.sb(pfx + "gik", [128, 32], F32)
        P.dma("sp", gik[:], io["g_ik"].partition_broadcast(128), writes=[pfx + "gik"])
        iw_all = P.sb(pfx + "iw_all", [128, NT, 8], F32)
    else:
        bfb = P.sb(pfx + "bfb", [128, 16], F32)
        P.dma("sp", bfb[:], io["b_f"].partition_broadcast(128), writes=[pfx + "bfb"])
        lfT = P.sb(pfx + "lfT", [16, NT * 128], F32)

    xt = [P.sb(pfx + "x%d" % i, [128, D], F32) for i in range(2)]
    junk = P.sb(pfx + "junk", [128, D], BF16)
    hb = [P.sb(pfx + "h%d" % i, [128, D], BF16) for i in range(2)]
    hT = [P.sb(pfx + "hT%d" % i, [128, 8, 128], BF16) for i in range(2)]
    pj = [P.sb(pfx + "pj%d" % i, [128, NCOL], F32) for i in range(2)]
    st_l = [P.sb(pfx + "st%d" % i, [128, 8], F32) for i in range(2)]
    tmpA_l = [P.sb(pfx + "tmpA%d" % i, [128, 32, 64], F32) for i in range(2)]
    hst_l = [P.sb(pfx + "hst%d" % i, [128, 3, 32], F32) for i in range(2)]
    rt_l = [P.sb(pfx + "rt%d" % i, [128, 4, 32, 8], F32) for i in range(2)]
    nr = [P.sb(pfx + "nr%d" % i, [128, 32, 64], BF16) for i in range(2)]
    vt = [P.sb(pfx + "vt%d" % i, [128, 1024], BF16) for i in range(2)]
    qkT_g = P.sb(pfx + "qkTg", [64, 32, 512], BF16)
    tp = [P.ps(pfx + "tp%d" % i, [128, 8, 128], BF16) for i in range(2)]
    mm = [P.ps(pfx + "mm%d" % i, [128, 512], F32) for i in range(3)]
    if layer == 0:
        tI = P.sb(pfx + "tI", [128, 9, 32], F32)
        rti_l = [P.sb(pfx + "rti%d" % i, [128, 4, 9, 4], F32) for i in range(2)]
        ihi = P.sb(pfx + "ihi", [128, 9, 32], BF16)
        ilo = P.sb(pfx + "ilo", [128, 9, 32], BF16)
        i3 = [P.sb(pfx + "i3_%d" % i, [128, 9, 96], BF16) for i in range(2)]
        iqT_g = P.sb(pfx + "iqTg", [96, 8, 512], BF16)
        ikT_g = P.sb(pfx + "ikTg", [96, 512], BF16)
    else:
        lft = P.sb(pfx + "lft", [128, 4, 16], F32)
        tpf = P.ps(pfx + "tpf", [16, 128], F32)

    def K(n, i=None):
        return (pfx + n) if i is None else (pfx + n, i)

    tpc = 0

    def tile_body(t):
        nonlocal tpc
        b = t % 2
        g, tl = t // 4, t % 4
        st, tmpA, hst, rt = st_l[b], tmpA_l[b], hst_l[b], rt_l[b]

        def KK(n, i=None, b=b):
            return K(n + "_b%d" % b, i)
        x = xt[b]
        kx = K("x", b)
        P.op("act", "activation", junk[:], x[:], AF.Square, accum_out=st[:, 0:1], reads=[kx], writes=[K("junk"), KK("st", 0)])
        P.op("act", "activation", st[:, 1:2], st[:, 0:1], AF.Sqrt, bias=eps[:], scale=1.0 / D, reads=[KK("st", 0), "c_eps"], writes=[KK("st", 1)])
        P.op("dve", "reciprocal", st[:, 2:3], st[:, 1:2], reads=[KK("st", 1)], writes=[KK("st", 2)])
        h = hb[b]
        P.op("dve", "scalar_tensor_tensor", out=h[:], in0=x[:], scalar=st[:, 2:3], in1=lnb[:], op0=ALU.mult, op1=ALU.mult, reads=[kx, KK("st", 2), K("lnb")], writes=[K("h", b)])
        if t + 2 < NT:
            P.dma("sp", x[:], io["x"][(t + 2) * 128:(t + 3) * 128, :], writes=[kx])
        tpp = tp[tpc % 2]
        ktp = K("tp", tpc % 2)
        tpc += 1
        for kc in range(8):
            P.op("pe", "transpose", tpp[:, kc, :], h[:, kc * 128:(kc + 1) * 128], ident[:], reads=[K("h", b), "c_ident"], writes=[ktp])
        hTt = hT[b]
        P.op("act", "copy", hTt[:], tpp[:], reads=[ktp], writes=[K("hT", b)])
        yield "A"
        pjt = pj[b]
        n0 = 0
        gi = 0
        while n0 < NCOL:
            n1 = min(n0 + 512, NCOL)
            m = mm[gi % 3]
            km = K("mm", gi % 3)
            for kc in range(8):
                P.op("pe", "matmul", m[:, 0:n1 - n0], lhsT=hTt[:, kc, :], rhs=wb[:, kc, n0:n1], start=(kc == 0), stop=(kc == 7), reads=[K("hT", b)] + wkeys(pfx + "w", n0, n1), writes=[km])
            if gi % 2 == 0:
                P.op("act", "copy", pjt[:, n0:n1], m[:, 0:n1 - n0], reads=[km], writes=[K("pj", b)])
            else:
                P.op("dve", "tensor_copy", pjt[:, n0:n1], m[:, 0:n1 - n0], reads=[km], writes=[K("pj", b)])
            n0 = n1
            gi += 1
            yield "B"
        yield "Bend"
        kpj = K("pj", b)
        nrt = nr[b]
        knr = K("nr", b)

        def normrope(src, H, h0, rope_half, cos, sin, Gsl, Gk):
            ta = tmpA[:, h0:h0 + H, :]
            kta = KK("tmpA", h0)
            P.op("act", "activation", ta, src, AF.Square, reads=[kpj], writes=[kta])
            yield "C"
            P.op("dve", "tensor_reduce", out=hst[:, 0, h0:h0 + H], in_=ta, axis=AX.X, op=ALU.add, reads=[kta], writes=[KK("hst", h0)])
            yield "C"
            P.op("act", "activation", hst[:, 1, h0:h0 + H], hst[:, 0, h0:h0 + H], AF.Sqrt, bias=eps[:], scale=1.0 / 64, reads=[KK("hst", h0), "c_eps"], writes=[KK("hst1", h0)])
            yield "C"
            P.op("dve", "reciprocal", hst[:, 2, h0:h0 + H], hst[:, 1, h0:h0 + H], reads=[KK("hst1", h0)], writes=[KK("hst2", h0)])
            yield "C"
            P.op("dve", "tensor_tensor", out=ta, in0=src, in1=hst[:, 2, h0:h0 + H].unsqueeze(2).to_broadcast([128, H, 64]), op=ALU.mult, reads=[kpj, KK("hst2", h0)], writes=[kta])
            yield "C"
            P.op("dve", "tensor_tensor", out=ta, in0=ta, in1=Gsl, op=ALU.mult, reads=[kta] + Gk, writes=[kta])
            yield "C"
            if rope_half:
                hf = rope_half
                x1 = tmpA[:, h0:h0 + H, 0:hf]
                x2 = tmpA[:, h0:h0 + H, hf:2 * hf]
                cb = cos[:, t, :].unsqueeze(1).to_broadcast([128, H, hf])
                sb_ = sin[:, t, :].unsqueeze(1).to_broadcast([128, H, hf])
                r = [rt[:, i, h0:h0 + H, :] for i in range(4)]
                krt = KK("rt", h0)
                P.op("dve", "tensor_tensor", out=r[0], in0=x1, in1=cb, op=ALU.mult, reads=[kta], writes=[(krt, 0)])
                yield "C"
                P.op("dve", "tensor_tensor", out=r[1], in0=x2, in1=sb_, op=ALU.mult, reads=[kta], writes=[(krt, 1)])
                yield "C"
                P.op("dve", "tensor_tensor", out=r[2], in0=x2, in1=cb, op=ALU.mult, reads=[kta], writes=[(krt, 2)])
                yield "C"
                P.op("dve", "tensor_tensor", out=r[3], in0=x1, in1=sb_, op=ALU.mult, reads=[kta], writes=[(krt, 3)])
                yield "C"
                P.op("dve", "tensor_tensor", out=nrt[:, h0:h0 + H, 0:hf], in0=r[0], in1=r[1], op=ALU.subtract, reads=[(krt, 0), (krt, 1)], writes=[(knr, h0, 0)])
                yield "C"
                P.op("dve", "tensor_tensor", out=nrt[:, h0:h0 + H, hf:2 * hf], in0=r[2], in1=r[3], op=ALU.add, reads=[(krt, 2), (krt, 3)], writes=[(knr, h0, 1)])
                yield "C"
                P.op("act", "copy", nrt[:, h0:h0 + H, 2 * hf:64], tmpA[:, h0:h0 + H, 2 * hf:64], reads=[kta], writes=[(knr, h0, 2)])
                yield "C"
                return [(knr, h0, 0), (knr, h0, 1), (knr, h0, 2)]
            else:
                P.op("act", "copy", nrt[:, h0:h0 + H, :], ta, reads=[kta], writes=[(knr, h0, 2)])
                yield "C"
                return [(knr, h0, 2)]

        if layer == 0:
            k1 = yield from normrope(pjt[:, 0:1024].rearrange("p (h d) -> p h d", d=64), 16, 0, 8, cos8, sin8, G[:, 0:16, :], Gkeys[0:2])
            k2 = yield from normrope(pjt[:, 1832:2856].rearrange("p (h d) -> p h d", d=64), 16, 16, 8, cos8, sin8, G[:, 16:32, :], Gkeys[2:4])
            nrkeys = {0: k1, 16: k2}
        else:
            k1 = yield from normrope(pjt[:, 0:1024].rearrange("p (h d) -> p h d", d=64), 16, 0, 0, None, None, G[:, 0:16, :], Gkeys[0:1])
            k2 = yield from normrope(pjt[:, 1024:2048].rearrange("p (h d) -> p h d", d=64), 16, 16, 0, None, None, G[:, 16:32, :], Gkeys[1:2])
            nrkeys = {0: k1, 16: k2}
        for q8 in range(4):
            tpp = tp[tpc % 2]
            ktp = K("tp", tpc % 2)
            tpc += 1
            for j in range(8):
                hh = q8 * 8 + j
                P.op("pe", "transpose", tpp[0:64, j, :], nrt[:, hh, :], ident[:], reads=nrkeys[(hh // 16) * 16] + ["c_ident"], writes=[ktp])
                yield "C"
            eng = "act" if q8 % 2 == 0 else "dve"
            if eng == "act":
                P.op("act", "copy", qkT_g[:, q8 * 8:(q8 + 1) * 8, tl * 128:(tl + 1) * 128], tpp[0:64, :, :], reads=[ktp], writes=[K("qkTg", q8)])
                yield "C"
            else:
                P.op("dve", "tensor_copy", qkT_g[:, q8 * 8:(q8 + 1) * 8, tl * 128:(tl + 1) * 128], tpp[0:64, :, :], reads=[ktp], writes=[K("qkTg", q8)])
                yield "C"
        vtt = vt[b]
        if layer == 0:
            P.op("dve", "tensor_copy", vtt[:, 0:512], pjt[:, 1024:1536], reads=[kpj], writes=[K("vt", b)])
            yield "C"
            P.op("dve", "tensor_copy", vtt[:, 512:1024], pjt[:, 2856:3368], reads=[kpj], writes=[K("vt", b)])
            yield "C"
        else:
            P.op("dve", "tensor_copy", vtt[:], pjt[:, 2048:3072], reads=[kpj], writes=[K("vt", b)])
            yield "C"
        gkeys = io.setdefault("grp_keys", {}).setdefault(g, [])
        if layer == 0:
            P.dma("act", io["va_p"][g][:, :, tl, :], vtt[:, 0:512].rearrange("p (h d) -> p h d", d=64), reads=[K("vt", b)], writes=[("dram_va", t)])
            yield "C"
            P.dma("act", io["vb_p"][g][:, :, tl, :], vtt[:, 512:1024].rearrange("p (h d) -> p h d", d=128), reads=[K("vt", b)], writes=[("dram_vb", t)])
            yield "C"
            gkeys += [("dram_va", t), ("dram_vb", t)]
        else:
            P.dma("act", io["vc_p"][g][:, :, tl, :], vtt[:, :].rearrange("p (h d) -> p h d", d=64), reads=[K("vt", b)], writes=[("dram_vc", t)])
            yield "C"
            gkeys += [("dram_vc", t)]
        if layer == 0:
            P.op("dve", "tensor_copy", tI[:, 0:8, :], pjt[:, 1536:1792].rearrange("p (h d) -> p h d", d=32), reads=[kpj], writes=[K("tI", 0)])
            yield "C"
            P.op("act", "activation", junk[:, 0:32], pjt[:, 1792:1824], AF.Square, accum_out=st[:, 3:4], reads=[kpj], writes=[K("junk"), KK("st", 3)])
            yield "C"
            P.op("act", "activation", st[:, 4:5], st[:, 3:4], AF.Sqrt, bias=eps[:], scale=1.0 / 32, reads=[KK("st", 3), "c_eps"], writes=[KK("st", 4)])
            yield "C"
            P.op("dve", "reciprocal", st[:, 5:6], st[:, 4:5], reads=[KK("st", 4)], writes=[KK("st", 5)])
            yield "C"
            P.op("dve", "scalar_tensor_tensor", out=tI[:, 8, :], in0=pjt[:, 1792:1824], scalar=st[:, 5:6], in1=gik[:], op0=ALU.mult, op1=ALU.mult, reads=[kpj, KK("st", 5), K("gik")], writes=[K("tI", 1)])
            yield "C"
            kti = [K("tI", 0), K("tI", 1)]
            x1 = tI[:, :, 0:4]
            x2 = tI[:, :, 4:8]
            cb = cos4[:, t, :].unsqueeze(1).to_broadcast([128, 9, 4])
            sb_ = sin4[:, t, :].unsqueeze(1).to_broadcast([128, 9, 4])
            rr = [rti_l[b][:, i, :, :] for i in range(4)]
            kr = KK("rti")
            P.op("dve", "tensor_tensor", out=rr[0], in0=x1, in1=cb, op=ALU.mult, reads=kti, writes=[(kr, 0)])
            yield "C"
            P.op("dve", "tensor_tensor", out=rr[1], in0=x2, in1=sb_, op=ALU.mult, reads=kti, writes=[(kr, 1)])
            yield "C"
            P.op("dve", "tensor_tensor", out=rr[2], in0=x2, in1=cb, op=ALU.mult, reads=kti, writes=[(kr, 2)])
            yield "C"
            P.op("dve", "tensor_tensor", out=rr[3], in0=x1, in1=sb_, op=ALU.mult, reads=kti, writes=[(kr, 3)])
            yield "C"
            P.op("dve", "tensor_tensor", out=tI[:, :, 0:4], in0=rr[0], in1=rr[1], op=ALU.subtract, reads=[(kr, 0), (kr, 1)], writes=kti)
            yield "C"
            P.op("dve", "tensor_tensor", out=tI[:, :, 4:8], in0=rr[2], in1=rr[3], op=ALU.add, reads=[(kr, 2), (kr, 3)], writes=kti)
            yield "C"
            P.op("dve", "tensor_copy", ihi[:], tI[:], reads=kti, writes=[K("ihi")])
            yield "C"
            P.op("dve", "tensor_tensor", out=ilo[:], in0=tI[:], in1=ihi[:], op=ALU.subtract, reads=kti + [K("ihi")], writes=[K("ilo")])
            yield "C"
            i3t = i3[b]
            ki3 = K("i3", b)
            P.op("dve", "tensor_copy", i3t[:, 0:8, 0:32], ihi[:, 0:8, :], reads=[K("ihi")], writes=[(ki3, 0)])
            yield "C"
            P.op("dve", "tensor_copy", i3t[:, 0:8, 32:64], ihi[:, 0:8, :], reads=[K("ihi")], writes=[(ki3, 1)])
            yield "C"
            P.op("dve", "tensor_copy", i3t[:, 0:8, 64:96], ilo[:, 0:8, :], reads=[K("ilo")], writes=[(ki3, 2)])
            yield "C"
            P.op("dve", "tensor_copy", i3t[:, 8, 0:32], ihi[:, 8, :], reads=[K("ihi")], writes=[(ki3, 3)])
            yield "C"
            P.op("dve", "tensor_copy", i3t[:, 8, 32:64], ilo[:, 8, :], reads=[K("ilo")], writes=[(ki3, 4)])
            yield "C"
            P.op("dve", "tensor_copy", i3t[:, 8, 64:96], ihi[:, 8, :], reads=[K("ihi")], writes=[(ki3, 5)])
            yield "C"
            i3keys = [(ki3, i) for i in range(6)]
            tpp = tp[tpc % 2]
            ktp = K("tp", tpc % 2)
            tpc += 1
            for j in range(8):
                P.op("pe", "transpose", tpp[0:96, j, :], i3t[:, j, :], ident[:], reads=i3keys + ["c_ident"], writes=[ktp])
                yield "C"
            P.op("act", "copy", iqT_g[:, :, tl * 128:(tl + 1) * 128], tpp[0:96, :, :], reads=[ktp], writes=[K("iqTg")])
            yield "C"
            tpp = tp[tpc % 2]
            ktp = K("tp", tpc % 2)
            tpc += 1
            P.op("pe", "transpose", tpp[0:96, 0, :], i3t[:, 8, :], ident[:], reads=i3keys + ["c_ident"], writes=[ktp])
            yield "C"
            P.op("dve", "tensor_copy", ikT_g[:, tl * 128:(tl + 1) * 128], tpp[0:96, 0, :], reads=[ktp], writes=[K("ikTg")])
            yield "C"
            P.op("dve", "tensor_scalar", iw_all[:, t, :], pjt[:, 1824:1832], 1.0 / 16.0, None, op0=ALU.mult, reads=[kpj], writes=[K("iw_all", t)])
            yield "C"
        else:
            z, a_, e_, m_ = lft[:, 0, :], lft[:, 1, :], lft[:, 2, :], lft[:, 3, :]
            P.op("dve", "tensor_tensor", out=z, in0=pjt[:, 3072:3088], in1=bfb[:], op=ALU.add, reads=[kpj, K("bfb")], writes=[K("lft", 0)])
            yield "C"
            P.op("act", "activation", a_, z, AF.Abs, reads=[K("lft", 0)], writes=[K("lft", 1)])
            yield "C"
            P.op("act", "activation", e_, a_, AF.Exp, scale=-1.0, reads=[K("lft", 1)], writes=[K("lft", 2)])
            yield "C"
            P.op("act", "activation", e_, e_, AF.Ln, bias=one[:], scale=1.0, reads=[K("lft", 2), "c_one"], writes=[K("lft", 2)])
            yield "C"
            P.op("dve", "tensor_scalar", m_, z, 0.0, None, op0=ALU.min, reads=[K("lft", 0)], writes=[K("lft", 3)])
            yield "C"
            P.op("dve", "tensor_tensor", out=m_, in0=m_, in1=e_, op=ALU.subtract, reads=[K("lft", 3), K("lft", 2)], writes=[K("lft", 3)])
            yield "C"
            P.op("pe", "transpose", tpf[:, :], m_, identf[:], reads=[K("lft", 3), "c_identf"], writes=[K("tpf")])
            yield "C"
            P.op("dve", "tensor_copy", lfT[:, t * 128:(t + 1) * 128], tpf[:, :], reads=[K("tpf")], writes=[K("lfT", t)])
            yield "C"
        if layer == 1 and t == NT - 1:
            P.dma("sp", io["logfT"], lfT[:], reads=[K("lfT", t_) for t_ in range(NT)], writes=["dram_logfT"])
            yield "C"
            if "on_lf" in io:
                io["on_lf"](["dram_logfT"])
        if tl == 3:
            for q8 in range(4):
                if layer == 0:
                    dst = (io["qT"], None, io["qT"], None)[q8]
                    h0 = (0, 0, 8, 8)[q8]
                else:
                    dst = (io["qT"], io["qT"], None, None)[q8]
                    h0 = (0, 8, 0, 8)[q8]
                if dst is io["qT"]:
                    P.dma("act", dst[:, h0:h0 + 8, g * 512:(g + 1) * 512], qkT_g[:, q8 * 8:(q8 + 1) * 8, :], reads=[K("qkTg", q8)], writes=[("dram_qkT", g, q8)])
                    yield "C"
                else:
                    P.dma("act", io["kT_p"][g][:, h0:h0 + 8, :], qkT_g[:, q8 * 8:(q8 + 1) * 8, :], reads=[K("qkTg", q8)], writes=[("dram_qkT", g, q8)])
                    gkeys.append(("dram_qkT", g, q8))
                    yield "C"
            if layer == 0:
                P.dma("act", io["iqT"][:, :, g * 512:(g + 1) * 512], iqT_g[:], reads=[K("iqTg")], writes=[("dram_iqT", g)])
                yield "C"
                P.dma("act", io["ikT_p"][g], ikT_g[:], reads=[K("ikTg")], writes=[("dram_ikT", g)])
                gkeys.append(("dram_ikT", g))
                yield "C"
            if "on_group" in io:
                io["on_group"](g, list(gkeys))

    gens = [tile_body(t) for t in range(NT)]

    def adv(gen, stops):
        while True:
            try:
                tag = next(gen)
            except StopIteration:
                return None
            if tag in stops:
                return tag

    for t_ in range(min(2, NT)):
        P.dma("sp", xt[t_ % 2][:], io["x"][t_ * 128:(t_ + 1) * 128, :], writes=[K("x", t_ % 2)])
    adv(gens[0], ("Bend",))
    for t in range(NT):
        cur = gens[t]
        if t + 1 < NT:
            nxt = gens[t + 1]
            adv(nxt, ("A",))
            alive = True
            while True:
                tag = adv(nxt, ("B", "Bend"))
                for _ in range(7):
                    if alive and adv(cur, ("C",)) is None:
                        alive = False
                if tag != "B":
                    break
        adv(cur, ())
    if layer == 0:
        P.dma("sp", io["iw"].rearrange("(t p) h -> p t h", p=128), iw_all[:], reads=[K("iw_all", t) for t in range(NT)], writes=["dram_iw"])


TOPK = 256
NITER = 20
BLO = -64.0
BW = 128.0
NEG = -1.0e30
MASK_DT = FP8
SAT = {"saturate": False}


def emit_attn(P, C, layer, io, NG=4, LA=2):
    ident, eps = C["ident"], C["eps"]
    pfx = "a%d_" % layer
    GK = io.get("gk", {})
    L0 = (layer == 0)
    NT = NG * 4
    Kd = 70

    def K(n, i=None):
        return (pfx + n) if i is None else (pfx + n, i)

    n_s = 3
    ps_s = [P.ps(pfx + "pss%d" % i, [128, 512], F32) for i in range(n_s)]
    n_os = 1 if L0 else 2
    ps_o_raw = [[P.ps(pfx + "pso%d_%d" % (s, b), [128, 512], F32) for b in range(2)] for s in range(n_os)]
    def oview_psum(os_, b, vd):
        return ps_o_raw[os_][b][:, 0:2 * (vd + 1)].rearrange("p (a b) -> p a b", b=vd + 1)
    n_x = 8 - n_s - 2 * n_os
    ps_x = [P.ps(pfx + "psx%d" % i, [128, 512], F32) for i in range(n_x)]
    xc = [0]

    def next_x():
        i = xc[0] % n_x
        xc[0] += 1
        return ps_x[i], K("psx", i)

    wout = load_weight_bf16(P, io["w_out"], 1024, pfx + "wo")
    if L0:
        score = P.sb(pfx + "score", [128, 8192], F32)
    kcol_i = P.sb(pfx + "kcol_i", [128, 64], I32)
    kcol = P.sb(pfx + "kcol", [128, 64], F32)
    qrow = P.sb(pfx + "qrow", [128, NT * 128], F32)
    P.dma("sp", qrow[:], io["qkey_row"].partition_broadcast(128), writes=[K("qrow")])
    if L0:
        qrow_b = P.sb(pfx + "qrow_b", [128, NT * 128], BF16)
        P.op("pool", "tensor_copy", qrow_b[:], qrow[:], reads=[K("qrow")], writes=[K("qrow_b")])
    if L0:
        P.op("pool", "iota", kcol_i[0:64, :], pattern=[[2, 64]], base=0, channel_multiplier=0, writes=[K("kcol_i")])
        P.op("pool", "iota", kcol_i[64:128, :], pattern=[[2, 64]], base=1, channel_multiplier=0, writes=[K("kcol_i")])
    else:
        P.op("pool", "iota", kcol_i[:, :], pattern=[[128, 64]], base=0, channel_multiplier=1, writes=[K("kcol_i")])
    P.op("dve", "tensor_copy", kcol[:], kcol_i[:], reads=[K("kcol_i")], writes=[K("kcol")])

    if L0:
        ikT = P.sb(pfx + "ikT", [96, 8192], BF16)
        for i_ in range(4):
            P.dma("sp", ikT[:, :].rearrange("d (i j q) -> d i j q", i=4, j=4, q=512)[:, i_, :, :], io["ikT_g"][i_], reads=GK.get(("ikT", i_), []), writes=[K("ikT", i_)])
        iw = P.sb(pfx + "iw", [128, NT, 8], F32)
        P.dma("sp", iw[:], io["iw"], writes=[K("iw")])
        qcrel = P.sb(pfx + "qcrel", [128, NT], F32)
        P.dma("sp", qcrel[:], io["qchk_col"], writes=[K("qcrel")])
        for i in range(NG):
            if i > 0:
                P.op("dve", "tensor_scalar", qcrel[:, 4 * i:4 * i + 4], qcrel[:, 4 * i:4 * i + 4], -32.0 * i, None, op0=ALU.add, reads=[K("qcrel")], writes=[K("qcrel")])
        relk_i = score[:, 4096:6144].bitcast(I32).rearrange("p (a b) -> p a b", b=64)
        relk = P.sb(pfx + "relk", [128, 2048], BF16)
        P.op("pool", "iota", relk_i, pattern=[[1, 32], [0, 64]], base=0, channel_multiplier=0, writes=[K("relk_i"), K("score")])
        P.op("dve", "tensor_copy", relk[:], relk_i.rearrange("p a b -> p (a b)"), reads=[K("relk_i"), K("score")], writes=[K("relk")])
        maskT = [P.sb(pfx + "maskT0", [128, 16 * (NG - 1 if NG > 1 else 1), 512], MASK_DT), P.sb(pfx + "maskT1", [128, 16 * NG, 512], MASK_DT)]
        mkb = [P.sb(pfx + "mk%d" % i, [128, 2048], BF16) for i in range(2)]
        rbuf = [P.sb(pfx + "r%d" % i, [128, 512], F32) for i in range(2)]
        bs = P.sb(pfx + "bs", [128, 8], F32)
        junkA = P.sb(pfx + "junkA", [128, 2048], FP8)
        iqb = [P.sb(pfx + "iqb%d" % i_, [96, 8, 128], BF16) for i_ in range(2)]
        ident2 = P.sb(pfx + "ident2", [128, 128], BF16)
        P.op("dve", "tensor_scalar", ident2[:], ident[:], 2.0, None, op0=ALU.mult, reads=["c_ident"], writes=[K("ident2")])
        L4 = P.sb(pfx + "L4", [128, 4, 64], F32)
        for i, nm in enumerate(["lq1", "lk1", "lq2", "lk2"]):
            P.dma("sp", L4[:, i, :], io[nm].partition_broadcast(128), writes=[K("L4", i)])
        lm = P.sb(pfx + "lm", [128, 8], F32)
        lj = P.sb(pfx + "lj", [128, 64], F32)
        for j in range(2):
            P.op("dve", "tensor_tensor", out=lj[:], in0=L4[:, 2 * j, :], in1=L4[:, 2 * j + 1, :], op=ALU.mult, reads=[K("L4", 2 * j), K("L4", 2 * j + 1)], writes=[K("lj")])
            P.op("dve", "tensor_reduce", out=lm[:, j:j + 1], in_=lj[:], axis=AX.X, op=ALU.add, reads=[K("lj")], writes=[K("lm", j)])
            P.op("act", "activation", lm[:, 2 + j:3 + j], lm[:, j:j + 1], AF.Exp, reads=[K("lm", j)], writes=[K("lm", 2 + j)])
        lam_init = 0.8 - 0.6 * math.exp(-0.3 * layer)
        P.op("dve", "tensor_tensor", out=lm[:, 4:5], in0=lm[:, 3:4], in1=lm[:, 2:3], op=ALU.subtract, reads=[K("lm", 2), K("lm", 3)], writes=[K("lm", 4)])
        P.op("dve", "tensor_scalar", lm[:, 5:6], lm[:, 4:5], -lam_init, None, op0=ALU.add, reads=[K("lm", 4)], writes=[K("neglam")])
        gsub = P.sb(pfx + "gsub", [128, 128], F32)
        P.dma("sp", gsub[:], io["subln"].partition_broadcast(128), writes=[K("gsub")])
        P.op("dve", "tensor_scalar", gsub[:], gsub[:], 1.0 - lam_init, None, op0=ALU.mult, reads=[K("gsub")], writes=[K("gsub")])
        o1 = P.sb(pfx + "o1", [128, 4, 128], F32)
        o2 = P.sb(pfx + "o2", [128, 4, 128], F32)
        sst = P.sb(pfx + "sst", [128, 3, 4], F32)

    NRING = 3
    Kc = [P.sb(pfx + "Kc%d" % i, [Kd, 2048], BF16) for i in range(NRING)]
    vdmax = 128 if L0 else 64
    Vc = [P.sb(pfx + "Vc%d" % i, [128, 16, vdmax + 1], BF16) for i in range(NRING)]
    for i in range(NRING):
        P.op("pool", "memset", Vc[i][:, :, 0:1], 1.0, writes=[K("Vone", i)])
        if L0:
            P.op("pool", "memset", Kc[i][64:70, :], 0.0, writes=[K("Kc6", i)])
    if L0:
        QTj = [P.sb(pfx + "QTj%d" % i_, [70, 512], BF16) for i_ in range(3)]
        for i_ in range(3):
            P.op("pool", "memset", QTj[i_][64:70, :], 0.0, writes=[K("QTj6", i_)])
    else:
        QTg = P.sb(pfx + "QTg", [Kd, 16, 512], BF16)
    NPT = 4
    PT = [P.sb(pfx + "pt%d" % i, [128, 512], BF16) for i in range(NPT)]
    ao = P.sb(pfx + "ao", [128, 4, 1024], BF16)
    aT = P.sb(pfx + "aT", [128, 8, 128], BF16)
    if L0:
        xt = [qrow[:, 0:1024], qrow[:, 1024:2048]]
    else:
        xt = [P.sb(pfx + "x%d" % i, [128, 1024], F32) for i in range(2)]
    rc = P.sb(pfx + "rc", [128, 4], F32)
    if L0:
        ocp = P.sb(pfx + "ocp", [128, 2, 258], F32)
    if not L0:
        negm = P.sb(pfx + "negm", [128, 16, 512], BF16)


    def prepass_steps(i):
        steps = []
        nch = i + 1
        nkg = 4 * nch
        q0 = i * 512
        mT = maskT[i % 2]
        mb = i % 2

        def s_load(qt):
            t = 4 * i + qt
            P.dma("pool", iqb[t % 2][:], io["iqT"][:, :, q0 + qt * 128:q0 + (qt + 1) * 128], writes=[K("iqb", t % 2)])

        def s_score(qt, kg, h):
            t = 4 * i + qt
            sk = K("score", kg)
            px, kpx = next_x()
            P.op("pe", "matmul", px[:, :], lhsT=iqb[t % 2][:, h, :], rhs=ikT[:, kg * 512:(kg + 1) * 512], start=True, stop=True,
                 reads=[K("iqb", t % 2), K("ikT", kg // 4)], writes=[kpx])
            r = rbuf[(kg * 8 + h) % 2]
            kr = K("r", (kg * 8 + h) % 2)
            P.op("act", "activation", r[:], px[:, :], AF.Relu, reads=[kr], writes=[kpx, kr])
            sc = score[:, kg * 512:(kg + 1) * 512]
            if h == 0:
                P.op("dve", "tensor_scalar", sc, r[:], iw[:, t, 0:1], None, op0=ALU.mult, reads=[kr, K("iw")], writes=[sk] + ([K("score")] if first_score[0] else []))
                first_score[0] = False
            else:
                P.op("dve", "scalar_tensor_tensor", out=sc, in0=r[:], scalar=iw[:, t, h:h + 1], in1=sc, op0=ALU.mult, op1=ALU.add, reads=[kr, K("iw"), sk], writes=[sk])

        def s_pen(qt, kg):
            t = 4 * i + qt
            sk = K("score", kg)
            a = kg - 4 * i
            r = rbuf[0]
            P.op("dve", "tensor_scalar", r[:], relk[:, a * 512:(a + 1) * 512], qcrel[:, t:t + 1], NEG, op0=ALU.is_gt, op1=ALU.mult,
                 reads=[K("relk"), K("qcrel")], writes=[K("r", 0)])
            P.op("dve", "tensor_tensor", out=score[:, kg * 512:(kg + 1) * 512], in0=score[:, kg * 512:(kg + 1) * 512], in1=r[:], op=ALU.add,
                 reads=[K("r", 0), sk], writes=[sk])

        skeys = [K("score", kg) for kg in range(nkg)]

        def s_binit():
            P.op("dve", "memset", bs[:, 1:2], BLO + BW / 2, writes=[K("bs", 1)])

        n_act = nch // 2
        n_d = nch - n_act
        BIG = 1.0e5

        def s_bis(it):
            w = BW / (2.0 ** (it + 1))
            if n_act:
                P.op("dve", "tensor_scalar", bs[:, 5:6], bs[:, 1:2], BIG, None, op0=ALU.mult, reads=[K("bs", 1)], writes=[K("bs", 5)])
            for c in range(n_d):
                P.op("dve", "tensor_scalar", mkb[0][:], score[:, c * 2048:(c + 1) * 2048], bs[:, 1:2], (bs[:, 2:3] if c > 0 else None),
                     op0=ALU.is_ge, op1=ALU.add, accum_out=bs[:, 2:3],
                     reads=skeys[4 * c:4 * c + 4] + [K("bs", 1)] + ([K("bs", 2)] if c > 0 else []), writes=[K("mk", 0), K("bs", 2)])
            for a_ in range(n_act):
                c = n_d + a_
                P.op("act", "activation", junkA[:], score[:, c * 2048:(c + 1) * 2048], AF.Tanh, bias=bs[:, 5:6], scale=-BIG, accum_out=bs[:, 6 + a_:7 + a_],
                     reads=skeys[4 * c:4 * c + 4] + [K("bs", 5)], writes=[K("junkA"), K("bs", 6 + a_)], saturate=False)
            thr = float(TOPK) - 0.5
            if n_act:
                P.op("dve", "scalar_tensor_tensor", out=bs[:, 2:3], in0=bs[:, 2:3], scalar=2.0, in1=bs[:, 6:7], op0=ALU.mult, op1=ALU.subtract,
                     reads=[K("bs", 2), K("bs", 6)], writes=[K("bs", 2)])
                if n_act == 2:
                    P.op("dve", "tensor_tensor", out=bs[:, 2:3], in0=bs[:, 2:3], in1=bs[:, 7:8], op=ALU.subtract, reads=[K("bs", 2), K("bs", 7)], writes=[K("bs", 2)])
                thr = 2.0 * thr - n_act * 2048.0
            P.op("dve", "tensor_scalar", bs[:, 3:4], bs[:, 2:3], thr, w, op0=ALU.is_ge, op1=ALU.mult, reads=[K("bs", 2)], writes=[K("bs", 3)])
            P.op("dve", "scalar_tensor_tensor", out=bs[:, 1:2], in0=bs[:, 3:4], scalar=-w / 2.0, in1=bs[:, 1:2], op0=ALU.add, op1=ALU.add,
                 reads=[K("bs", 3), K("bs", 1)], writes=[K("bs", 1)])

        def s_bfin():
            P.op("dve", "tensor_scalar", bs[:, 0:1], bs[:, 1:2], -BW / (2.0 ** (NITER + 1)), None, op0=ALU.add, reads=[K("bs", 1)], writes=[K("bs", 0)])

        def s_mask(qt, c):
            mk = mkb[c % 2]
            kmk = K("mk", c % 2)
            P.op("dve", "tensor_scalar", mk[:], score[:, c * 2048:(c + 1) * 2048], bs[:, 0:1], -240.0, op0=ALU.is_lt, op1=ALU.mult,
                 reads=skeys[4 * c:4 * c + 4] + [K("bs", 0)], writes=[kmk])

        def s_tr(qt, c, hf):
            mk = mkb[c % 2]
            kmk = K("mk", c % 2)
            px, kpx = next_x()
            pxb = px[:, :].bitcast(BF16).rearrange("p (a b) -> p a b", b=128)
            for j in range(8):
                kt = hf * 8 + j
                P.op("pe", "transpose", pxb[:, j, :], mk[:, kt * 128:(kt + 1) * 128], ident[:], reads=[kmk, "c_ident"], writes=[kpx])
            dst = mT[:, c * 16 + hf * 8:c * 16 + hf * 8 + 8, qt * 128:(qt + 1) * 128]
            if hf == 0:
                P.op("act", "copy", dst, pxb[:, 0:8, :], writes=[kpx, K("maskT", (mb, c * 2 + hf, qt))], **SAT)
            else:
                P.op("act", "copy", dst, pxb[:, 0:8, :], writes=[kpx, K("maskT", (mb, c * 2 + hf, qt))], **SAT)

        for qt in range(4):
            steps.append(lambda qt=qt: s_load(qt))
            for kg in range(nkg):
                for h in range(8):
                    steps.append(lambda qt=qt, kg=kg, h=h: s_score(qt, kg, h))
                if kg >= 4 * i:
                    steps.append(lambda qt=qt, kg=kg: s_pen(qt, kg))
            steps.append(s_binit)
            for it in range(NITER):
                steps.append(lambda it=it: s_bis(it))
            steps.append(s_bfin)
            for c in range(nch):
                steps.append(lambda qt=qt, c=c: s_mask(qt, c))
                for hf in range(2):
                    steps.append(lambda qt=qt, c=c, hf=hf: s_tr(qt, c, hf))
        return steps

    slot_base = [0]
    order = ([1, 2, 3, 0] if (L0 and NG == 4) else list(range(NG)))
    first_score = [True]
    for oi, i in enumerate(order):
        nch = i + 1
        q0 = i * 512
        if not L0:
            P.dma("sp", QTg[0:64, :, :], io["qT"][:, :, q0:q0 + 512], writes=[K("QTg")])
            P.dma("sp", QTg[64:70, :, :], io["caug_q"][:, :, q0:q0 + 512], writes=[K("QTg6")])
        qkeys = [] if L0 else [K("QTg"), K("QTg6")]
        if not L0:
            for kt in range(16):
                ktg = i * 16 + kt
                P.op("dve", "tensor_scalar", negm[:, kt, :], qrow[:, q0:q0 + 512], kcol[:, ktg:ktg + 1], -30000.0, op0=ALU.is_lt, op1=ALU.mult,
                     reads=[K("qrow"), K("kcol")], writes=[K("negm", kt)])

        if L0 and oi == 0:
            for st_ in prepass_steps(i):
                st_()
        nxt_steps = prepass_steps(order[oi + 1]) if (L0 and oi + 1 < NG) else []

        if L0:
            jobs = [dict(mode="dsa", vd=64, qh=h, kh=h, vname="va_g", vh=h) for h in range(8)]
            jobs += [dict(mode="chunk", vd=128, qh=8 + j, kh=8 + j, vname="vb_g", vh=j // 2, pair=j) for j in range(8)]
        else:
            jobs = [dict(mode="causal", vd=64, qh=h, kh=h, vname="vc_g", vh=h) for h in range(16)]
        steps = [(ji, c) for ji in range(len(jobs)) for c in range(nch)]
        units = [(ji, c, kt) for (ji, c) in steps for kt in range(16)]

        def slot_of(si):
            return (slot_base[0] + si) % NRING

        def emit_load2(si):
            ji, c = steps[si]
            jb = jobs[ji]
            sl = slot_of(si)
            if L0 and c == 0:
                P.dma("sp", QTj[ji % 3][0:64, :], io["qT"][:, jb["qh"], q0:q0 + 512], writes=[K("QTj", ji % 3)])
            P.dma("sp", Kc[sl][0:64, :].rearrange("d (r q) -> d r q", r=4), io["kT_g"][c][:, :, jb["kh"], :], reads=GK.get(("kT", c), []), writes=[K("Kc", sl)])
            if not L0:
                P.dma("sp", Kc[sl][64:70, :], io["caug_k"][jb["kh"], :, c * 2048:(c + 1) * 2048], writes=[K("Kc6", sl)])
            vd = jb["vd"]
            for r in range(4):
                P.dma("sp", Vc[sl][:, 4 * r:4 * r + 4, 1:1 + vd], io[jb["vname"]][c][jb["vh"]][:, r, :, :], reads=GK.get((jb["vname"], c), []), writes=[K("Vc", sl)])

        def ksl(si):
            sl = slot_of(si)
            return [K("Kc", sl), K("Kc6", sl)]

        def emit_qk(u):
            ji, c, kt = units[u]
            jb = jobs[ji]
            si = ji * nch + c
            sl = slot_of(si)
            ps = ps_s[u % n_s]
            kps = K("pss", u % n_s)
            addm = (jb["mode"] == "causal" and c == i) or jb["mode"] == "dsa"
            P.op("pe", "matmul", ps[:, :], lhsT=Kc[sl][0:Kd, kt * 128:(kt + 1) * 128], rhs=(QTj[ji % 3][:, :] if L0 else QTg[0:Kd, jb["qh"], :]), start=True, stop=(not addm),
                 reads=ksl(si) + qkeys + ([K("QTj", ji % 3), K("QTj6", ji % 3)] if L0 else []), writes=[kps])
            if jb["mode"] == "dsa":
                ktg_ = c * 16 + kt
                P.op("pe", "matmul", ps[:, :], lhsT=ident2[:, :], rhs=maskT[i % 2][:, ktg_, :], start=False, stop=True,
                     reads=[K("ident2")] + [K("maskT", (i % 2, ktg_ // 8, qt)) for qt in range(4)], writes=[kps])
            elif addm:
                P.op("pe", "matmul", ps[:, :], lhsT=ident[:, :], rhs=negm[:, kt, :], start=False, stop=True,
                     reads=["c_ident", K("negm", kt)], writes=[kps])
            pt = PT[u % NPT]
            kpt = K("pt", u % NPT)
            P.op("act", "activation", pt[:], ps[:, :], AF.Exp, scale=0.125, reads=[kps], writes=[kpt])
            ktg = c * 16 + kt
            if c == i and jb["mode"] == "chunk":
                P.op("dve", "scalar_tensor_tensor", out=pt[:], in0=qrow_b[:, q0:q0 + 512], scalar=kcol[:, ktg:ktg + 1], in1=pt[:], op0=ALU.is_ge, op1=ALU.mult,
                     reads=[kpt, K("qrow_b"), K("kcol")], writes=[kpt])

        def emit_pv(u):
            ji, c, kt = units[u]
            jb = jobs[ji]
            si = ji * nch + c
            sl = slot_of(si)
            vd = jb["vd"]
            os_ = ji % n_os
            pt = PT[u % NPT]
            kpt = K("pt", u % NPT)
            first = (c == 0 and kt == 0)
            last = (c == nch - 1 and kt == 15)
            for s in range(4):
                ob = oview_psum(os_, s // 2, vd)
                P.op("pe", "matmul", ob[:, s % 2, 0:vd + 1], lhsT=pt[:, s * 128:(s + 1) * 128], rhs=Vc[sl][:, kt, 0:vd + 1],
                     start=(first and s % 2 == 0), stop=last, skip_group_check=True,
                     reads=[kpt, K("Vc", sl), K("Vone", sl)], writes=[K("pso", (os_, s // 2))])
            if last:
                emit_norm(ji)

        def emit_norm(ji):
            jb = jobs[ji]
            vd = jb["vd"]
            os_ = ji % n_os
            okeys = [K("pso", (os_, 0)), K("pso", (os_, 1))]
            if L0:
                P.op("act", "copy", ocp[:, 0, 0:2 * (vd + 1)], ps_o_raw[os_][0][:, 0:2 * (vd + 1)], writes=[okeys[0], K("ocp", 0)])
                P.op("dve", "tensor_copy", ocp[:, 1, 0:2 * (vd + 1)], ps_o_raw[os_][1][:, 0:2 * (vd + 1)], writes=[okeys[1], K("ocp", 1)])
                okeys = [K("ocp", 0), K("ocp", 1)]

                def oview(os__, b_, vd_):
                    return ocp[:, b_, 0:2 * (vd_ + 1)].rearrange("p (a b) -> p a b", b=vd_ + 1)
            else:
                oview = oview_psum
            for s in range(4):
                ob = oview(os_, s // 2, vd)
                P.op("dve", "reciprocal", rc[:, s:s + 1], ob[:, s % 2, 0:1], writes=[okeys[s // 2], K("rc", s)])
            if jb["mode"] != "chunk":
                h = jb["qh"]
                for s in range(4):
                    ob = oview(os_, s // 2, vd)
                    if s // 2 == 0:
                        P.op("act", "activation", ao[:, s, h * 64:(h + 1) * 64], ob[:, s % 2, 1:1 + vd], AF.Copy, scale=rc[:, s:s + 1],
                             reads=[K("rc", s)], writes=[okeys[s // 2], K("ao", (s, h))])
                    else:
                        P.op("dve", "tensor_scalar", ao[:, s, h * 64:(h + 1) * 64], ob[:, s % 2, 1:1 + vd], rc[:, s:s + 1], None, op0=ALU.mult,
                             reads=[K("rc", s)], writes=[okeys[s // 2], K("ao", (s, h))])
            else:
                j = jb["pair"]
                hb, m = j // 2, j % 2
                dst = o1 if m == 0 else o2
                for s in range(4):
                    ob = oview(os_, s // 2, vd)
                    P.op("dve", "tensor_scalar", dst[:, s, :], ob[:, s % 2, 1:1 + vd], rc[:, s:s + 1], None, op0=ALU.mult,
                         reads=[K("rc", s)], writes=[okeys[s // 2], K("o12", (m, s))])
                if m == 1:
                    o12 = [K("o12", (mm_, s)) for mm_ in range(2) for s in range(4)]
                    P.op("dve", "scalar_tensor_tensor", out=o1[:], in0=o2[:], scalar=lm[:, 5:6], in1=o1[:], op0=ALU.mult, op1=ALU.add,
                         reads=o12 + [K("neglam")], writes=o12)
                    P.op("dve", "tensor_tensor", out=o2[:], in0=o1[:], in1=o1[:], op=ALU.mult, reads=o12, writes=o12)
                    P.op("dve", "tensor_reduce", out=sst[:, 0, :], in_=o2[:], axis=AX.X, op=ALU.add, reads=o12, writes=[K("sst", 0)])
                    P.op("act", "activation", sst[:, 1, :], sst[:, 0, :], AF.Sqrt, bias=eps[:], scale=1.0 / 128, reads=[K("sst", 0), "c_eps"], writes=[K("sst", 1)])
                    P.op("dve", "reciprocal", sst[:, 2, :], sst[:, 1, :], reads=[K("sst", 1)], writes=[K("sst", 2)])
                    P.op("dve", "tensor_tensor", out=o1[:], in0=o1[:], in1=sst[:, 2, :].unsqueeze(2).to_broadcast([128, 4, 128]), op=ALU.mult,
                         reads=o12 + [K("sst", 2)], writes=o12)
                    P.op("dve", "tensor_tensor", out=ao[:, :, 512 + hb * 128:512 + (hb + 1) * 128], in0=o1[:], in1=gsub[:].unsqueeze(1).to_broadcast([128, 4, 128]), op=ALU.mult,
                         reads=o12 + [K("gsub")], writes=[K("ao", (s, 8 + 2 * hb + e)) for s in range(4) for e in range(2)])

        nst = len(steps)
        nxt_done = [0]
        emit_load2(0)
        if nst > 1:
            emit_load2(1)
        nu = len(units)
        for idx in range(nu + LA):
            if idx < nu:
                emit_qk(idx)
            if idx - LA >= 0:
                emit_pv(idx - LA)
            if idx < nu:
                ji, c, kt = units[idx]
                si = ji * nch + c
                if kt == LA and si + 2 < nst:
                    emit_load2(si + 2)
                want = ((idx + 1) * len(nxt_steps)) // nu
                while nxt_done[0] < want:
                    nxt_steps[nxt_done[0]]()
                    nxt_done[0] += 1
        slot_base[0] = (slot_base[0] + nst) % NRING
        while nxt_done[0] < len(nxt_steps):
            nxt_steps[nxt_done[0]]()
            nxt_done[0] += 1

        aokeys = lambda s: [K("ao", (s, h)) for h in range(16)]
        for s in range(4):
            t = 4 * i + s
            px, kpx = next_x()
            pxb = px[:, :].bitcast(BF16).rearrange("p (a b) -> p a b", b=128)
            for kc in range(8):
                P.op("pe", "transpose", pxb[:, kc, :], ao[:, s, kc * 128:(kc + 1) * 128], ident[:], reads=aokeys(s) + ["c_ident"], writes=[kpx])
            P.op("act", "copy", aT[:], pxb[:, 0:8, :], reads=[kpx], writes=[K("aT")])
            x = xt[t % 2]
            kx = K("x", t % 2)
            P.dma("sp", x[:], io["x"][t * 128:(t + 1) * 128, :], writes=[kx] + ([K("qrow")] if L0 else []))
            for n in range(2):
                px, kpx = next_x()
                for kc in range(8):
                    P.op("pe", "matmul", px[:, :], lhsT=aT[:, kc, :], rhs=wout[:, kc, n * 512:(n + 1) * 512], start=(kc == 0), stop=(kc == 7),
                         reads=[K("aT")] + wkeys(pfx + "wo", n * 512, (n + 1) * 512), writes=[kpx])
                P.op("dve", "tensor_tensor", out=x[:, n * 512:(n + 1) * 512], in0=x[:, n * 512:(n + 1) * 512], in1=px[:, :], op=ALU.add, reads=[kx, kpx], writes=[kx])
            P.dma("sp", io["out"][t * 128:(t + 1) * 128, :], x[:], reads=[kx], writes=[("dram_out", t)])
            if s == 3:
                P.dma("sp", io["tails"][2 * i:2 * i + 2, :], x[126:128, :], reads=[kx], writes=[("dram_tails", i)])


DFF = 2816
NFC = 44
NGC = 22


def emit_ffn(P, C, io, NG=4, pfx="f_"):
    ident, eps = C["ident"], C["eps"]

    def K(n, i=None):
        return (pfx + n) if i is None else (pfx + n, i)

    ps_u = [P.ps(pfx + "psu%d" % i, [128, 512], F32) for i in range(3)]
    ps_h = [P.ps(pfx + "psh%d" % i, [128, 512], F32) for i in range(2)]
    ps_d = [P.ps(pfx + "psd%d" % i, [128, 512], F32) for i in range(2)]
    ps_t = P.ps(pfx + "pst", [128, 8, 128], BF16)

    xg = P.sb(pfx + "xg", [128, 4, 1024], F32)
    wup = load_weight_bf16(P, io["w_up"], 2 * DFF, pfx + "wup")
    wdn = P.sb(pfx + "wdn", [128, NGC, 1024], BF16)
    wdv = io["w_down"].rearrange("(c p) n -> p c n", p=128)
    for c0 in range(0, NGC, 4):
        c1 = min(c0 + 4, NGC)
        P.dma("pool", wdn[:, c0:c1, :], wdv[:, c0:c1, :], writes=[K("wdn", c0)])
    wdn_keys = [K("wdn", c0) for c0 in range(0, NGC, 4)]
    lnb = P.sb(pfx + "lnb", [128, 1024], F32)
    P.dma("sp", lnb[:], io["ln"].partition_broadcast(128), writes=[K("lnb")])
    cw = P.sb(pfx + "cw", [128, 4, NFC], F32)
    P.dma("sp", cw[:], io["cw"], writes=[K("cw")])

    xh = P.sb(pfx + "xh", [2, 1024], F32)
    e1h = P.sb(pfx + "e1h", [2, 4], F32)
    P.dma("sp", e1h[:], io["e1h"][0:2, :], writes=[K("e1h")])
    st = P.sb(pfx + "st", [128, 4], F32)
    hb = [P.sb(pfx + "h%d" % i, [128, 1024], BF16) for i in range(2)]
    hT = P.sb(pfx + "hT", [128, 8, 514], BF16)
    U = [P.sb(pfx + "U%d" % i, [128, 514], F32) for i in range(2)]
    cv = [P.sb(pfx + "cv%d" % i, [128, 512], F32) for i in range(3)]
    sg = [P.sb(pfx + "sg%d" % i, [128, 512], F32) for i in range(2)]
    actT = P.sb(pfx + "actT", [128, NGC, 512], BF16)
    cand = actT[0:2, 0:4, :].rearrange("p a b -> p (a b)").bitcast(F32)

    for g in range(NG):
        ck = [K("cand")] + [K("actT", fc) for fc in range(4)]
        for jp in range(4):
            if jp == 0 and g == 0:
                P.op("pool", "memset", cand, 0.0, writes=ck)
            else:
                r_src = (jp - 1) % 4
                g_src = g - (1 if jp == 0 else 0)
                P.dma("sp", cand, io["tails_g"][r_src, 2 * g_src:2 * g_src + 2, :], reads=io.get("gk_tails", []), writes=ck)
            if jp == 0:
                P.op("dve", "tensor_scalar", xh[:], cand, e1h[:, 0:1], None, op0=ALU.mult, reads=ck + [K("e1h")], writes=[K("xh")])
            else:
                P.op("dve", "scalar_tensor_tensor", out=xh[:], in0=cand, scalar=e1h[:, jp:jp + 1], in1=xh[:], op0=ALU.mult, op1=ALU.add,
                     reads=ck + [K("e1h"), K("xh")], writes=[K("xh")])
        P.dma("sp", xg[:], io["x"][g * 512:(g + 1) * 512, :].rearrange("(t p) d -> p t d", p=128), writes=[K("xg")])
        for tl in range(-1, 4):
            npart = 2 if tl < 0 else 128
            src = xh[:, :] if tl < 0 else xg[:, tl, :]
            ksrc = K("xh") if tl < 0 else K("xg")
            b = (tl + 1) % 2
            P.op("act", "activation", hb[b][0:npart, :], src, AF.Square, accum_out=st[0:npart, 0:1], reads=[ksrc], writes=[K("h", b), K("st", 0)])
            P.op("act", "activation", st[0:npart, 1:2], st[0:npart, 0:1], AF.Sqrt, bias=eps[0:npart, :], scale=1.0 / 1024, reads=[K("st", 0), "c_eps"], writes=[K("st", 1)])
            P.op("dve", "reciprocal", st[0:npart, 2:3], st[0:npart, 1:2], reads=[K("st", 1)], writes=[K("st", 2)])
            h = hb[b]
            P.op("dve", "scalar_tensor_tensor", out=h[0:npart, :], in0=src, scalar=st[0:npart, 2:3], in1=lnb[0:npart, :], op0=ALU.mult, op1=ALU.mult,
                 reads=[ksrc, K("st", 2), K("lnb")], writes=[K("h", b)])
            if tl < 0:
                for kc in range(8):
                    P.op("pe", "transpose", ps_t[:, kc, 0:2], h[0:2, kc * 128:(kc + 1) * 128], ident[0:2, 0:2], reads=[K("h", b), "c_ident"], writes=[K("pst")])
                P.op("act", "copy", hT[:, :, 0:2], ps_t[:, :, 0:2], writes=[K("pst"), K("hT", -1)])
            else:
                for kc in range(8):
                    P.op("pe", "transpose", ps_t[:, kc, :], h[:, kc * 128:(kc + 1) * 128], ident[:], reads=[K("h", b), "c_ident"], writes=[K("pst")])
                P.op("act", "copy", hT[:, :, 2 + tl * 128:2 + (tl + 1) * 128], ps_t[:, :, :], writes=[K("pst"), K("hT", tl)])
        hTkeys = [K("hT", tl) for tl in range(-1, 4)]

        def up_chunk(fc, n):
            pu = ps_u[n % 3]
            kpu = K("psu", n % 3)
            ph = ps_h[n % 2]
            kph = K("psh", n % 2)
            for kc in range(8):
                P.op("pe", "matmul", pu[:, :], lhsT=wup[:, kc, fc * 128:(fc + 1) * 128], rhs=hT[:, kc, 2:514], start=(kc == 0), stop=(kc == 7),
                     reads=hTkeys + wkeys(pfx + "wup", fc * 128, (fc + 1) * 128), writes=[kpu])
            for kc in range(8):
                P.op("pe", "matmul", ph[:, 0:2], lhsT=wup[:, kc, fc * 128:(fc + 1) * 128], rhs=hT[:, kc, 0:2], start=(kc == 0), stop=(kc == 7),
                     reads=hTkeys + wkeys(pfx + "wup", fc * 128, (fc + 1) * 128), writes=[kph])
            u = U[n % 2]
            ku = K("U", n % 2)
            c = cv[n % 3]
            kc_ = K("cv", n % 3)
            P.op("act", "copy", u[:, 2:514], pu[:, :], writes=[kpu, (ku, 1)])
            P.op("act", "activation", c[:], pu[:, :], AF.Identity, bias=cw[:, 3, fc:fc + 1], scale=cw[:, 2, fc:fc + 1], reads=[K("cw")], writes=[kpu, kc_])
            P.op("act", "copy", u[:, 0:2], ph[:, 0:2], writes=[kph, (ku, 0)])
            P.op("dve", "scalar_tensor_tensor", out=c[:], in0=u[:, 1:513], scalar=cw[:, 1, fc:fc + 1], in1=c[:], op0=ALU.mult, op1=ALU.add, reads=[(ku, 0), (ku, 1), K("cw"), kc_], writes=[kc_])
            P.op("dve", "scalar_tensor_tensor", out=c[:], in0=u[:, 0:512], scalar=cw[:, 0, fc:fc + 1], in1=c[:], op0=ALU.mult, op1=ALU.add, reads=[(ku, 0), (ku, 1), K("cw"), kc_], writes=[kc_])
            return c, kc_

        n = 0
        for fc in range(NGC):
            cg, kcg = up_chunk(fc, n)
            n += 1
            cvl, kcv = up_chunk(fc + NGC, n)
            n += 1
            s = sg[fc % 2]
            ks = K("sg", fc % 2)
            P.op("act", "activation", s[:], cg[:], AF.Silu, reads=[kcg], writes=[ks])
            P.op("dve", "tensor_tensor", out=actT[:, fc, :], in0=s[:], in1=cvl[:], op=ALU.mult, reads=[ks, kcv], writes=[K("actT", fc)])
        akeys = [K("actT", fc) for fc in range(NGC)]

        for tl in range(4):
            t = g * 4 + tl
            for nh in range(2):
                pd = ps_d[(tl * 2 + nh) % 2]
                kpd = K("psd", (tl * 2 + nh) % 2)
                for fc in range(NGC):
                    P.op("pe", "matmul", pd[:, :], lhsT=actT[:, fc, tl * 128:(tl + 1) * 128], rhs=wdn[:, fc, nh * 512:(nh + 1) * 512], start=(fc == 0), stop=(fc == NGC - 1),
                         reads=akeys + wdn_keys, writes=[kpd])
                P.op("dve", "tensor_tensor", out=xg[:, tl, nh * 512:(nh + 1) * 512], in0=xg[:, tl, nh * 512:(nh + 1) * 512], in1=pd[:, :], op=ALU.add,
                     reads=[K("xg")], writes=[kpd, K("xo", (tl, nh))])
            P.dma("sp", io["out"][t * 128:(t + 1) * 128, :], xg[:, tl, :], reads=[K("xo", (tl, 0)), K("xo", (tl, 1)), K("xg")], writes=[("dram_out", t)])


def emit_cumsum(P, io):
    pfx = "cs_"

    def K(n, i=None):
        return (pfx + n) if i is None else (pfx + n, i)
    lfb = P.sb(pfx + "lfb", [16, 8192], F32)
    for j in range(4):
        P.dma("sp", lfb[:, :].rearrange("h (i j q) -> h j i q", i=4, j=4, q=512)[:, j, :, :],
              io["logf_g"][:, j, :].rearrange("h (i q) -> h i q", q=512), reads=io.get("gk_lf", []), writes=[K("lfb")])
    lfo = P.sb(pfx + "lfo", [16, 2048], F32)
    P.dma("sp", lfo[:], io["logf_own"], writes=[K("lfo")])
    e1h = P.sb(pfx + "e1h", [16, 4], F32)
    P.dma("sp", e1h[:], io["e1h"][0:16, :], writes=[K("e1h")])
    ones = P.sb(pfx + "ones", [16, 512], F32)
    P.op("dve", "memset", ones[:], 1.0, writes=[K("ones")])
    c = P.sb(pfx + "c", [16, 8192], F32)
    cq = P.sb(pfx + "cq", [16, 2048], F32)
    Pc = P.sb(pfx + "Pc", [16, 17], F32)
    P.op("dve", "memset", Pc[:, 0:1], 0.0, writes=[K("Pc", 0)])
    for g in range(16):
        P.op("dve", "tensor_tensor_scan", c[:, g * 512:(g + 1) * 512], ones[:], lfb[:, g * 512:(g + 1) * 512], Pc[:, g:g + 1], op0=ALU.mult, op1=ALU.add,
             reads=[K("ones"), K("lfb"), K("Pc", g)], writes=[K("c", g)])
        P.op("dve", "tensor_copy", Pc[:, g + 1:g + 2], c[:, (g + 1) * 512 - 1:(g + 1) * 512], reads=[K("c", g)], writes=[K("Pc", g + 1)])
    ini = P.sb(pfx + "ini", [16, 4], F32)
    for i in range(4):
        P.op("dve", "tensor_scalar", ini[:, i:i + 1], Pc[:, 4 * i:4 * i + 1], e1h[:, 0:1], None, op0=ALU.mult, reads=[K("Pc", 4 * i), K("e1h")], writes=[K("ini", i)])
        for j in range(1, 4):
            P.op("dve", "scalar_tensor_tensor", out=ini[:, i:i + 1], in0=Pc[:, 4 * i + j:4 * i + j + 1], scalar=e1h[:, j:j + 1], in1=ini[:, i:i + 1], op0=ALU.mult, op1=ALU.add,
                 reads=[K("Pc", 4 * i + j), K("e1h"), K("ini", i)], writes=[K("ini", i)])
        P.op("dve", "tensor_tensor_scan", cq[:, i * 512:(i + 1) * 512], ones[:], lfo[:, i * 512:(i + 1) * 512], ini[:, i:i + 1], op0=ALU.mult, op1=ALU.add,
             reads=[K("ones"), K("lfo"), K("ini", i)], writes=[K("cq", i)])
    c8 = P.sb(pfx + "c8", [16, 2048], F32)
    r = P.sb(pfx + "r", [16, 2048], F32)
    o = [P.sb(pfx + "o%d" % i, [16, 6, 2048], BF16) for i in range(2)]
    for n in range(5):
        b = n % 2
        src = cq[:, :] if n == 4 else c[:, n * 2048:(n + 1) * 2048]
        ksrc = [K("cq", i) for i in range(4)] if n == 4 else [K("c", g) for g in range(4 * n, 4 * n + 4)]
        ot = o[b]
        ko = K("o", b)
        P.op("dve", "tensor_scalar", c8[:], src, 8.0, None, op0=ALU.mult, reads=ksrc, writes=[K("c8")])
        P.op("dve", "tensor_copy", ot[:, 0, :], c8[:], reads=[K("c8")], writes=[(ko, 0)])
        P.op("dve", "tensor_tensor", out=r[:], in0=c8[:], in1=ot[:, 0, :], op=ALU.subtract, reads=[K("c8"), (ko, 0)], writes=[K("r")])
        P.op("dve", "tensor_copy", ot[:, 1, :], r[:], reads=[K("r")], writes=[(ko, 1)])
        P.op("dve", "tensor_tensor", out=r[:], in0=r[:], in1=ot[:, 1, :], op=ALU.subtract, reads=[K("r"), (ko, 1)], writes=[K("r")])
        P.op("dve", "tensor_copy", ot[:, 2, :], r[:], reads=[K("r")], writes=[(ko, 2)])
        hk = [(ko, 0), (ko, 1), (ko, 2)]
        if n == 4:
            P.op("dve", "memset", ot[:, 3:6, :], 1.0, writes=[(ko, 3)])
            P.dma("sp", io["caug_q"].rearrange("r h q -> h r q"), ot[:], reads=hk + [(ko, 3)], writes=["dram_cq"] + hk + [(ko, 3)])
        else:
            P.op("dve", "tensor_scalar", ot[:, 3:6, :], ot[:, 0:3, :], -1.0, None, op0=ALU.mult, reads=hk, writes=[(ko, 3)])
            P.op("dve", "memset", ot[:, 0:3, :], 1.0, reads=[(ko, 3)], writes=hk)
            P.dma("sp", io["caug_k"][:, :, n * 2048:(n + 1) * 2048], ot[:], reads=hk + [(ko, 3)], writes=[("dram_ck", n)] + hk + [(ko, 3)])


S_ = 8192
NCORES = 8
GROUPS = [[0, 1, 2, 3], [4, 5, 6, 7]]


def _toks(j, NG=4):
    return np.concatenate([np.arange(512) + 512 * (4 * i + j) for i in range(NG)])


def build_fused():
    nc = bass.Bass("TRN2", target_bir_lowering=False)
    NTOK = 2048

    def din(name, shape, dt=F32):
        return nc.dram_tensor(name, list(shape), dt, kind="ExternalInput").ap()

    def scr(name, shape, dt):
        return nc.dram_tensor(name, list(shape), dt)

    E = {}
    for name, shape in [("x", [NTOK, 1024]), ("pos", [128, 16]), ("qchk_row", [NTOK]), ("qpos_row", [NTOK]), ("qchk_col", [128, 16]), ("e1h", [16, 4]),
                        ("ln_mix", [2, 1024]), ("ln_ffn", [2, 1024]), ("ev_w_in", [1024, 3368]), ("ev_w_out", [1024, 1024]),
                        ("ev_a_qnorm", [64]), ("ev_a_knorm", [64]), ("ev_idx_knorm", [32]), ("ev_b_qnorm", [64]), ("ev_b_knorm", [64]),
                        ("ev_lam_q1", [64]), ("ev_lam_k1", [64]), ("ev_lam_q2", [64]), ("ev_lam_k2", [64]), ("ev_b_subln", [128]),
                        ("od_w_in", [1024, 3088]), ("od_b_f", [16]), ("od_w_out", [1024, 1024]), ("od_c_qnorm", [64]), ("od_c_knorm", [64]),
                        ("ffn_up", [2, 1024, 5632]), ("cw", [2, 128, 4, 44]), ("ffn_down", [2, 2816, 1024])]:
        E[name] = din(name, shape)
    out = nc.dram_tensor("out", [NTOK, 1024], F32, kind="ExternalOutput").ap()

    T = {}
    specs = [("qT0", [1024, NTOK], BF16), ("iqT", [768, NTOK], BF16), ("iw", [NTOK, 8], F32),
             ("x1", [NTOK, 1024], F32), ("tl1", [8, 1024], F32), ("tl1_g", [32, 1024], F32), ("x2", [NTOK, 1024], F32),
             ("qT1", [1024, NTOK], BF16), ("lf", [16, NTOK], F32), ("lf_g", [64, NTOK], F32),
             ("cak", [96, S_], BF16), ("caq", [96, NTOK], BF16),
             ("x3", [NTOK, 1024], F32), ("tl3", [8, 1024], F32), ("tl3_g", [32, 1024], F32)]
    for g in range(4):
        specs += [("kT0_%d" % g, [1024, 512], BF16), ("kT0_%d_g" % g, [4096, 512], BF16), ("kT1_%d" % g, [1024, 512], BF16), ("kT1_%d_g" % g, [4096, 512], BF16),
                  ("vc_%d" % g, [2048, 256], BF16), ("vc_%d_g" % g, [8192, 256], BF16),
                  ("va_%d" % g, [1024, 256], BF16), ("va_%d_g" % g, [4096, 256], BF16), ("vb_%d" % g, [512, 512], BF16), ("vb_%d_g" % g, [2048, 512], BF16),
                  ("ikT_%d" % g, [96, 512], BF16), ("ikT_%d_g" % g, [384, 512], BF16)]
    for name, shape, dt in specs:
        T[name] = scr("s_" + name, shape, dt)

    def A(name):
        return T[name].ap()

    with ExitStack() as es:
        P = Prog(nc, es)
        C = build_consts(P)
        with P.phase():
            io = {"x": E["x"], "pos": E["pos"], "ln": E["ln_mix"][0], "w_in": E["ev_w_in"],
                  "gains": [E["ev_a_qnorm"], E["ev_a_knorm"], E["ev_b_qnorm"], E["ev_b_knorm"]], "g_ik": E["ev_idx_knorm"],
                  "qT": A("qT0").rearrange("(d h) q -> d h q", h=16), "kT_p": [A("kT0_%d" % g).rearrange("(h d) q -> d h q", h=16) for g in range(4)],
                  "iqT": A("iqT").rearrange("(d h) q -> d h q", h=8), "ikT_p": [A("ikT_%d" % g) for g in range(4)], "iw": A("iw"),
                  "va_p": [A("va_%d" % g).rearrange("(h p) (t d) -> p h t d", p=128, d=64) for g in range(4)],
                  "vb_p": [A("vb_%d" % g).rearrange("(h p) (t d) -> p h t d", p=128, d=128) for g in range(4)]}

            def on_group0(g, keys):
                for nm in ["ikT_%d" % g, "kT0_%d" % g, "va_%d" % g, "vb_%d" % g]:
                    P.coll(T[nm], T[nm + "_g"], GROUPS, reads=keys, writes=[("g", nm)])
            io["on_group"] = on_group0
            emit_proj(P, C, 0, io, 16)
        gk0 = {}
        for g in range(4):
            gk0[("ikT", g)] = [("g", "ikT_%d" % g)]
            gk0[("kT", g)] = [("g", "kT0_%d" % g)]
            gk0[("va_g", g)] = [("g", "va_%d" % g)]
            gk0[("vb_g", g)] = [("g", "vb_%d" % g)]
        with P.phase():
            io = {"qT": A("qT0").rearrange("(d h) q -> d h q", h=16), "kT_g": [A("kT0_%d_g" % g).rearrange("(r h d) q -> d r h q", r=4, h=16) for g in range(4)],
                  "ikT_g": [A("ikT_%d_g" % g).rearrange("(j d) q -> d j q", j=4) for g in range(4)], "iqT": A("iqT").rearrange("(d h) q -> d h q", h=8),
                  "iw": A("iw").rearrange("(t p) h -> p t h", p=128), "qkey_row": E["qchk_row"], "qchk_col": E["qchk_col"],
                  "va_g": [A("va_%d_g" % g).rearrange("(r h p) (t d) -> h p r t d", r=4, p=128, d=64) for g in range(4)],
                  "vb_g": [A("vb_%d_g" % g).rearrange("(r h p) (t d) -> h p r t d", r=4, p=128, d=128) for g in range(4)],
                  "lq1": E["ev_lam_q1"], "lk1": E["ev_lam_k1"], "lq2": E["ev_lam_q2"], "lk2": E["ev_lam_k2"], "subln": E["ev_b_subln"],
                  "x": E["x"], "w_out": E["ev_w_out"], "out": A("x1"), "tails": A("tl1"), "gk": gk0}
            emit_attn(P, C, 0, io, NG=4)
        P.coll(T["tl1"], T["tl1_g"], GROUPS, writes=[("g", "tl1")])
        with P.phase():
            io = {"x": A("x1"), "tails_g": A("tl1_g").rearrange("(r g) d -> r g d", r=4), "e1h": E["e1h"], "ln": E["ln_ffn"][0], "w_up": E["ffn_up"][0],
                  "cw": E["cw"][0], "w_down": E["ffn_down"][0], "out": A("x2"), "gk_tails": [("g", "tl1")]}
            emit_ffn(P, C, io, NG=4, pfx="f0_")
        with P.phase():
            io = {"x": A("x2"), "pos": E["pos"], "ln": E["ln_mix"][1], "w_in": E["od_w_in"], "gains": [E["od_c_qnorm"], E["od_c_knorm"]], "b_f": E["od_b_f"],
                  "qT": A("qT1").rearrange("(d h) q -> d h q", h=16), "kT_p": [A("kT1_%d" % g).rearrange("(h d) q -> d h q", h=16) for g in range(4)],
                  "vc_p": [A("vc_%d" % g).rearrange("(h p) (t d) -> p h t d", p=128, d=64) for g in range(4)], "logfT": A("lf")}

            def on_group1(g, keys):
                for nm in ["kT1_%d" % g, "vc_%d" % g]:
                    P.coll(T[nm], T[nm + "_g"], GROUPS, reads=keys, writes=[("g", nm)])
            io["on_group"] = on_group1
            io["on_lf"] = lambda keys: P.coll(T["lf"], T["lf_g"], GROUPS, reads=keys, writes=[("g", "lf")])
            emit_proj(P, C, 1, io, 16)
        gk1 = {}
        for g in range(4):
            gk1[("kT", g)] = [("g", "kT1_%d" % g)]
            gk1[("vc_g", g)] = [("g", "vc_%d" % g)]
        with P.phase():
            io = {"logf_g": A("lf_g").rearrange("(j h) q -> h j q", j=4), "logf_own": A("lf"), "e1h": E["e1h"],
                  "caug_k": A("cak").rearrange("(h r) q -> h r q", r=6), "caug_q": A("caq").rearrange("(r h) q -> r h q", h=16), "gk_lf": [("g", "lf")]}
            emit_cumsum(P, io)
        with P.phase():
            io = {"qT": A("qT1").rearrange("(d h) q -> d h q", h=16), "kT_g": [A("kT1_%d_g" % g).rearrange("(r h d) q -> d r h q", r=4, h=16) for g in range(4)],
                  "qkey_row": E["qpos_row"], "vc_g": [A("vc_%d_g" % g).rearrange("(r h p) (t d) -> h p r t d", r=4, p=128, d=64) for g in range(4)],
                  "caug_k": A("cak").rearrange("(h r) q -> h r q", r=6), "caug_q": A("caq").rearrange("(r h) q -> r h q", h=16),
                  "x": A("x2"), "w_out": E["od_w_out"], "out": A("x3"), "tails": A("tl3"), "gk": gk1}
            emit_attn(P, C, 1, io, NG=4)
        P.coll(T["tl3"], T["tl3_g"], GROUPS, writes=[("g", "tl3")])
        with P.phase():
            io = {"x": A("x3"), "tails_g": A("tl3_g").rearrange("(r g) d -> r g d", r=4), "e1h": E["e1h"], "ln": E["ln_ffn"][1], "w_up": E["ffn_up"][1],
                  "cw": E["cw"][1], "w_down": E["ffn_down"][1], "out": out, "gk_tails": [("g", "tl3")]}
            emit_ffn(P, C, io, NG=4, pfx="f1_")
        P.finish()
        P.emit()
    return nc


def kernel(**inp):
    inp = {k: np.asarray(v) for k, v in inp.items()}
    x = np.ascontiguousarray(inp["x"], dtype=np.float32)
    nc = build_fused()
    cw = np.stack([np.concatenate([inp["ffn_conv"][l], inp["ffn_conv_b"][l][None]], 0).reshape(4, 44, 128).transpose(2, 0, 1) for l in range(2)], 0)
    shared = {"ln_mix": inp["ln_mix"], "ln_ffn": inp["ln_ffn"], "ev_w_in": inp["ev_w_in"][0], "ev_w_out": inp["ev_w_out"][0],
              "od_w_in": inp["od_w_in"][0], "od_b_f": inp["od_b_f"][0], "od_w_out": inp["od_w_out"][0],
              "ffn_up": inp["ffn_up"], "cw": np.ascontiguousarray(cw, dtype=np.float32), "ffn_down": inp["ffn_down"]}
    for k in ["ev_a_qnorm", "ev_a_knorm", "ev_idx_knorm", "ev_b_qnorm", "ev_b_knorm", "ev_lam_q1", "ev_lam_k1", "ev_lam_q2", "ev_lam_k2", "ev_b_subln",
              "od_c_qnorm", "od_c_knorm"]:
        shared[k] = inp[k][0]
    shared = {k: np.ascontiguousarray(v, dtype=np.float32) for k, v in shared.items()}
    maps = []
    for c in range(NCORES):
        b, j = c // 4, c % 4
        toks = _toks(j)
        m = dict(shared)
        e1h = np.zeros((16, 4), np.float32)
        e1h[:, j] = 1.0
        m.update(x=np.ascontiguousarray(x[b, toks]), pos=np.ascontiguousarray(toks.astype(np.float32).reshape(16, 128).T),
                 qchk_row=(toks // 64).astype(np.float32), qpos_row=toks.astype(np.float32),
                 qchk_col=np.ascontiguousarray((toks // 64).astype(np.float32).reshape(16, 128).T), e1h=e1h)
        maps.append(m)
    res = run_bass_kernel_spmd(nc, maps, core_ids=list(range(NCORES)))
    out = np.empty_like(x)
    for c in range(NCORES):
        out[c // 4, _toks(c % 4)] = np.asarray(res.results[c]["out"])
    return out.astype(np.float32)
```

```python
import math
from contextlib import ExitStack
import numpy as np
import concourse.bass as bass
import concourse.mybir as mybir
from concourse.bass_utils import run_bass_kernel_spmd


F32 = mybir.dt.float32
BF16 = mybir.dt.bfloat16
I32 = mybir.dt.int32
FP8 = mybir.dt.float8e4
ALU = mybir.AluOpType
AF = mybir.ActivationFunctionType
AX = mybir.AxisListType

ENGS = ["pe", "act", "dve", "pool", "sp"]
N_DMA_SEMS = 32
N_SW_SEMS = 8


class Prog:
    def __init__(self, nc, es, same_engine_sync=True):
        self.nc = nc
        self.es = es
        self.same = same_engine_sync
        self.sems = {}
        for e in ENGS:
            self.sems["e_" + e] = es.enter_context(nc.semaphore("se_" + e))
        self.cnt = {e: 0 for e in ENGS}
        self.known = {e: {} for e in ENGS}
        self.ops = {e: [] for e in ENGS}
        self.reg = {}
        self.dsem = []
        for i in range(N_DMA_SEMS):
            nm = "d_%d" % i
            self.sems[nm] = es.enter_context(nc.semaphore("sd_%d" % i))
            self.dsem.append([nm, 0])
        self.drr = 0
        self.drr_sw = 0
        self.n_inst = 0
        self.cur_es = es
        self.csem = []
        self.n_coll = 0

    def phase(self):
        prog = self

        class _Ph:
            def __enter__(self_):
                self_.st = ExitStack()
                self_.prev = prog.cur_es
                prog.cur_es = self_.st
                return self_

            def __exit__(self_, *a):
                prog.barrier()
                prog.cur_es = self_.prev
                self_.st.close()
                return False
        return _Ph()

    def barrier(self):
        targets = {}
        for e in ENGS:
            if self.cnt[e] > 0:
                targets["e_" + e] = self.cnt[e]
        for nm, v in self.dsem:
            if v > 0:
                targets[nm] = v
        for e in ENGS:
            waits = []
            for s_, v in targets.items():
                if s_ == "e_" + e:
                    continue
                if self.known[e].get(s_, 0) < v:
                    waits.append((s_, v))
                    self.known[e][s_] = v
            if waits:
                self.ops[e].append((waits, None, None))

    def coll(self, src_t, dst_t, groups, reads=(), writes=()):
        waits = self._deps("pool", reads, writes)
        nm = "c_%d" % self.n_coll
        self.n_coll += 1
        self.sems[nm] = self.es.enter_context(self.nc.semaphore("sc_%d" % (self.n_coll - 1)))
        self.csem.append(nm)
        tag = (nm, 1)
        self._mark(reads, writes, tag)
        kw = dict(replica_groups=groups, ins=[src_t.ap().opt()], outs=[dst_t.ap().opt()])
        self.ops["pool"].append((waits, ("collective_compute", ("AllGather", ALU.bypass), kw), (nm, 1)))
        self.n_inst += 1

    def sb(self, name, shape, dt):
        return self.cur_es.enter_context(self.nc.sbuf_tensor(name, list(shape), dt))

    def ps(self, name, shape, dt):
        return self.cur_es.enter_context(self.nc.psum_tensor(name, list(shape), dt))

    def _deps(self, eng, reads, writes):
        deps = {}

        def add(sv):
            if sv is None:
                return
            s, v = sv
            if deps.get(s, 0) < v:
                deps[s] = v

        for k in reads:
            st = self.reg.get(k)
            if st is not None:
                add(st[0])
        for k in writes:
            st = self.reg.get(k)
            if st is not None:
                add(st[0])
                for s, v in st[1].items():
                    add((s, v))
        waits = []
        me = "e_" + eng
        for s, v in deps.items():
            if s == me and (eng == "pe" or not self.same):
                continue
            if self.known[eng].get(s, 0) < v:
                waits.append((s, v))
                self.known[eng][s] = v
        return waits

    def _mark(self, reads, writes, tag):
        for k in writes:
            self.reg[k] = [tag, {}]
        for k in reads:
            st = self.reg.setdefault(k, [None, {}])
            s, v = tag
            if st[1].get(s, 0) < v:
                st[1][s] = v

    def op(self, eng, meth, *args, reads=(), writes=(), **kw):
        fn = (meth, args, kw)
        waits = self._deps(eng, reads, writes)
        self.cnt[eng] += 1
        tag = ("e_" + eng, self.cnt[eng])
        self._mark(reads, writes, tag)
        self.ops[eng].append((waits, fn, ("e_" + eng, 1)))
        self.n_inst += 1

    def dma(self, q, out, in_, reads=(), writes=(), **kw):
        waits = self._deps(q, reads, writes)
        if q == "pool":
            ent = self.dsem[N_DMA_SEMS - N_SW_SEMS + self.drr_sw]
            self.drr_sw = (self.drr_sw + 1) % N_SW_SEMS
        else:
            ent = self.dsem[self.drr]
            self.drr = (self.drr + 1) % (N_DMA_SEMS - N_SW_SEMS)
        nm, prev = ent
        if prev > 0 and self.known[q].get(nm, 0) < prev:
            waits.append((nm, prev))
            self.known[q][nm] = prev
        ent[1] = prev + 16
        tag = (nm, prev + 16)
        self._mark(reads, writes, tag)
        kw = dict(kw); kw["out"] = out; kw["in_"] = in_
        self.ops[q].append((waits, ("dma_start", (), kw), (nm, 16)))
        self.n_inst += 1

    def finish(self):
        waits = []
        for nm, v in self.dsem:
            if v > 0 and self.known["sp"].get(nm, 0) < v:
                waits.append((nm, v))
        self.ops["sp"].append((waits, None, None))

    def emit(self):
        nc = self.nc
        block = self.es.enter_context(nc.Block())
        names = {"pe": "tensor", "act": "scalar", "dve": "vector", "pool": "gpsimd", "sp": "sync"}

        def mk(e):
            def body(engine):
                for waits, fn, inc in self.ops[e]:
                    for s, v in waits:
                        engine.wait_ge(self.sems[s], v)
                    if fn is not None:
                        inst = getattr(engine, fn[0])(*fn[1], **fn[2])
                        if inc is not None:
                            inst.then_inc(self.sems[inc[0]], inc[1])
            return body

        for e in ENGS:
            getattr(block, names[e])(mk(e))


D = 1024
EPS = 1e-6
THETA = 500000.0
TWO_PI = 2.0 * math.pi
C1 = 6.28125
C2 = TWO_PI - 6.28125


def build_consts(P):
    C = {}
    it = P.sb("c_iota_i", [128, 128], I32)
    itf = P.sb("c_iota_f", [128, 128], F32)
    ident = P.sb("c_ident", [128, 128], BF16)
    identf = P.sb("c_identf", [128, 128], F32)
    P.op("pool", "iota", it[:], pattern=[[1, 128]], base=0, channel_multiplier=-1, writes=["c_iota_i"])
    P.op("dve", "tensor_copy", itf[:], it[:], reads=["c_iota_i"], writes=["c_iota_f"])
    P.op("dve", "tensor_single_scalar", ident[:], itf[:], 0.0, op=ALU.is_equal, reads=["c_iota_f"], writes=["c_ident"])
    P.op("dve", "tensor_single_scalar", identf[:], itf[:], 0.0, op=ALU.is_equal, reads=["c_iota_f"], writes=["c_identf"])
    eps = P.sb("c_eps", [128, 1], F32)
    one = P.sb("c_one", [128, 1], F32)
    P.op("dve", "memset", eps[:], EPS, writes=["c_eps"])
    P.op("dve", "memset", one[:], 1.0, writes=["c_one"])
    C["ident"] = ident
    C["identf"] = identf
    C["eps"] = eps
    C["one"] = one
    return C


def rope_tables(P, posf, kp, NT, half, name):
    cos = P.sb(name + "_cos", [128, NT, half], F32)
    sin = P.sb(name + "_sin", [128, NT, half], F32)
    ang = P.sb(name + "_ang", [128, NT, half], F32)
    kf = P.sb(name + "_kf", [128, NT, half], F32)
    ki = P.sb(name + "_ki", [128, NT, half], I32)
    r = P.sb(name + "_r", [128, NT, half], F32)
    m = P.sb(name + "_m", [128, NT, half], F32)
    ka, kk, kr, km = name + "_ang", name + "_kf", name + "_r", name + "_m"
    for i in range(half):
        inv = float(np.float32(THETA ** (-float(i) / half)))
        P.op("dve", "tensor_scalar", ang[:, :, i], posf[:], inv, None, op0=ALU.mult, reads=[kp], writes=[ka])
    for which, dst, shift in (("s", sin, 0.0), ("c", cos, math.pi / 2)):
        kd = name + ("_sin" if which == "s" else "_cos")
        P.op("dve", "tensor_scalar", kf[:], ang[:], shift, 1.0 / TWO_PI, op0=ALU.add, op1=ALU.mult, reads=[ka], writes=[kk])
        P.op("dve", "tensor_copy", ki[:], kf[:], reads=[kk], writes=[name + "_ki"])
        P.op("dve", "tensor_copy", kf[:], ki[:], reads=[name + "_ki"], writes=[kk])
        P.op("dve", "scalar_tensor_tensor", out=r[:], in0=kf[:], scalar=-C1, in1=ang[:], op0=ALU.mult, op1=ALU.add, reads=[kk, ka], writes=[kr])
        P.op("dve", "scalar_tensor_tensor", out=r[:], in0=kf[:], scalar=-C2, in1=r[:], op0=ALU.mult, op1=ALU.add, reads=[kk, kr], writes=[kr])
        if shift != 0.0:
            P.op("dve", "tensor_scalar", r[:], r[:], shift, None, op0=ALU.add, reads=[kr], writes=[kr])
        P.op("dve", "tensor_scalar", m[:], r[:], math.pi, -TWO_PI, op0=ALU.is_gt, op1=ALU.mult, reads=[kr], writes=[km])
        P.op("dve", "tensor_tensor", out=r[:], in0=r[:], in1=m[:], op=ALU.add, reads=[kr, km], writes=[kr])
        P.op("dve", "tensor_scalar", m[:], r[:], -math.pi, TWO_PI, op0=ALU.is_lt, op1=ALU.mult, reads=[kr], writes=[km])
        P.op("dve", "tensor_tensor", out=r[:], in0=r[:], in1=m[:], op=ALU.add, reads=[kr, km], writes=[kr])
        P.op("dve", "tensor_scalar", r[:], r[:], -3.14159, 3.14159, op0=ALU.max, op1=ALU.min, reads=[kr], writes=[kr])
        P.op("act", "activation", dst[:], r[:], AF.Sin, reads=[kr], writes=[kd])
    return cos, sin


def load_weight_bf16(P, w_dram, ncol, name, stg=None, stg_keys=None, kchunks=8, engines=None):
    wb = P.sb(name, [128, kchunks, ncol], BF16)
    wv = w_dram.rearrange("(kc p) n -> p kc n", p=128)
    n0 = 0
    while n0 < ncol:
        n1 = min(n0 + 512, ncol)
        P.dma("pool", wb[:, :, n0:n1], wv[:, :, n0:n1], writes=[(name, n0)])
        n0 = n1
    return wb


def wkeys(name, n0, n1):
    return [(name, c) for c in range((n0 // 512) * 512, n1, 512)]


def emit_proj(P, C, layer, io, NT):
    nc = P.nc
    NCOL = 3368 if layer == 0 else 3088
    NG = NT // 4
    ident, identf, eps, one = C["ident"], C["identf"], C["eps"], C["one"]
    pfx = "p%d_" % layer

    wb = load_weight_bf16(P, io["w_in"], NCOL, pfx + "w")
    lnb = P.sb(pfx + "lnb", [128, D], F32)
    P.dma("sp", lnb[:], io["ln"].partition_broadcast(128), writes=[pfx + "lnb"])
    G = P.sb(pfx + "G", [128, 32, 64], F32)
    G4 = P.sb(pfx + "G4", [128, 4, 64], F32)
    gl = io["gains"]
    for i, g in enumerate(gl):
        P.dma("sp", G4[:, i, :], g.partition_broadcast(128), writes=[(pfx + "G4", i)])
    per = 32 // len(gl)
    for i in range(len(gl)):
        P.op("dve", "tensor_copy", G[:, i * per:(i + 1) * per, :], G4[:, i, :].unsqueeze(1).to_broadcast([128, per, 64]), reads=[(pfx + "G4", i)], writes=[(pfx + "G", i)])
    Gkeys = [(pfx + "G", i) for i in range(len(gl))]
    posf = P.sb(pfx + "posf", [128, NT], F32)
    P.dma("sp", posf[:], io["pos"], writes=[pfx + "posf"])
    if layer == 0:
        cos8, sin8 = rope_tables(P, posf, pfx + "posf", NT, 8, pfx + "r8")
        cos4, sin4 = rope_tables(P, posf, pfx + "posf", NT, 4, pfx + "r4")
        gik = P.sb(pfx + "gik", [128, 32], F32)
        P.dma("sp", gik[:], io["g_ik"].partition_broadcast(128), writes=[pfx + "gik"])
        iw_all = P.sb(pfx + "iw_all", [128, NT, 8], F32)
    else:
        bfb = P.sb(pfx + "bfb", [128, 16], F32)
        P.dma("sp", bfb[:], io["b_f"].partition_broadcast(128), writes=[pfx + "bfb"])
        lfT = P.sb(pfx + "lfT", [16, NT * 128], F32)

    xt = [P.sb(pfx + "x%d" % i, [128, D], F32) for i in range(2)]
    junk = P.sb(pfx + "junk", [128, D], BF16)
    hb = [P.sb(pfx + "h%d" % i, [128, D], BF16) for i in range(2)]
    hT = [P.sb(pfx + "hT%d" % i, [128, 8, 128], BF16) for i in range(2)]
    pj = [P.sb(pfx + "pj%d" % i, [128, NCOL], F32) for i in range(2)]
    st_l = [P.sb(pfx + "st%d" % i, [128, 8], F32) for i in range(2)]
    tmpA_l = [P.sb(pfx + "tmpA%d" % i, [128, 32, 64], F32) for i in range(2)]
    hst_l = [P.sb(pfx + "hst%d" % i, [128, 3, 32], F32) for i in range(2)]
    rt_l = [P.sb(pfx + "rt%d" % i, [128, 4, 32, 8], F32) for i in range(2)]
    nr = [P.sb(pfx + "nr%d" % i, [128, 32, 64], BF16) for i in range(2)]
    vt = [P.sb(pfx + "vt%d" % i, [128, 1024], BF16) for i in range(2)]
    qkT_g = P.sb(pfx + "qkTg", [64, 32, 512], BF16)
    tp = [P.ps(pfx + "tp%d" % i, [128, 8, 128], BF16) for i in range(2)]
    mm = [P.ps(pfx + "mm%d" % i, [128, 512], F32) for i in range(3)]
    if layer == 0:
        tI = P.sb(pfx + "tI", [128, 9, 32], F32)
        rti_l = [P.sb(pfx + "rti%d" % i, [128, 4, 9, 4], F32) for i in range(2)]
        ihi = P.sb(pfx + "ihi", [128, 9, 32], BF16)
        ilo = P.sb(pfx + "ilo", [128, 9, 32], BF16)
        i3 = [P.sb(pfx + "i3_%d" % i, [128, 9, 96], BF16) for i in range(2)]
        iqT_g = P.sb(pfx + "iqTg", [96, 8, 512], BF16)
        ikT_g = P.sb(pfx + "ikTg", [96, 512], BF16)
    else:
        lft = P.sb(pfx + "lft", [128, 4, 16], F32)
        tpf = P.ps(pfx + "tpf", [16, 128], F32)

    def K(n, i=None):
        return (pfx + n) if i is None else (pfx + n, i)

    tpc = 0

    def tile_body(t):
        nonlocal tpc
        b = t % 2
        g, tl = t // 4, t % 4
        st, tmpA, hst, rt = st_l[b], tmpA_l[b], hst_l[b], rt_l[b]

        def KK(n, i=None, b=b):
            return K(n + "_b%d" % b, i)
        x = xt[b]
        kx = K("x", b)
        P.op("act", "activation", junk[:], x[:], AF.Square, accum_out=st[:, 0:1], reads=[kx], writes=[K("junk"), KK("st", 0)])
        P.op("act", "activation", st[:, 1:2], st[:, 0:1], AF.Sqrt, bias=eps[:], scale=1.0 / D, reads=[KK("st", 0), "c_eps"], writes=[KK("st", 1)])
        P.op("dve", "reciprocal", st[:, 2:3], st[:, 1:2], reads=[KK("st", 1)], writes=[KK("st", 2)])
        h = hb[b]
        P.op("dve", "scalar_tensor_tensor", out=h[:], in0=x[:], scalar=st[:, 2:3], in1=lnb[:], op0=ALU.mult, op1=ALU.mult, reads=[kx, KK("st", 2), K("lnb")], writes=[K("h", b)])
        if t + 2 < NT:
            P.dma("sp", x[:], io["x"][(t + 2) * 128:(t + 3) * 128, :], writes=[kx])
        tpp = tp[tpc % 2]
        ktp = K("tp", tpc % 2)
        tpc += 1
        for kc in range(8):
            P.op("pe", "transpose", tpp[:, kc, :], h[:, kc * 128:(kc + 1) * 128], ident[:], reads=[K("h", b), "c_ident"], writes=[ktp])
        hTt = hT[b]
        P.op("act", "copy", hTt[:], tpp[:], reads=[ktp], writes=[K("hT", b)])
        yield "A"
        pjt = pj[b]
        n0 = 0
        gi = 0
        while n0 < NCOL:
            n1 = min(n0 + 512, NCOL)
            m = mm[gi % 3]
            km = K("mm", gi % 3)
            for kc in range(8):
                P.op("pe", "matmul", m[:, 0:n1 - n0], lhsT=hTt[:, kc, :], rhs=wb[:, kc, n0:n1], start=(kc == 0), stop=(kc == 7), reads=[K("hT", b)] + wkeys(pfx + "w", n0, n1), writes=[km])
            if gi % 2 == 0:
                P.op("act", "copy", pjt[:, n0:n1], m[:, 0:n1 - n0], reads=[km], writes=[K("pj", b)])
            else:
                P.op("dve", "tensor_copy", pjt[:, n0:n1], m[:, 0:n1 - n0], reads=[km], writes=[K("pj", b)])
            n0 = n1
            gi += 1
            yield "B"
        yield "Bend"
        kpj = K("pj", b)
        nrt = nr[b]
        knr = K("nr", b)

        def normrope(src, H, h0, rope_half, cos, sin, Gsl, Gk):
            ta = tmpA[:, h0:h0 + H, :]
            kta = KK("tmpA", h0)
            P.op("act", "activation", ta, src, AF.Square, reads=[kpj], writes=[kta])
            yield "C"
            P.op("dve", "tensor_reduce", out=hst[:, 0, h0:h0 + H], in_=ta, axis=AX.X, op=ALU.add, reads=[kta], writes=[KK("hst", h0)])
            yield "C"
            P.op("act", "activation", hst[:, 1, h0:h0 + H], hst[:, 0, h0:h0 + H], AF.Sqrt, bias=eps[:], scale=1.0 / 64, reads=[KK("hst", h0), "c_eps"], writes=[KK("hst1", h0)])
            yield "C"
            P.op("dve", "reciprocal", hst[:, 2, h0:h0 + H], hst[:, 1, h0:h0 + H], reads=[KK("hst1", h0)], writes=[KK("hst2", h0)])
            yield "C"
            P.op("dve", "tensor_tensor", out=ta, in0=src, in1=hst[:, 2, h0:h0 + H].unsqueeze(2).to_broadcast([128, H, 64]), op=ALU.mult, reads=[kpj, KK("hst2", h0)], writes=[kta])
            yield "C"
            P.op("dve", "tensor_tensor", out=ta, in0=ta, in1=Gsl, op=ALU.mult, reads=[kta] + Gk, writes=[kta])
            yield "C"
            if rope_half:
                hf = rope_half
                x1 = tmpA[:, h0:h0 + H, 0:hf]
                x2 = tmpA[:, h0:h0 + H, hf:2 * hf]
                cb = cos[:, t, :].unsqueeze(1).to_broadcast([128, H, hf])
                sb_ = sin[:, t, :].unsqueeze(1).to_broadcast([128, H, hf])
                r = [rt[:, i, h0:h0 + H, :] for i in range(4)]
                krt = KK("rt", h0)
                P.op("dve", "tensor_tensor", out=r[0], in0=x1, in1=cb, op=ALU.mult, reads=[kta], writes=[(krt, 0)])
                yield "C"
                P.op("dve", "tensor_tensor", out=r[1], in0=x2, in1=sb_, op=ALU.mult, reads=[kta], writes=[(krt, 1)])
                yield "C"
                P.op("dve", "tensor_tensor", out=r[2], in0=x2, in1=cb, op=ALU.mult, reads=[kta], writes=[(krt, 2)])
                yield "C"
                P.op("dve", "tensor_tensor", out=r[3], in0=x1, in1=sb_, op=ALU.mult, reads=[kta], writes=[(krt, 3)])
                yield "C"
                P.op("dve", "tensor_tensor", out=nrt[:, h0:h0 + H, 0:hf], in0=r[0], in1=r[1], op=ALU.subtract, reads=[(krt, 0), (krt, 1)], writes=[(knr, h0, 0)])
                yield "C"
                P.op("dve", "tensor_tensor", out=nrt[:, h0:h0 + H, hf:2 * hf], in0=r[2], in1=r[3], op=ALU.add, reads=[(krt, 2), (krt, 3)], writes=[(knr, h0, 1)])
                yield "C"
                P.op("act", "copy", nrt[:, h0:h0 + H, 2 * hf:64], tmpA[:, h0:h0 + H, 2 * hf:64], reads=[kta], writes=[(knr, h0, 2)])
                yield "C"
                return [(knr, h0, 0), (knr, h0, 1), (knr, h0, 2)]
            else:
                P.op("act", "copy", nrt[:, h0:h0 + H, :], ta, reads=[kta], writes=[(knr, h0, 2)])
                yield "C"
                return [(knr, h0, 2)]

        if layer == 0:
            k1 = yield from normrope(pjt[:, 0:1024].rearrange("p (h d) -> p h d", d=64), 16, 0, 8, cos8, sin8, G[:, 0:16, :], Gkeys[0:2])
            k2 = yield from normrope(pjt[:, 1832:2856].rearrange("p (h d) -> p h d", d=64), 16, 16, 8, cos8, sin8, G[:, 16:32, :], Gkeys[2:4])
            nrkeys = {0: k1, 16: k2}
        else:
            k1 = yield from normrope(pjt[:, 0:1024].rearrange("p (h d) -> p h d", d=64), 16, 0, 0, None, None, G[:, 0:16, :], Gkeys[0:1])
            k2 = yield from normrope(pjt[:, 1024:2048].rearrange("p (h d) -> p h d", d=64), 16, 16, 0, None, None, G[:, 16:32, :], Gkeys[1:2])
            nrkeys = {0: k1, 16: k2}
        for q8 in range(4):
            tpp = tp[tpc % 2]
            ktp = K("tp", tpc % 2)
            tpc += 1
            for j in range(8):
                hh = q8 * 8 + j
                P.op("pe", "transpose", tpp[0:64, j, :], nrt[:, hh, :], ident[:], reads=nrkeys[(hh // 16) * 16] + ["c_ident"], writes=[ktp])
                yield "C"
            eng = "act" if q8 % 2 == 0 else "dve"
            if eng == "act":
                P.op("act", "copy", qkT_g[:, q8 * 8:(q8 + 1) * 8, tl * 128:(tl + 1) * 128], tpp[0:64, :, :], reads=[ktp], writes=[K("qkTg", q8)])
                yield "C"
            else:
                P.op("dve", "tensor_copy", qkT_g[:, q8 * 8:(q8 + 1) * 8, tl * 128:(tl + 1) * 128], tpp[0:64, :, :], reads=[ktp], writes=[K("qkTg", q8)])
                yield "C"
        vtt = vt[b]
        if layer == 0:
            P.op("dve", "tensor_copy", vtt[:, 0:512], pjt[:, 1024:1536], reads=[kpj], writes=[K("vt", b)])
            yield "C"
            P.op("dve", "tensor_copy", vtt[:, 512:1024], pjt[:, 2856:3368], reads=[kpj], writes=[K("vt", b)])
            yield "C"
        else:
            P.op("dve", "tensor_copy", vtt[:], pjt[:, 2048:3072], reads=[kpj], writes=[K("vt", b)])
            yield "C"
        gkeys = io.setdefault("grp_keys", {}).setdefault(g, [])
        if layer == 0:
            P.dma("act", io["va_p"][g][:, :, tl, :], vtt[:, 0:512].rearrange("p (h d) -> p h d", d=64), reads=[K("vt", b)], writes=[("dram_va", t)])
            yield "C"
            P.dma("act", io["vb_p"][g][:, :, tl, :], vtt[:, 512:1024].rearrange("p (h d) -> p h d", d=128), reads=[K("vt", b)], writes=[("dram_vb", t)])
            yield "C"
            gkeys += [("dram_va", t), ("dram_vb", t)]
        else:
            P.dma("act", io["vc_p"][g][:, :, tl, :], vtt[:, :].rearrange("p (h d) -> p h d", d=64), reads=[K("vt", b)], writes=[("dram_vc", t)])
            yield "C"
            gkeys += [("dram_vc", t)]
        if layer == 0:
            P.op("dve", "tensor_copy", tI[:, 0:8, :], pjt[:, 1536:1792].rearrange("p (h d) -> p h d", d=32), reads=[kpj], writes=[K("tI", 0)])
            yield "C"
            P.op("act", "activation", junk[:, 0:32], pjt[:, 1792:1824], AF.Square, accum_out=st[:, 3:4], reads=[kpj], writes=[K("junk"), KK("st", 3)])
            yield "C"
            P.op("act", "activation", st[:, 4:5], st[:, 3:4], AF.Sqrt, bias=eps[:], scale=1.0 / 32, reads=[KK("st", 3), "c_eps"], writes=[KK("st", 4)])
            yield "C"
            P.op("dve", "reciprocal", st[:, 5:6], st[:, 4:5], reads=[KK("st", 4)], writes=[KK("st", 5)])
            yield "C"
            P.op("dve", "scalar_tensor_tensor", out=tI[:, 8, :], in0=pjt[:, 1792:1824], scalar=st[:, 5:6], in1=gik[:], op0=ALU.mult, op1=ALU.mult, reads=[kpj, KK("st", 5), K("gik")], writes=[K("tI", 1)])
            yield "C"
            kti = [K("tI", 0), K("tI", 1)]
            x1 = tI[:, :, 0:4]
            x2 = tI[:, :, 4:8]
            cb = cos4[:, t, :].unsqueeze(1).to_broadcast([128, 9, 4])
            sb_ = sin4[:, t, :].unsqueeze(1).to_broadcast([128, 9, 4])
            rr = [rti_l[b][:, i, :, :] for i in range(4)]
            kr = KK("rti")
            P.op("dve", "tensor_tensor", out=rr[0], in0=x1, in1=cb, op=ALU.mult, reads=kti, writes=[(kr, 0)])
            yield "C"
            P.op("dve", "tensor_tensor", out=rr[1], in0=x2, in1=sb_, op=ALU.mult, reads=kti, writes=[(kr, 1)])
            yield "C"
            P.op("dve", "tensor_tensor", out=rr[2], in0=x2, in1=cb, op=ALU.mult, reads=kti, writes=[(kr, 2)])
            yield "C"
            P.op("dve", "tensor_tensor", out=rr[3], in0=x1, in1=sb_, op=ALU.mult, reads=kti, writes=[(kr, 3)])
            yield "C"
            P.op("dve", "tensor_tensor", out=tI[:, :, 0:4], in0=rr[0], in1=rr[1], op=ALU.subtract, reads=[(kr, 0), (kr, 1)], writes=kti)
            yield "C"
            P.op("dve", "tensor_tensor", out=tI[:, :, 4:8], in0=rr[2], in1=rr[3], op=ALU.add, reads=[(kr, 2), (kr, 3)], writes=kti)
            yield "C"
            P.op("dve", "tensor_copy", ihi[:], tI[:], reads=kti, writes=[K("ihi")])
            yield "C"
            P.op("dve", "tensor_tensor", out=ilo[:], in0=tI[:], in1=ihi[:], op=ALU.subtract, reads=kti + [K("ihi")], writes=[K("ilo")])
            yield "C"
            i3t = i3[b]
            ki3 = K("i3", b)
            P.op("dve", "tensor_copy", i3t[:, 0:8, 0:32], ihi[:, 0:8, :], reads=[K("ihi")], writes=[(ki3, 0)])
            yield "C"
            P.op("dve", "tensor_copy", i3t[:, 0:8, 32:64], ihi[:, 0:8, :], reads=[K("ihi")], writes=[(ki3, 1)])
            yield "C"
            P.op("dve", "tensor_copy", i3t[:, 0:8, 64:96], ilo[:, 0:8, :], reads=[K("ilo")], writes=[(ki3, 2)])
            yield "C"
            P.op("dve", "tensor_copy", i3t[:, 8, 0:32], ihi[:, 8, :], reads=[K("ihi")], writes=[(ki3, 3)])
            yield "C"
            P.op("dve", "tensor_copy", i3t[:, 8, 32:64], ilo[:, 8, :], reads=[K("ilo")], writes=[(ki3, 4)])
            yield "C"
            P.op("dve", "tensor_copy", i3t[:, 8, 64:96], ihi[:, 8, :], reads=[K("ihi")], writes=[(ki3, 5)])
            yield "C"
            i3keys = [(ki3, i) for i in range(6)]
            tpp = tp[tpc % 2]
            ktp = K("tp", tpc % 2)
            tpc += 1
            for j in range(8):
                P.op("pe", "transpose", tpp[0:96, j, :], i3t[:, j, :], ident[:], reads=i3keys + ["c_ident"], writes=[ktp])
                yield "C"
            P.op("act", "copy", iqT_g[:, :, tl * 128:(tl + 1) * 128], tpp[0:96, :, :], reads=[ktp], writes=[K("iqTg")])
            yield "C"
            tpp = tp[tpc % 2]
            ktp = K("tp", tpc % 2)
            tpc += 1
            P.op("pe", "transpose", tpp[0:96, 0, :], i3t[:, 8, :], ident[:], reads=i3keys + ["c_ident"], writes=[ktp])
            yield "C"
            P.op("dve", "tensor_copy", ikT_g[:, tl * 128:(tl + 1) * 128], tpp[0:96, 0, :], reads=[ktp], writes=[K("ikTg")])
            yield "C"
            P.op("dve", "tensor_scalar", iw_all[:, t, :], pjt[:, 1824:1832], 1.0 / 16.0, None, op0=ALU.mult, reads=[kpj], writes=[K("iw_all", t)])
            yield "C"
        else:
            z, a_, e_, m_ = lft[:, 0, :], lft[:, 1, :], lft[:, 2, :], lft[:, 3, :]
            P.op("dve", "tensor_tensor", out=z, in0=pjt[:, 3072:3088], in1=bfb[:], op=ALU.add, reads=[kpj, K("bfb")], writes=[K("lft", 0)])
            yield "C"
            P.op("act", "activation", a_, z, AF.Abs, reads=[K("lft", 0)], writes=[K("lft", 1)])
            yield "C"
            P.op("act", "activation", e_, a_, AF.Exp, scale=-1.0, reads=[K("lft", 1)], writes=[K("lft", 2)])
            yield "C"
            P.op("act", "activation", e_, e_, AF.Ln, bias=one[:], scale=1.0, reads=[K("lft", 2), "c_one"], writes=[K("lft", 2)])
            yield "C"
            P.op("dve", "tensor_scalar", m_, z, 0.0, None, op0=ALU.min, reads=[K("lft", 0)], writes=[K("lft", 3)])
            yield "C"
            P.op("dve", "tensor_tensor", out=m_, in0=m_, in1=e_, op=ALU.subtract, reads=[K("lft", 3), K("lft", 2)], writes=[K("lft", 3)])
            yield "C"
            P.op("pe", "transpose", tpf[:, :], m_, identf[:], reads=[K("lft", 3), "c_identf"], writes=[K("tpf")])
            yield "C"
            P.op("dve", "tensor_copy", lfT[:, t * 128:(t + 1) * 128], tpf[:, :], reads=[K("tpf")], writes=[K("lfT", t)])
            yield "C"
        if layer == 1 and t == NT - 1:
            P.dma("sp", io["logfT"], lfT[:], reads=[K("lfT", t_) for t_ in range(NT)], writes=["dram_logfT"])
            yield "C"
            if "on_lf" in io:
                io["on_lf"](["dram_logfT"])
        if tl == 3:
            for q8 in range(4):
                if layer == 0:
                    dst = (io["qT"], None, io["qT"], None)[q8]
                    h0 = (0, 0, 8, 8)[q8]
                else:
                    dst = (io["qT"], io["qT"], None, None)[q8]
                    h0 = (0, 8, 0, 8)[q8]
                if dst is io["qT"]:
                    P.dma("act", dst[:, h0:h0 + 8, g * 512:(g + 1) * 512], qkT_g[:, q8 * 8:(q8 + 1) * 8, :], reads=[K("qkTg", q8)], writes=[("dram_qkT", g, q8)])
                    yield "C"
                else:
                    P.dma("act", io["kT_p"][g][:, h0:h0 + 8, :], qkT_g[:, q8 * 8:(q8 + 1) * 8, :], reads=[K("qkTg", q8)], writes=[("dram_qkT", g, q8)])
                    gkeys.append(("dram_qkT", g, q8))
                    yield "C"
            if layer == 0:
                P.dma("act", io["iqT"][:, :, g * 512:(g + 1) * 512], iqT_g[:], reads=[K("iqTg")], writes=[("dram_iqT", g)])
                yield "C"
                P.dma("act", io["ikT_p"][g], ikT_g[:], reads=[K("ikTg")], writes=[("dram_ikT", g)])
                gkeys.append(("dram_ikT", g))
                yield "C"
            if "on_group" in io:
                io["on_group"](g, list(gkeys))

    gens = [tile_body(t) for t in range(NT)]

    def adv(gen, stops):
        while True:
            try:
                tag = next(gen)
            except StopIteration:
                return None
            if tag in stops:
                return tag

    for t_ in range(min(2, NT)):
        P.dma("sp", xt[t_ % 2][:], io["x"][t_ * 128:(t_ + 1) * 128, :], writes=[K("x", t_ % 2)])
    adv(gens[0], ("Bend",))
    for t in range(NT):
        cur = gens[t]
        if t + 1 < NT:
            nxt = gens[t + 1]
            adv(nxt, ("A",))
            alive = True
            while True:
                tag = adv(nxt, ("B", "Bend"))
                for _ in range(7):
                    if alive and adv(cur, ("C",)) is None:
                        alive = False
                if tag != "B":
                    break
        adv(cur, ())
    if layer == 0:
        P.dma("sp", io["iw"].rearrange("(t p) h -> p t h", p=128), iw_all[:], reads=[K("iw_all", t) for t in range(NT)], writes=["dram_iw"])


TOPK = 256
NITER = 20
BLO = -64.0
BW = 128.0
NEG = -1.0e30
MASK_DT = FP8
SAT = {"saturate": False}


def emit_attn(P, C, layer, io, NG=4, LA=2):
    ident, eps = C["ident"], C["eps"]
    pfx = "a%d_" % layer
    GK = io.get("gk", {})
    L0 = (layer == 0)
    NT = NG * 4
    Kd = 70

    def K(n, i=None):
        return (pfx + n) if i is None else (pfx + n, i)

    n_s = 3
    ps_s = [P.ps(pfx + "pss%d" % i, [128, 512], F32) for i in range(n_s)]
    n_os = 1 if L0 else 2
    ps_o_raw = [[P.ps(pfx + "pso%d_%d" % (s, b), [128, 512], F32) for b in range(2)] for s in range(n_os)]
    def oview_psum(os_, b, vd):
        return ps_o_raw[os_][b][:, 0:2 * (vd + 1)].rearrange("p (a b) -> p a b", b=vd + 1)
    n_x = 8 - n_s - 2 * n_os
    ps_x = [P.ps(pfx + "psx%d" % i, [128, 512], F32) for i in range(n_x)]
    xc = [0]

    def next_x():
        i = xc[0] % n_x
        xc[0] += 1
        return ps_x[i], K("psx", i)

    wout = load_weight_bf16(P, io["w_out"], 1024, pfx + "wo")
    if L0:
        score = P.sb(pfx + "score", [128, 8192], F32)
    kcol_i = P.sb(pfx + "kcol_i", [128, 64], I32)
    kcol = P.sb(pfx + "kcol", [128, 64], F32)
    qrow = P.sb(pfx + "qrow", [128, NT * 128], F32)
    P.dma("sp", qrow[:], io["qkey_row"].partition_broadcast(128), writes=[K("qrow")])
    if L0:
        qrow_b = P.sb(pfx + "qrow_b", [128, NT * 128], BF16)
        P.op("pool", "tensor_copy", qrow_b[:], qrow[:], reads=[K("qrow")], writes=[K("qrow_b")])
    if L0:
        P.op("pool", "iota", kcol_i[0:64, :], pattern=[[2, 64]], base=0, channel_multiplier=0, writes=[K("kcol_i")])
        P.op("pool", "iota", kcol_i[64:128, :], pattern=[[2, 64]], base=1, channel_multiplier=0, writes=[K("kcol_i")])
    else:
        P.op("pool", "iota", kcol_i[:, :], pattern=[[128, 64]], base=0, channel_multiplier=1, writes=[K("kcol_i")])
    P.op("dve", "tensor_copy", kcol[:], kcol_i[:], reads=[K("kcol_i")], writes=[K("kcol")])

    if L0:
        ikT = P.sb(pfx + "ikT", [96, 8192], BF16)
        for i_ in range(4):
            P.dma("sp", ikT[:, :].rearrange("d (i j q) -> d i j q", i=4, j=4, q=512)[:, i_, :, :], io["ikT_g"][i_], reads=GK.get(("ikT", i_), []), writes=[K("ikT", i_)])
        iw = P.sb(pfx + "iw", [128, NT, 8], F32)
        P.dma("sp", iw[:], io["iw"], writes=[K("iw")])
        qcrel = P.sb(pfx + "qcrel", [128, NT], F32)
        P.dma("sp", qcrel[:], io["qchk_col"], writes=[K("qcrel")])
        for i in range(NG):
            if i > 0:
                P.op("dve", "tensor_scalar", qcrel[:, 4 * i:4 * i + 4], qcrel[:, 4 * i:4 * i + 4], -32.0 * i, None, op0=ALU.add, reads=[K("qcrel")], writes=[K("qcrel")])
        relk_i = score[:, 4096:6144].bitcast(I32).rearrange("p (a b) -> p a b", b=64)
        relk = P.sb(pfx + "relk", [128, 2048], BF16)
        P.op("pool", "iota", relk_i, pattern=[[1, 32], [0, 64]], base=0, channel_multiplier=0, writes=[K("relk_i"), K("score")])
        P.op("dve", "tensor_copy", relk[:], relk_i.rearrange("p a b -> p (a b)"), reads=[K("relk_i"), K("score")], writes=[K("relk")])
        maskT = [P.sb(pfx + "maskT0", [128, 16 * (NG - 1 if NG > 1 else 1), 512], MASK_DT), P.sb(pfx + "maskT1", [128, 16 * NG, 512], MASK_DT)]
        mkb = [P.sb(pfx + "mk%d" % i, [128, 2048], BF16) for i in range(2)]
        rbuf = [P.sb(pfx + "r%d" % i, [128, 512], F32) for i in range(2)]
        bs = P.sb(pfx + "bs", [128, 8], F32)
        junkA = P.sb(pfx + "junkA", [128, 2048], FP8)
        iqb = [P.sb(pfx + "iqb%d" % i_, [96, 8, 128], BF16) for i_ in range(2)]
        ident2 = P.sb(pfx + "ident2", [128, 128], BF16)
        P.op("dve", "tensor_scalar", ident2[:], ident[:], 2.0, None, op0=ALU.mult, reads=["c_ident"], writes=[K("ident2")])
        L4 = P.sb(pfx + "L4", [128, 4, 64], F32)
        for i, nm in enumerate(["lq1", "lk1", "lq2", "lk2"]):
            P.dma("sp", L4[:, i, :], io[nm].partition_broadcast(128), writes=[K("L4", i)])
        lm = P.sb(pfx + "lm", [128, 8], F32)
        lj = P.sb(pfx + "lj", [128, 64], F32)
        for j in range(2):
            P.op("dve", "tensor_tensor", out=lj[:], in0=L4[:, 2 * j, :], in1=L4[:, 2 * j + 1, :], op=ALU.mult, reads=[K("L4", 2 * j), K("L4", 2 * j + 1)], writes=[K("lj")])
            P.op("dve", "tensor_reduce", out=lm[:, j:j + 1], in_=lj[:], axis=AX.X, op=ALU.add, reads=[K("lj")], writes=[K("lm", j)])
            P.op("act", "activation", lm[:, 2 + j:3 + j], lm[:, j:j + 1], AF.Exp, reads=[K("lm", j)], writes=[K("lm", 2 + j)])
        lam_init = 0.8 - 0.6 * math.exp(-0.3 * layer)
        P.op("dve", "tensor_tensor", out=lm[:, 4:5], in0=lm[:, 3:4], in1=lm[:, 2:3], op=ALU.subtract, reads=[K("lm", 2), K("lm", 3)], writes=[K("lm", 4)])
        P.op("dve", "tensor_scalar", lm[:, 5:6], lm[:, 4:5], -lam_init, None, op0=ALU.add, reads=[K("lm", 4)], writes=[K("neglam")])
        gsub = P.sb(pfx + "gsub", [128, 128], F32)
        P.dma("sp", gsub[:], io["subln"].partition_broadcast(128), writes=[K("gsub")])
        P.op("dve", "tensor_scalar", gsub[:], gsub[:], 1.0 - lam_init, None, op0=ALU.mult, reads=[K("gsub")], writes=[K("gsub")])
        o1 = P.sb(pfx + "o1", [128, 4, 128], F32)
        o2 = P.sb(pfx + "o2", [128, 4, 128], F32)
        sst = P.sb(pfx + "sst", [128, 3, 4], F32)

    NRING = 3
    Kc = [P.sb(pfx + "Kc%d" % i, [Kd, 2048], BF16) for i in range(NRING)]
    vdmax = 128 if L0 else 64
    Vc = [P.sb(pfx + "Vc%d" % i, [128, 16, vdmax + 1], BF16) for i in range(NRING)]
    for i in range(NRING):
        P.op("pool", "memset", Vc[i][:, :, 0:1], 1.0, writes=[K("Vone", i)])
        if L0:
            P.op("pool", "memset", Kc[i][64:70, :], 0.0, writes=[K("Kc6", i)])
    if L0:
        QTj = [P.sb(pfx + "QTj%d" % i_, [70, 512], BF16) for i_ in range(3)]
        for i_ in range(3):
            P.op("pool", "memset", QTj[i_][64:70, :], 0.0, writes=[K("QTj6", i_)])
    else:
        QTg = P.sb(pfx + "QTg", [Kd, 16, 512], BF16)
    NPT = 4
    PT = [P.sb(pfx + "pt%d" % i, [128, 512], BF16) for i in range(NPT)]
    ao = P.sb(pfx + "ao", [128, 4, 1024], BF16)
    aT = P.sb(pfx + "aT", [128, 8, 128], BF16)
    if L0:
        xt = [qrow[:, 0:1024], qrow[:, 1024:2048]]
    else:
        xt = [P.sb(pfx + "x%d" % i, [128, 1024], F32) for i in range(2)]
    rc = P.sb(pfx + "rc", [128, 4], F32)
    if L0:
        ocp = P.sb(pfx + "ocp", [128, 2, 258], F32)
    if not L0:
        negm = P.sb(pfx + "negm", [128, 16, 512], BF16)


    def prepass_steps(i):
        steps = []
        nch = i + 1
        nkg = 4 * nch
        q0 = i * 512
        mT = maskT[i % 2]
        mb = i % 2

        def s_load(qt):
            t = 4 * i + qt
            P.dma("pool", iqb[t % 2][:], io["iqT"][:, :, q0 + qt * 128:q0 + (qt + 1) * 128], writes=[K("iqb", t % 2)])

        def s_score(qt, kg, h):
            t = 4 * i + qt
            sk = K("score", kg)
            px, kpx = next_x()
            P.op("pe", "matmul", px[:, :], lhsT=iqb[t % 2][:, h, :], rhs=ikT[:, kg * 512:(kg + 1) * 512], start=True, stop=True,
                 reads=[K("iqb", t % 2), K("ikT", kg // 4)], writes=[kpx])
            r = rbuf[(kg * 8 + h) % 2]
            kr = K("r", (kg * 8 + h) % 2)
            P.op("act", "activation", r[:], px[:, :], AF.Relu, reads=[kr], writes=[kpx, kr])
            sc = score[:, kg * 512:(kg + 1) * 512]
            if h == 0:
                P.op("dve", "tensor_scalar", sc, r[:], iw[:, t, 0:1], None, op0=ALU.mult, reads=[kr, K("iw")], writes=[sk] + ([K("score")] if first_score[0] else []))
                first_score[0] = False
            else:
                P.op("dve", "scalar_tensor_tensor", out=sc, in0=r[:], scalar=iw[:, t, h:h + 1], in1=sc, op0=ALU.mult, op1=ALU.add, reads=[kr, K("iw"), sk], writes=[sk])

        def s_pen(qt, kg):
            t = 4 * i + qt
            sk = K("score", kg)
            a = kg - 4 * i
            r = rbuf[0]
            P.op("dve", "tensor_scalar", r[:], relk[:, a * 512:(a + 1) * 512], qcrel[:, t:t + 1], NEG, op0=ALU.is_gt, op1=ALU.mult,
                 reads=[K("relk"), K("qcrel")], writes=[K("r", 0)])
            P.op("dve", "tensor_tensor", out=score[:, kg * 512:(kg + 1) * 512], in0=score[:, kg * 512:(kg + 1) * 512], in1=r[:], op=ALU.add,
                 reads=[K("r", 0), sk], writes=[sk])

        skeys = [K("score", kg) for kg in range(nkg)]

        def s_binit():
            P.op("dve", "memset", bs[:, 1:2], BLO + BW / 2, writes=[K("bs", 1)])

        n_act = nch // 2
        n_d = nch - n_act
        BIG = 1.0e5

        def s_bis(it):
            w = BW / (2.0 ** (it + 1))
            if n_act:
                P.op("dve", "tensor_scalar", bs[:, 5:6], bs[:, 1:2], BIG, None, op0=ALU.mult, reads=[K("bs", 1)], writes=[K("bs", 5)])
            for c in range(n_d):
                P.op("dve", "tensor_scalar", mkb[0][:], score[:, c * 2048:(c + 1) * 2048], bs[:, 1:2], (bs[:, 2:3] if c > 0 else None),
                     op0=ALU.is_ge, op1=ALU.add, accum_out=bs[:, 2:3],
                     reads=skeys[4 * c:4 * c + 4] + [K("bs", 1)] + ([K("bs", 2)] if c > 0 else []), writes=[K("mk", 0), K("bs", 2)])
            for a_ in range(n_act):
                c = n_d + a_
                P.op("act", "activation", junkA[:], score[:, c * 2048:(c + 1) * 2048], AF.Tanh, bias=bs[:, 5:6], scale=-BIG, accum_out=bs[:, 6 + a_:7 + a_],
                     reads=skeys[4 * c:4 * c + 4] + [K("bs", 5)], writes=[K("junkA"), K("bs", 6 + a_)], saturate=False)
            thr = float(TOPK) - 0.5
            if n_act:
                P.op("dve", "scalar_tensor_tensor", out=bs[:, 2:3], in0=bs[:, 2:3], scalar=2.0, in1=bs[:, 6:7], op0=ALU.mult, op1=ALU.subtract,
                     reads=[K("bs", 2), K("bs", 6)], writes=[K("bs", 2)])
                if n_act == 2:
                    P.op("dve", "tensor_tensor", out=bs[:, 2:3], in0=bs[:, 2:3], in1=bs[:, 7:8], op=ALU.subtract, reads=[K("bs", 2), K("bs", 7)], writes=[K("bs", 2)])
                thr = 2.0 * thr - n_act * 2048.0
            P.op("dve", "tensor_scalar", bs[:, 3:4], bs[:, 2:3], thr, w, op0=ALU.is_ge, op1=ALU.mult, reads=[K("bs", 2)], writes=[K("bs", 3)])
            P.op("dve", "scalar_tensor_tensor", out=bs[:, 1:2], in0=bs[:, 3:4], scalar=-w / 2.0, in1=bs[:, 1:2], op0=ALU.add, op1=ALU.add,
                 reads=[K("bs", 3), K("bs", 1)], writes=[K("bs", 1)])

        def s_bfin():
            P.op("dve", "tensor_scalar", bs[:, 0:1], bs[:, 1:2], -BW / (2.0 ** (NITER + 1)), None, op0=ALU.add, reads=[K("bs", 1)], writes=[K("bs", 0)])

        def s_mask(qt, c):
            mk = mkb[c % 2]
            kmk = K("mk", c % 2)
            P.op("dve", "tensor_scalar", mk[:], score[:, c * 2048:(c + 1) * 2048], bs[:, 0:1], -240.0, op0=ALU.is_lt, op1=ALU.mult,
                 reads=skeys[4 * c:4 * c + 4] + [K("bs", 0)], writes=[kmk])

        def s_tr(qt, c, hf):
            mk = mkb[c % 2]
            kmk = K("mk", c % 2)
            px, kpx = next_x()
            pxb = px[:, :].bitcast(BF16).rearrange("p (a b) -> p a b", b=128)
            for j in range(8):
                kt = hf * 8 + j
                P.op("pe", "transpose", pxb[:, j, :], mk[:, kt * 128:(kt + 1) * 128], ident[:], reads=[kmk, "c_ident"], writes=[kpx])
            dst = mT[:, c * 16 + hf * 8:c * 16 + hf * 8 + 8, qt * 128:(qt + 1) * 128]
            if hf == 0:
                P.op("act", "copy", dst, pxb[:, 0:8, :], writes=[kpx, K("maskT", (mb, c * 2 + hf, qt))], **SAT)
            else:
                P.op("act", "copy", dst, pxb[:, 0:8, :], writes=[kpx, K("maskT", (mb, c * 2 + hf, qt))], **SAT)

        for qt in range(4):
            steps.append(lambda qt=qt: s_load(qt))
            for kg in range(nkg):
                for h in range(8):
                    steps.append(lambda qt=qt, kg=kg, h=h: s_score(qt, kg, h))
                if kg >= 4 * i:
                    steps.append(lambda qt=qt, kg=kg: s_pen(qt, kg))
            steps.append(s_binit)
            for it in range(NITER):
                steps.append(lambda it=it: s_bis(it))
            steps.append(s_bfin)
            for c in range(nch):
                steps.append(lambda qt=qt, c=c: s_mask(qt, c))
                for hf in range(2):
                    steps.append(lambda qt=qt, c=c, hf=hf: s_tr(qt, c, hf))
        return steps

    slot_base = [0]
    order = ([1, 2, 3, 0] if (L0 and NG == 4) else list(range(NG)))
    first_score = [True]
    for oi, i in enumerate(order):
        nch = i + 1
        q0 = i * 512
        if not L0:
            P.dma("sp", QTg[0:64, :, :], io["qT"][:, :, q0:q0 + 512], writes=[K("QTg")])
            P.dma("sp", QTg[64:70, :, :], io["caug_q"][:, :, q0:q0 + 512], writes=[K("QTg6")])
        qkeys = [] if L0 else [K("QTg"), K("QTg6")]
        if not L0:
            for kt in range(16):
                ktg = i * 16 + kt
                P.op("dve", "tensor_scalar", negm[:, kt, :], qrow[:, q0:q0 + 512], kcol[:, ktg:ktg + 1], -30000.0, op0=ALU.is_lt, op1=ALU.mult,
                     reads=[K("qrow"), K("kcol")], writes=[K("negm", kt)])

        if L0 and oi == 0:
            for st_ in prepass_steps(i):
                st_()
        nxt_steps = prepass_steps(order[oi + 1]) if (L0 and oi + 1 < NG) else []

        if L0:
            jobs = [dict(mode="dsa", vd=64, qh=h, kh=h, vname="va_g", vh=h) for h in range(8)]
            jobs += [dict(mode="chunk", vd=128, qh=8 + j, kh=8 + j, vname="vb_g", vh=j // 2, pair=j) for j in range(8)]
        else:
            jobs = [dict(mode="causal", vd=64, qh=h, kh=h, vname="vc_g", vh=h) for h in range(16)]
        steps = [(ji, c) for ji in range(len(jobs)) for c in range(nch)]
        units = [(ji, c, kt) for (ji, c) in steps for kt in range(16)]

        def slot_of(si):
            return (slot_base[0] + si) % NRING

        def emit_load2(si):
            ji, c = steps[si]
            jb = jobs[ji]
            sl = slot_of(si)
            if L0 and c == 0:
                P.dma("sp", QTj[ji % 3][0:64, :], io["qT"][:, jb["qh"], q0:q0 + 512], writes=[K("QTj", ji % 3)])
            P.dma("sp", Kc[sl][0:64, :].rearrange("d (r q) -> d r q", r=4), io["kT_g"][c][:, :, jb["kh"], :], reads=GK.get(("kT", c), []), writes=[K("Kc", sl)])
            if not L0:
                P.dma("sp", Kc[sl][64:70, :], io["caug_k"][jb["kh"], :, c * 2048:(c + 1) * 2048], writes=[K("Kc6", sl)])
            vd = jb["vd"]
            for r in range(4):
                P.dma("sp", Vc[sl][:, 4 * r:4 * r + 4, 1:1 + vd], io[jb["vname"]][c][jb["vh"]][:, r, :, :], reads=GK.get((jb["vname"], c), []), writes=[K("Vc", sl)])

        def ksl(si):
            sl = slot_of(si)
            return [K("Kc", sl), K("Kc6", sl)]

        def emit_qk(u):
            ji, c, kt = units[u]
            jb = jobs[ji]
            si = ji * nch + c
            sl = slot_of(si)
            ps = ps_s[u % n_s]
            kps = K("pss", u % n_s)
            addm = (jb["mode"] == "causal" and c == i) or jb["mode"] == "dsa"
            P.op("pe", "matmul", ps[:, :], lhsT=Kc[sl][0:Kd, kt * 128:(kt + 1) * 128], rhs=(QTj[ji % 3][:, :] if L0 else QTg[0:Kd, jb["qh"], :]), start=True, stop=(not addm),
                 reads=ksl(si) + qkeys + ([K("QTj", ji % 3), K("QTj6", ji % 3)] if L0 else []), writes=[kps])
            if jb["mode"] == "dsa":
                ktg_ = c * 16 + kt
                P.op("pe", "matmul", ps[:, :], lhsT=ident2[:, :], rhs=maskT[i % 2][:, ktg_, :], start=False, stop=True,
                     reads=[K("ident2")] + [K("maskT", (i % 2, ktg_ // 8, qt)) for qt in range(4)], writes=[kps])
            elif addm:
                P.op("pe", "matmul", ps[:, :], lhsT=ident[:, :], rhs=negm[:, kt, :], start=False, stop=True,
                     reads=["c_ident", K("negm", kt)], writes=[kps])
            pt = PT[u % NPT]
            kpt = K("pt", u % NPT)
            P.op("act", "activation", pt[:], ps[:, :], AF.Exp, scale=0.125, reads=[kps], writes=[kpt])
            ktg = c * 16 + kt
            if c == i and jb["mode"] == "chunk":
                P.op("dve", "scalar_tensor_tensor", out=pt[:], in0=qrow_b[:, q0:q0 + 512], scalar=kcol[:, ktg:ktg + 1], in1=pt[:], op0=ALU.is_ge, op1=ALU.mult,
                     reads=[kpt, K("qrow_b"), K("kcol")], writes=[kpt])

        def emit_pv(u):
            ji, c, kt = units[u]
            jb = jobs[ji]
            si = ji * nch + c
            sl = slot_of(si)
            vd = jb["vd"]
            os_ = ji % n_os
            pt = PT[u % NPT]
            kpt = K("pt", u % NPT)
            first = (c == 0 and kt == 0)
            last = (c == nch - 1 and kt == 15)
            for s in range(4):
                ob = oview_psum(os_, s // 2, vd)
                P.op("pe", "matmul", ob[:, s % 2, 0:vd + 1], lhsT=pt[:, s * 128:(s + 1) * 128], rhs=Vc[sl][:, kt, 0:vd + 1],
                     start=(first and s % 2 == 0), stop=last, skip_group_check=True,
                     reads=[kpt, K("Vc", sl), K("Vone", sl)], writes=[K("pso", (os_, s // 2))])
            if last:
                emit_norm(ji)

        def emit_norm(ji):
            jb = jobs[ji]
            vd = jb["vd"]
            os_ = ji % n_os
            okeys = [K("pso", (os_, 0)), K("pso", (os_, 1))]
            if L0:
                P.op("act", "copy", ocp[:, 0, 0:2 * (vd + 1)], ps_o_raw[os_][0][:, 0:2 * (vd + 1)], writes=[okeys[0], K("ocp", 0)])
                P.op("dve", "tensor_copy", ocp[:, 1, 0:2 * (vd + 1)], ps_o_raw[os_][1][:, 0:2 * (vd + 1)], writes=[okeys[1], K("ocp", 1)])
                okeys = [K("ocp", 0), K("ocp", 1)]

                def oview(os__, b_, vd_):
                    return ocp[:, b_, 0:2 * (vd_ + 1)].rearrange("p (a b) -> p a b", b=vd_ + 1)
            else:
                oview = oview_psum
            for s in range(4):
                ob = oview(os_, s // 2, vd)
                P.op("dve", "reciprocal", rc[:, s:s + 1], ob[:, s % 2, 0:1], writes=[okeys[s // 2], K("rc", s)])
            if jb["mode"] != "chunk":
                h = jb["qh"]
                for s in range(4):
                    ob = oview(os_, s // 2, vd)
                    if s // 2 == 0:
                        P.op("act", "activation", ao[:, s, h * 64:(h + 1) * 64], ob[:, s % 2, 1:1 + vd], AF.Copy, scale=rc[:, s:s + 1],
                             reads=[K("rc", s)], writes=[okeys[s // 2], K("ao", (s, h))])
                    else:
                        P.op("dve", "tensor_scalar", ao[:, s, h * 64:(h + 1) * 64], ob[:, s % 2, 1:1 + vd], rc[:, s:s + 1], None, op0=ALU.mult,
                             reads=[K("rc", s)], writes=[okeys[s // 2], K("ao", (s, h))])
            else:
                j = jb["pair"]
                hb, m = j // 2, j % 2
                dst = o1 if m == 0 else o2
                for s in range(4):
                    ob = oview(os_, s // 2, vd)
                    P.op("dve", "tensor_scalar", dst[:, s, :], ob[:, s % 2, 1:1 + vd], rc[:, s:s + 1], None, op0=ALU.mult,
                         reads=[K("rc", s)], writes=[okeys[s // 2], K("o12", (m, s))])
                if m == 1:
                    o12 = [K("o12", (mm_, s)) for mm_ in range(2) for s in range(4)]
                    P.op("dve", "scalar_tensor_tensor", out=o1[:], in0=o2[:], scalar=lm[:, 5:6], in1=o1[:], op0=ALU.mult, op1=ALU.add,
                         reads=o12 + [K("neglam")], writes=o12)
                    P.op("dve", "tensor_tensor", out=o2[:], in0=o1[:], in1=o1[:], op=ALU.mult, reads=o12, writes=o12)
                    P.op("dve", "tensor_reduce", out=sst[:, 0, :], in_=o2[:], axis=AX.X, op=ALU.add, reads=o12, writes=[K("sst", 0)])
                    P.op("act", "activation", sst[:, 1, :], sst[:, 0, :], AF.Sqrt, bias=eps[:], scale=1.0 / 128, reads=[K("sst", 0), "c_eps"], writes=[K("sst", 1)])
                    P.op("dve", "reciprocal", sst[:, 2, :], sst[:, 1, :], reads=[K("sst", 1)], writes=[K("sst", 2)])
                    P.op("dve", "tensor_tensor", out=o1[:], in0=o1[:], in1=sst[:, 2, :].unsqueeze(2).to_broadcast([128, 4, 128]), op=ALU.mult,
                         reads=o12 + [K("sst", 2)], writes=o12)
                    P.op("dve", "tensor_tensor", out=ao[:, :, 512 + hb * 128:512 + (hb + 1) * 128], in0=o1[:], in1=gsub[:].unsqueeze(1).to_broadcast([128, 4, 128]), op=ALU.mult,
                         reads=o12 + [K("gsub")], writes=[K("ao", (s, 8 + 2 * hb + e)) for s in range(4) for e in range(2)])

        nst = len(steps)
        nxt_done = [0]
        emit_load2(0)
        if nst > 1:
            emit_load2(1)
        nu = len(units)
        for idx in range(nu + LA):
            if idx < nu:
                emit_qk(idx)
            if idx - LA >= 0:
                emit_pv(idx - LA)
            if idx < nu:
                ji, c, kt = units[idx]
                si = ji * nch + c
                if kt == LA and si + 2 < nst:
                    emit_load2(si + 2)
                want = ((idx + 1) * len(nxt_steps)) // nu
                while nxt_done[0] < want:
                    nxt_steps[nxt_done[0]]()
                    nxt_done[0] += 1
        slot_base[0] = (slot_base[0] + nst) % NRING
        while nxt_done[0] < len(nxt_steps):
            nxt_steps[nxt_done[0]]()
            nxt_done[0] += 1

        aokeys = lambda s: [K("ao", (s, h)) for h in range(16)]
        for s in range(4):
            t = 4 * i + s
            px, kpx = next_x()
            pxb = px[:, :].bitcast(BF16).rearrange("p (a b) -> p a b", b=128)
            for kc in range(8):
                P.op("pe", "transpose", pxb[:, kc, :], ao[:, s, kc * 128:(kc + 1) * 128], ident[:], reads=aokeys(s) + ["c_ident"], writes=[kpx])
            P.op("act", "copy", aT[:], pxb[:, 0:8, :], reads=[kpx], writes=[K("aT")])
            x = xt[t % 2]
            kx = K("x", t % 2)
            P.dma("sp", x[:], io["x"][t * 128:(t + 1) * 128, :], writes=[kx] + ([K("qrow")] if L0 else []))
            for n in range(2):
                px, kpx = next_x()
                for kc in range(8):
                    P.op("pe", "matmul", px[:, :], lhsT=aT[:, kc, :], rhs=wout[:, kc, n * 512:(n + 1) * 512], start=(kc == 0), stop=(kc == 7),
                         reads=[K("aT")] + wkeys(pfx + "wo", n * 512, (n + 1) * 512), writes=[kpx])
                P.op("dve", "tensor_tensor", out=x[:, n * 512:(n + 1) * 512], in0=x[:, n * 512:(n + 1) * 512], in1=px[:, :], op=ALU.add, reads=[kx, kpx], writes=[kx])
            P.dma("sp", io["out"][t * 128:(t + 1) * 128, :], x[:], reads=[kx], writes=[("dram_out", t)])
            if s == 3:
                P.dma("sp", io["tails"][2 * i:2 * i + 2, :], x[126:128, :], reads=[kx], writes=[("dram_tails", i)])


DFF = 2816
NFC = 44
NGC = 22


def emit_ffn(P, C, io, NG=4, pfx="f_"):
    ident, eps = C["ident"], C["eps"]

    def K(n, i=None):
        return (pfx + n) if i is None else (pfx + n, i)

    ps_u = [P.ps(pfx + "psu%d" % i, [128, 512], F32) for i in range(3)]
    ps_h = [P.ps(pfx + "psh%d" % i, [128, 512], F32) for i in range(2)]
    ps_d = [P.ps(pfx + "psd%d" % i, [128, 512], F32) for i in range(2)]
    ps_t = P.ps(pfx + "pst", [128, 8, 128], BF16)

    xg = P.sb(pfx + "xg", [128, 4, 1024], F32)
    wup = load_weight_bf16(P, io["w_up"], 2 * DFF, pfx + "wup")
    wdn = P.sb(pfx + "wdn", [128, NGC, 1024], BF16)
    wdv = io["w_down"].rearrange("(c p) n -> p c n", p=128)
    for c0 in range(0, NGC, 4):
        c1 = min(c0 + 4, NGC)
        P.dma("pool", wdn[:, c0:c1, :], wdv[:, c0:c1, :], writes=[K("wdn", c0)])
    wdn_keys = [K("wdn", c0) for c0 in range(0, NGC, 4)]
    lnb = P.sb(pfx + "lnb", [128, 1024], F32)
    P.dma("sp", lnb[:], io["ln"].partition_broadcast(128), writes=[K("lnb")])
    cw = P.sb(pfx + "cw", [128, 4, NFC], F32)
    P.dma("sp", cw[:], io["cw"], writes=[K("cw")])

    xh = P.sb(pfx + "xh", [2, 1024], F32)
    e1h = P.sb(pfx + "e1h", [2, 4], F32)
    P.dma("sp", e1h[:], io["e1h"][0:2, :], writes=[K("e1h")])
    st = P.sb(pfx + "st", [128, 4], F32)
    hb = [P.sb(pfx + "h%d" % i, [128, 1024], BF16) for i in range(2)]
    hT = P.sb(pfx + "hT", [128, 8, 514], BF16)
    U = [P.sb(pfx + "U%d" % i, [128, 514], F32) for i in range(2)]
    cv = [P.sb(pfx + "cv%d" % i, [128, 512], F32) for i in range(3)]
    sg = [P.sb(pfx + "sg%d" % i, [128, 512], F32) for i in range(2)]
    actT = P.sb(pfx + "actT", [128, NGC, 512], BF16)
    cand = actT[0:2, 0:4, :].rearrange("p a b -> p (a b)").bitcast(F32)

    for g in range(NG):
        ck = [K("cand")] + [K("actT", fc) for fc in range(4)]
        for jp in range(4):
            if jp == 0 and g == 0:
                P.op("pool", "memset", cand, 0.0, writes=ck)
            else:
                r_src = (jp - 1) % 4
                g_src = g - (1 if jp == 0 else 0)
                P.dma("sp", cand, io["tails_g"][r_src, 2 * g_src:2 * g_src + 2, :], reads=io.get("gk_tails", []), writes=ck)
            if jp == 0:
                P.op("dve", "tensor_scalar", xh[:], cand, e1h[:, 0:1], None, op0=ALU.mult, reads=ck + [K("e1h")], writes=[K("xh")])
            else:
                P.op("dve", "scalar_tensor_tensor", out=xh[:], in0=cand, scalar=e1h[:, jp:jp + 1], in1=xh[:], op0=ALU.mult, op1=ALU.add,
                     reads=ck + [K("e1h"), K("xh")], writes=[K("xh")])
        P.dma("sp", xg[:], io["x"][g * 512:(g + 1) * 512, :].rearrange("(t p) d -> p t d", p=128), writes=[K("xg")])
        for tl in range(-1, 4):
            npart = 2 if tl < 0 else 128
            src = xh[:, :] if tl < 0 else xg[:, tl, :]
            ksrc = K("xh") if tl < 0 else K("xg")
            b = (tl + 1) % 2
            P.op("act", "activation", hb[b][0:npart, :], src, AF.Square, accum_out=st[0:npart, 0:1], reads=[ksrc], writes=[K("h", b), K("st", 0)])
            P.op("act", "activation", st[0:npart, 1:2], st[0:npart, 0:1], AF.Sqrt, bias=eps[0:npart, :], scale=1.0 / 1024, reads=[K("st", 0), "c_eps"], writes=[K("st", 1)])
            P.op("dve", "reciprocal", st[0:npart, 2:3], st[0:npart, 1:2], reads=[K("st", 1)], writes=[K("st", 2)])
            h = hb[b]
            P.op("dve", "scalar_tensor_tensor", out=h[0:npart, :], in0=src, scalar=st[0:npart, 2:3], in1=lnb[0:npart, :], op0=ALU.mult, op1=ALU.mult,
                 reads=[ksrc, K("st", 2), K("lnb")], writes=[K("h", b)])
            if tl < 0:
                for kc in range(8):
                    P.op("pe", "transpose", ps_t[:, kc, 0:2], h[0:2, kc * 128:(kc + 1) * 128], ident[0:2, 0:2], reads=[K("h", b), "c_ident"], writes=[K("pst")])
                P.op("act", "copy", hT[:, :, 0:2], ps_t[:, :, 0:2], writes=[K("pst"), K("hT", -1)])
            else:
                for kc in range(8):
                    P.op("pe", "transpose", ps_t[:, kc, :], h[:, kc * 128:(kc + 1) * 128], ident[:], reads=[K("h", b), "c_ident"], writes=[K("pst")])
                P.op("act", "copy", hT[:, :, 2 + tl * 128:2 + (tl + 1) * 128], ps_t[:, :, :], writes=[K("pst"), K("hT", tl)])
        hTkeys = [K("hT", tl) for tl in range(-1, 4)]

        def up_chunk(fc, n):
            pu = ps_u[n % 3]
            kpu = K("psu", n % 3)
            ph = ps_h[n % 2]
            kph = K("psh", n % 2)
            for kc in range(8):
                P.op("pe", "matmul", pu[:, :], lhsT=wup[:, kc, fc * 128:(fc + 1) * 128], rhs=hT[:, kc, 2:514], start=(kc == 0), stop=(kc == 7),
                     reads=hTkeys + wkeys(pfx + "wup", fc * 128, (fc + 1) * 128), writes=[kpu])
            for kc in range(8):
                P.op("pe", "matmul", ph[:, 0:2], lhsT=wup[:, kc, fc * 128:(fc + 1) * 128], rhs=hT[:, kc, 0:2], start=(kc == 0), stop=(kc == 7),
                     reads=hTkeys + wkeys(pfx + "wup", fc * 128, (fc + 1) * 128), writes=[kph])
            u = U[n % 2]
            ku = K("U", n % 2)
            c = cv[n % 3]
            kc_ = K("cv", n % 3)
            P.op("act", "copy", u[:, 2:514], pu[:, :], writes=[kpu, (ku, 1)])
            P.op("act", "activation", c[:], pu[:, :], AF.Identity, bias=cw[:, 3, fc:fc + 1], scale=cw[:, 2, fc:fc + 1], reads=[K("cw")], writes=[kpu, kc_])
            P.op("act", "copy", u[:, 0:2], ph[:, 0:2], writes=[kph, (ku, 0)])
            P.op("dve", "scalar_tensor_tensor", out=c[:], in0=u[:, 1:513], scalar=cw[:, 1, fc:fc + 1], in1=c[:], op0=ALU.mult, op1=ALU.add, reads=[(ku, 0), (ku, 1), K("cw"), kc_], writes=[kc_])
            P.op("dve", "scalar_tensor_tensor", out=c[:], in0=u[:, 0:512], scalar=cw[:, 0, fc:fc + 1], in1=c[:], op0=ALU.mult, op1=ALU.add, reads=[(ku, 0), (ku, 1), K("cw"), kc_], writes=[kc_])
            return c, kc_

        n = 0
        for fc in range(NGC):
            cg, kcg = up_chunk(fc, n)
            n += 1
            cvl, kcv = up_chunk(fc + NGC, n)
            n += 1
            s = sg[fc % 2]
            ks = K("sg", fc % 2)
            P.op("act", "activation", s[:], cg[:], AF.Silu, reads=[kcg], writes=[ks])
            P.op("dve", "tensor_tensor", out=actT[:, fc, :], in0=s[:], in1=cvl[:], op=ALU.mult, reads=[ks, kcv], writes=[K("actT", fc)])
        akeys = [K("actT", fc) for fc in range(NGC)]

        for tl in range(4):
            t = g * 4 + tl
            for nh in range(2):
                pd = ps_d[(tl * 2 + nh) % 2]
                kpd = K("psd", (tl * 2 + nh) % 2)
                for fc in range(NGC):
                    P.op("pe", "matmul", pd[:, :], lhsT=actT[:, fc, tl * 128:(tl + 1) * 128], rhs=wdn[:, fc, nh * 512:(nh + 1) * 512], start=(fc == 0), stop=(fc == NGC - 1),
                         reads=akeys + wdn_keys, writes=[kpd])
                P.op("dve", "tensor_tensor", out=xg[:, tl, nh * 512:(nh + 1) * 512], in0=xg[:, tl, nh * 512:(nh + 1) * 512], in1=pd[:, :], op=ALU.add,
                     reads=[K("xg")], writes=[kpd, K("xo", (tl, nh))])
            P.dma("act", io["out"][t * 128:(t + 1) * 128, :], xg[:, tl, :], reads=[K("xo", (tl, 0)), K("xo", (tl, 1)), K("xg")], writes=[("dram_out", t)])


def emit_cumsum(P, io):
    pfx = "cs_"

    def K(n, i=None):
        return (pfx + n) if i is None else (pfx + n, i)
    lfb = P.sb(pfx + "lfb", [16, 8192], F32)
    for j in range(4):
        P.dma("sp", lfb[:, :].rearrange("h (i j q) -> h j i q", i=4, j=4, q=512)[:, j, :, :],
              io["logf_g"][:, j, :].rearrange("h (i q) -> h i q", q=512), reads=io.get("gk_lf", []), writes=[K("lfb")])
    lfo = P.sb(pfx + "lfo", [16, 2048], F32)
    P.dma("sp", lfo[:], io["logf_own"], writes=[K("lfo")])
    e1h = P.sb(pfx + "e1h", [16, 4], F32)
    P.dma("sp", e1h[:], io["e1h"][0:16, :], writes=[K("e1h")])
    ones = P.sb(pfx + "ones", [16, 512], F32)
    P.op("dve", "memset", ones[:], 1.0, writes=[K("ones")])
    c = P.sb(pfx + "c", [16, 8192], F32)
    cq = P.sb(pfx + "cq", [16, 2048], F32)
    Pc = P.sb(pfx + "Pc", [16, 17], F32)
    P.op("dve", "memset", Pc[:, 0:1], 0.0, writes=[K("Pc", 0)])
    for g in range(16):
        P.op("dve", "tensor_tensor_scan", c[:, g * 512:(g + 1) * 512], ones[:], lfb[:, g * 512:(g + 1) * 512], Pc[:, g:g + 1], op0=ALU.mult, op1=ALU.add,
             reads=[K("ones"), K("lfb"), K("Pc", g)], writes=[K("c", g)])
        P.op("dve", "tensor_copy", Pc[:, g + 1:g + 2], c[:, (g + 1) * 512 - 1:(g + 1) * 512], reads=[K("c", g)], writes=[K("Pc", g + 1)])
    ini = P.sb(pfx + "ini", [16, 4], F32)
    for i in range(4):
        P.op("dve", "tensor_scalar", ini[:, i:i + 1], Pc[:, 4 * i:4 * i + 1], e1h[:, 0:1], None, op0=ALU.mult, reads=[K("Pc", 4 * i), K("e1h")], writes=[K("ini", i)])
        for j in range(1, 4):
            P.op("dve", "scalar_tensor_tensor", out=ini[:, i:i + 1], in0=Pc[:, 4 * i + j:4 * i + j + 1], scalar=e1h[:, j:j + 1], in1=ini[:, i:i + 1], op0=ALU.mult, op1=ALU.add,
                 reads=[K("Pc", 4 * i + j), K("e1h"), K("ini", i)], writes=[K("ini", i)])
        P.op("dve", "tensor_tensor_scan", cq[:, i * 512:(i + 1) * 512], ones[:], lfo[:, i * 512:(i + 1) * 512], ini[:, i:i + 1], op0=ALU.mult, op1=ALU.add,
             reads=[K("ones"), K("lfo"), K("ini", i)], writes=[K("cq", i)])
    c8 = P.sb(pfx + "c8", [16, 2048], F32)
    r = P.sb(pfx + "r", [16, 2048], F32)
    o = [P.sb(pfx + "o%d" % i, [16, 6, 2048], BF16) for i in range(2)]
    for n in range(5):
        b = n % 2
        src = cq[:, :] if n == 4 else c[:, n * 2048:(n + 1) * 2048]
        ksrc = [K("cq", i) for i in range(4)] if n == 4 else [K("c", g) for g in range(4 * n, 4 * n + 4)]
        ot = o[b]
        ko = K("o", b)
        P.op("dve", "tensor_scalar", c8[:], src, 8.0, None, op0=ALU.mult, reads=ksrc, writes=[K("c8")])
        P.op("dve", "tensor_copy", ot[:, 0, :], c8[:], reads=[K("c8")], writes=[(ko, 0)])
        P.op("dve", "tensor_tensor", out=r[:], in0=c8[:], in1=ot[:, 0, :], op=ALU.subtract, reads=[K("c8"), (ko, 0)], writes=[K("r")])
        P.op("dve", "tensor_copy", ot[:, 1, :], r[:], reads=[K("r")], writes=[(ko, 1)])
        P.op("dve", "tensor_tensor", out=r[:], in0=r[:], in1=ot[:, 1, :], op=ALU.subtract, reads=[K("r"), (ko, 1)], writes=[K("r")])
        P.op("dve", "tensor_copy", ot[:, 2, :], r[:], reads=[K("r")], writes=[(ko, 2)])
        hk = [(ko, 0), (ko, 1), (ko, 2)]
        if n == 4:
            P.op("dve", "memset", ot[:, 3:6, :], 1.0, writes=[(ko, 3)])
            P.dma("sp", io["caug_q"].rearrange("r h q -> h r q"), ot[:], reads=hk + [(ko, 3)], writes=["dram_cq"] + hk + [(ko, 3)])
        else:
            P.op("dve", "tensor_scalar", ot[:, 3:6, :], ot[:, 0:3, :], -1.0, None, op0=ALU.mult, reads=hk, writes=[(ko, 3)])
            P.op("dve", "memset", ot[:, 0:3, :], 1.0, reads=[(ko, 3)], writes=hk)
            P.dma("sp", io["caug_k"][:, :, n * 2048:(n + 1) * 2048], ot[:], reads=hk + [(ko, 3)], writes=[("dram_ck", n)] + hk + [(ko, 3)])


S_ = 8192
NCORES = 8
GROUPS = [[0, 1, 2, 3], [4, 5, 6, 7]]


def _toks(j, NG=4):
    return np.concatenate([np.arange(512) + 512 * (4 * i + j) for i in range(NG)])


def build_fused():
    nc = bass.Bass("TRN2", target_bir_lowering=False)
    NTOK = 2048

    def din(name, shape, dt=F32):
        return nc.dram_tensor(name, list(shape), dt, kind="ExternalInput").ap()

    def scr(name, shape, dt):
        return nc.dram_tensor(name, list(shape), dt)

    E = {}
    for name, shape in [("x", [NTOK, 1024]), ("pos", [128, 16]), ("qchk_row", [NTOK]), ("qpos_row", [NTOK]), ("qchk_col", [128, 16]), ("e1h", [16, 4]),
                        ("ln_mix", [2, 1024]), ("ln_ffn", [2, 1024]), ("ev_w_in", [1024, 3368]), ("ev_w_out", [1024, 1024]),
                        ("ev_a_qnorm", [64]), ("ev_a_knorm", [64]), ("ev_idx_knorm", [32]), ("ev_b_qnorm", [64]), ("ev_b_knorm", [64]),
                        ("ev_lam_q1", [64]), ("ev_lam_k1", [64]), ("ev_lam_q2", [64]), ("ev_lam_k2", [64]), ("ev_b_subln", [128]),
                        ("od_w_in", [1024, 3088]), ("od_b_f", [16]), ("od_w_out", [1024, 1024]), ("od_c_qnorm", [64]), ("od_c_knorm", [64]),
                        ("ffn_up", [2, 1024, 5632]), ("cw", [2, 128, 4, 44]), ("ffn_down", [2, 2816, 1024])]:
        E[name] = din(name, shape)
    out = nc.dram_tensor("out", [NTOK, 1024], F32, kind="ExternalOutput").ap()

    T = {}
    specs = [("qT0", [1024, NTOK], BF16), ("iqT", [768, NTOK], BF16), ("iw", [NTOK, 8], F32),
             ("x1", [NTOK, 1024], F32), ("tl1", [8, 1024], F32), ("tl1_g", [32, 1024], F32), ("x2", [NTOK, 1024], F32),
             ("qT1", [1024, NTOK], BF16), ("lf", [16, NTOK], F32), ("lf_g", [64, NTOK], F32),
             ("cak", [96, S_], BF16), ("caq", [96, NTOK], BF16),
             ("x3", [NTOK, 1024], F32), ("tl3", [8, 1024], F32), ("tl3_g", [32, 1024], F32)]
    for g in range(4):
        specs += [("kT0_%d" % g, [1024, 512], BF16), ("kT0_%d_g" % g, [4096, 512], BF16), ("kT1_%d" % g, [1024, 512], BF16), ("kT1_%d_g" % g, [4096, 512], BF16),
                  ("vc_%d" % g, [2048, 256], BF16), ("vc_%d_g" % g, [8192, 256], BF16),
                  ("va_%d" % g, [1024, 256], BF16), ("va_%d_g" % g, [4096, 256], BF16), ("vb_%d" % g, [512, 512], BF16), ("vb_%d_g" % g, [2048, 512], BF16),
                  ("ikT_%d" % g, [96, 512], BF16), ("ikT_%d_g" % g, [384, 512], BF16)]
    for name, shape, dt in specs:
        T[name] = scr("s_" + name, shape, dt)

    def A(name):
        return T[name].ap()

    with ExitStack() as es:
        P = Prog(nc, es)
        C = build_consts(P)
        with P.phase():
            io = {"x": E["x"], "pos": E["pos"], "ln": E["ln_mix"][0], "w_in": E["ev_w_in"],
                  "gains": [E["ev_a_qnorm"], E["ev_a_knorm"], E["ev_b_qnorm"], E["ev_b_knorm"]], "g_ik": E["ev_idx_knorm"],
                  "qT": A("qT0").rearrange("(d h) q -> d h q", h=16), "kT_p": [A("kT0_%d" % g).rearrange("(h d) q -> d h q", h=16) for g in range(4)],
                  "iqT": A("iqT").rearrange("(d h) q -> d h q", h=8), "ikT_p": [A("ikT_%d" % g) for g in range(4)], "iw": A("iw"),
                  "va_p": [A("va_%d" % g).rearrange("(h p) (t d) -> p h t d", p=128, d=64) for g in range(4)],
                  "vb_p": [A("vb_%d" % g).rearrange("(h p) (t d) -> p h t d", p=128, d=128) for g in range(4)]}

            def on_group0(g, keys):
                for nm in ["ikT_%d" % g, "kT0_%d" % g, "va_%d" % g, "vb_%d" % g]:
                    P.coll(T[nm], T[nm + "_g"], GROUPS, reads=keys, writes=[("g", nm)])
            io["on_group"] = on_group0
            emit_proj(P, C, 0, io, 16)
        gk0 = {}
        for g in range(4):
            gk0[("ikT", g)] = [("g", "ikT_%d" % g)]
            gk0[("kT", g)] = [("g", "kT0_%d" % g)]
            gk0[("va_g", g)] = [("g", "va_%d" % g)]
            gk0[("vb_g", g)] = [("g", "vb_%d" % g)]
        with P.phase():
            io = {"qT": A("qT0").rearrange("(d h) q -> d h q", h=16), "kT_g": [A("kT0_%d_g" % g).rearrange("(r h d) q -> d r h q", r=4, h=16) for g in range(4)],
                  "ikT_g": [A("ikT_%d_g" % g).rearrange("(j d) q -> d j q", j=4) for g in range(4)], "iqT": A("iqT").rearrange("(d h) q -> d h q", h=8),
                  "iw": A("iw").rearrange("(t p) h -> p t h", p=128), "qkey_row": E["qchk_row"], "qchk_col": E["qchk_col"],
                  "va_g": [A("va_%d_g" % g).rearrange("(r h p) (t d) -> h p r t d", r=4, p=128, d=64) for g in range(4)],
                  "vb_g": [A("vb_%d_g" % g).rearrange("(r h p) (t d) -> h p r t d", r=4, p=128, d=128) for g in range(4)],
                  "lq1": E["ev_lam_q1"], "lk1": E["ev_lam_k1"], "lq2": E["ev_lam_q2"], "lk2": E["ev_lam_k2"], "subln": E["ev_b_subln"],
                  "x": E["x"], "w_out": E["ev_w_out"], "out": A("x1"), "tails": A("tl1"), "gk": gk0}
            emit_attn(P, C, 0, io, NG=4)
        P.coll(T["tl1"], T["tl1_g"], GROUPS, writes=[("g", "tl1")])
        with P.phase():
            io = {"x": A("x1"), "tails_g": A("tl1_g").rearrange("(r g) d -> r g d", r=4), "e1h": E["e1h"], "ln": E["ln_ffn"][0], "w_up": E["ffn_up"][0],
                  "cw": E["cw"][0], "w_down": E["ffn_down"][0], "out": A("x2"), "gk_tails": [("g", "tl1")]}
            emit_ffn(P, C, io, NG=4, pfx="f0_")
        with P.phase():
            io = {"x": A("x2"), "pos": E["pos"], "ln": E["ln_mix"][1], "w_in": E["od_w_in"], "gains": [E["od_c_qnorm"], E["od_c_knorm"]], "b_f": E["od_b_f"],
                  "qT": A("qT1").rearrange("(d h) q -> d h q", h=16), "kT_p": [A("kT1_%d" % g).rearrange("(h d) q -> d h q", h=16) for g in range(4)],
                  "vc_p": [A("vc_%d" % g).rearrange("(h p) (t d) -> p h t d", p=128, d=64) for g in range(4)], "logfT": A("lf")}

            def on_group1(g, keys):
                for nm in ["kT1_%d" % g, "vc_%d" % g]:
                    P.coll(T[nm], T[nm + "_g"], GROUPS, reads=keys, writes=[("g", nm)])
            io["on_group"] = on_group1
            io["on_lf"] = lambda keys: P.coll(T["lf"], T["lf_g"], GROUPS, reads=keys, writes=[("g", "lf")])
            emit_proj(P, C, 1, io, 16)
        gk1 = {}
        for g in range(4):
            gk1[("kT", g)] = [("g", "kT1_%d" % g)]
            gk1[("vc_g", g)] = [("g", "vc_%d" % g)]
        with P.phase():
            io = {"logf_g": A("lf_g").rearrange("(j h) q -> h j q", j=4), "logf_own": A("lf"), "e1h": E["e1h"],
                  "caug_k": A("cak").rearrange("(h r) q -> h r q", r=6), "caug_q": A("caq").rearrange("(r h) q -> r h q", h=16), "gk_lf": [("g", "lf")]}
            emit_cumsum(P, io)
        with P.phase():
            io = {"qT": A("qT1").rearrange("(d h) q -> d h q", h=16), "kT_g": [A("kT1_%d_g" % g).rearrange("(r h d) q -> d r h q", r=4, h=16) for g in range(4)],
                  "qkey_row": E["qpos_row"], "vc_g": [A("vc_%d_g" % g).rearrange("(r h p) (t d) -> h p r t d", r=4, p=128, d=64) for g in range(4)],
                  "caug_k": A("cak").rearrange("(h r) q -> h r q", r=6), "caug_q": A("caq").rearrange("(r h) q -> r h q", h=16),
                  "x": A("x2"), "w_out": E["od_w_out"], "out": A("x3"), "tails": A("tl3"), "gk": gk1}
            emit_attn(P, C, 1, io, NG=4)
        P.coll(T["tl3"], T["tl3_g"], GROUPS, writes=[("g", "tl3")])
        with P.phase():
            io = {"x": A("x3"), "tails_g": A("tl3_g").rearrange("(r g) d -> r g d", r=4), "e1h": E["e1h"], "ln": E["ln_ffn"][1], "w_up": E["ffn_up"][1],
                  "cw": E["cw"][1], "w_down": E["ffn_down"][1], "out": out, "gk_tails": [("g", "tl3")]}
            emit_ffn(P, C, io, NG=4, pfx="f1_")
        P.finish()
        P.emit()
    return nc


def kernel(**inp):
    inp = {k: np.asarray(v) for k, v in inp.items()}
    x = np.ascontiguousarray(inp["x"], dtype=np.float32)
    nc = build_fused()
    cw = np.stack([np.concatenate([inp["ffn_conv"][l], inp["ffn_conv_b"][l][None]], 0).reshape(4, 44, 128).transpose(2, 0, 1) for l in range(2)], 0)
    shared = {"ln_mix": inp["ln_mix"], "ln_ffn": inp["ln_ffn"], "ev_w_in": inp["ev_w_in"][0], "ev_w_out": inp["ev_w_out"][0],
              "od_w_in": inp["od_w_in"][0], "od_b_f": inp["od_b_f"][0], "od_w_out": inp["od_w_out"][0],
              "ffn_up": inp["ffn_up"], "cw": np.ascontiguousarray(cw, dtype=np.float32), "ffn_down": inp["ffn_down"]}
    for k in ["ev_a_qnorm", "ev_a_knorm", "ev_idx_knorm", "ev_b_qnorm", "ev_b_knorm", "ev_lam_q1", "ev_lam_k1", "ev_lam_q2", "ev_lam_k2", "ev_b_subln",
              "od_c_qnorm", "od_c_knorm"]:
        shared[k] = inp[k][0]
    shared = {k: np.ascontiguousarray(v, dtype=np.float32) for k, v in shared.items()}
    maps = []
    for c in range(NCORES):
        b, j = c // 4, c % 4
        toks = _toks(j)
        m = dict(shared)
        e1h = np.zeros((16, 4), np.float32)
        e1h[:, j] = 1.0
        m.update(x=np.ascontiguousarray(x[b, toks]), pos=np.ascontiguousarray(toks.astype(np.float32).reshape(16, 128).T),
                 qchk_row=(toks // 64).astype(np.float32), qpos_row=toks.astype(np.float32),
                 qchk_col=np.ascontiguousarray((toks // 64).astype(np.float32).reshape(16, 128).T), e1h=e1h)
        maps.append(m)
    res = run_bass_kernel_spmd(nc, maps, core_ids=list(range(NCORES)))
    out = np.empty_like(x)
    for c in range(NCORES):
        out[c // 4, _toks(c % 4)] = np.asarray(res.results[c]["out"])
    return out.astype(np.float32)
```

```python
import math
from contextlib import ExitStack
import numpy as np
import concourse.bass as bass
import concourse.mybir as mybir
from concourse.bass_utils import run_bass_kernel_spmd


F32 = mybir.dt.float32
BF16 = mybir.dt.bfloat16
I32 = mybir.dt.int32
FP8 = mybir.dt.float8e4
ALU = mybir.AluOpType
AF = mybir.ActivationFunctionType
AX = mybir.AxisListType

ENGS = ["pe", "act", "dve", "pool", "sp"]
N_DMA_SEMS = 32
N_SW_SEMS = 8


class Prog:
    def __init__(self, nc, es, same_engine_sync=True):
        self.nc = nc
        self.es = es
        self.same = same_engine_sync
        self.sems = {}
        for e in ENGS:
            self.sems["e_" + e] = es.enter_context(nc.semaphore("se_" + e))
        self.cnt = {e: 0 for e in ENGS}
        self.known = {e: {} for e in ENGS}
        self.ops = {e: [] for e in ENGS}
        self.reg = {}
        self.dsem = []
        for i in range(N_DMA_SEMS):
            nm = "d_%d" % i
            self.sems[nm] = es.enter_context(nc.semaphore("sd_%d" % i))
            self.dsem.append([nm, 0])
        self.drr = 0
        self.drr_sw = 0
        self.n_inst = 0
        self.cur_es = es
        self.csem = []
        self.n_coll = 0

    def phase(self):
        prog = self

        class _Ph:
            def __enter__(self_):
                self_.st = ExitStack()
                self_.prev = prog.cur_es
                prog.cur_es = self_.st
                return self_

            def __exit__(self_, *a):
                prog.barrier()
                prog.cur_es = self_.prev
                self_.st.close()
                return False
        return _Ph()

    def barrier(self):
        targets = {}
        for e in ENGS:
            if self.cnt[e] > 0:
                targets["e_" + e] = self.cnt[e]
        for nm, v in self.dsem:
            if v > 0:
                targets[nm] = v
        for e in ENGS:
            waits = []
            for s_, v in targets.items():
                if s_ == "e_" + e:
                    continue
                if self.known[e].get(s_, 0) < v:
                    waits.append((s_, v))
                    self.known[e][s_] = v
            if waits:
                self.ops[e].append((waits, None, None))

    def coll(self, src_t, dst_t, groups, reads=(), writes=()):
        waits = self._deps("pool", reads, writes)
        nm = "c_%d" % self.n_coll
        self.n_coll += 1
        self.sems[nm] = self.es.enter_context(self.nc.semaphore("sc_%d" % (self.n_coll - 1)))
        self.csem.append(nm)
        tag = (nm, 1)
        self._mark(reads, writes, tag)
        kw = dict(replica_groups=groups, ins=[src_t.ap().opt()], outs=[dst_t.ap().opt()])
        self.ops["pool"].append((waits, ("collective_compute", ("AllGather", ALU.bypass), kw), (nm, 1)))
        self.n_inst += 1

    def sb(self, name, shape, dt):
        return self.cur_es.enter_context(self.nc.sbuf_tensor(name, list(shape), dt))

    def ps(self, name, shape, dt):
        return self.cur_es.enter_context(self.nc.psum_tensor(name, list(shape), dt))

    def _deps(self, eng, reads, writes):
        deps = {}

        def add(sv):
            if sv is None:
                return
            s, v = sv
            if deps.get(s, 0) < v:
                deps[s] = v

        for k in reads:
            st = self.reg.get(k)
            if st is not None:
                add(st[0])
        for k in writes:
            st = self.reg.get(k)
            if st is not None:
                add(st[0])
                for s, v in st[1].items():
                    add((s, v))
        waits = []
        me = "e_" + eng
        for s, v in deps.items():
            if s == me and (eng == "pe" or not self.same):
                continue
            if self.known[eng].get(s, 0) < v:
                waits.append((s, v))
                self.known[eng][s] = v
        return waits

    def _mark(self, reads, writes, tag):
        for k in writes:
            self.reg[k] = [tag, {}]
        for k in reads:
            st = self.reg.setdefault(k, [None, {}])
            s, v = tag
            if st[1].get(s, 0) < v:
                st[1][s] = v

    def op(self, eng, meth, *args, reads=(), writes=(), **kw):
        fn = (meth, args, kw)
        waits = self._deps(eng, reads, writes)
        self.cnt[eng] += 1
        tag = ("e_" + eng, self.cnt[eng])
        self._mark(reads, writes, tag)
        self.ops[eng].append((waits, fn, ("e_" + eng, 1)))
        self.n_inst += 1

    def dma(self, q, out, in_, reads=(), writes=(), **kw):
        waits = self._deps(q, reads, writes)
        if q == "pool":
            ent = self.dsem[N_DMA_SEMS - N_SW_SEMS + self.drr_sw]
            self.drr_sw = (self.drr_sw + 1) % N_SW_SEMS
        else:
            ent = self.dsem[self.drr]
            self.drr = (self.drr + 1) % (N_DMA_SEMS - N_SW_SEMS)
        nm, prev = ent
        if prev > 0 and self.known[q].get(nm, 0) < prev:
            waits.append((nm, prev))
            self.known[q][nm] = prev
        ent[1] = prev + 16
        tag = (nm, prev + 16)
        self._mark(reads, writes, tag)
        kw = dict(kw); kw["out"] = out; kw["in_"] = in_
        self.ops[q].append((waits, ("dma_start", (), kw), (nm, 16)))
        self.n_inst += 1

    def finish(self):
        waits = []
        for nm, v in self.dsem:
            if v > 0 and self.known["sp"].get(nm, 0) < v:
                waits.append((nm, v))
        self.ops["sp"].append((waits, None, None))

    def emit(self):
        nc = self.nc
        block = self.es.enter_context(nc.Block())
        names = {"pe": "tensor", "act": "scalar", "dve": "vector", "pool": "gpsimd", "sp": "sync"}

        def mk(e):
            def body(engine):
                for waits, fn, inc in self.ops[e]:
                    for s, v in waits:
                        engine.wait_ge(self.sems[s], v)
                    if fn is not None:
                        inst = getattr(engine, fn[0])(*fn[1], **fn[2])
                        if inc is not None:
                            inst.then_inc(self.sems[inc[0]], inc[1])
            return body

        for e in ENGS:
            getattr(block, names[e])(mk(e))


D = 1024
EPS = 1e-6
THETA = 500000.0
TWO_PI = 2.0 * math.pi
C1 = 6.28125
C2 = TWO_PI - 6.28125


def build_consts(P):
    C = {}
    it = P.sb("c_iota_i", [128, 128], I32)
    itf = P.sb("c_iota_f", [128, 128], F32)
    ident = P.sb("c_ident", [128, 128], BF16)
    identf = P.sb("c_identf", [128, 128], F32)
    P.op("pool", "iota", it[:], pattern=[[1, 128]], base=0, channel_multiplier=-1, writes=["c_iota_i"])
    P.op("dve", "tensor_copy", itf[:], it[:], reads=["c_iota_i"], writes=["c_iota_f"])
    P.op("dve", "tensor_single_scalar", ident[:], itf[:], 0.0, op=ALU.is_equal, reads=["c_iota_f"], writes=["c_ident"])
    P.op("dve", "tensor_single_scalar", identf[:], itf[:], 0.0, op=ALU.is_equal, reads=["c_iota_f"], writes=["c_identf"])
    eps = P.sb("c_eps", [128, 1], F32)
    one = P.sb("c_one", [128, 1], F32)
    P.op("dve", "memset", eps[:], EPS, writes=["c_eps"])
    P.op("dve", "memset", one[:], 1.0, writes=["c_one"])
    C["ident"] = ident
    C["identf"] = identf
    C["eps"] = eps
    C["one"] = one
    return C


def rope_tables(P, posf, kp, NT, half, name):
    cos = P.sb(name + "_cos", [128, NT, half], F32)
    sin = P.sb(name + "_sin", [128, NT, half], F32)
    ang = P.sb(name + "_ang", [128, NT, half], F32)
    kf = P.sb(name + "_kf", [128, NT, half], F32)
    ki = P.sb(name + "_ki", [128, NT, half], I32)
    r = P.sb(name + "_r", [128, NT, half], F32)
    m = P.sb(name + "_m", [128, NT, half], F32)
    ka, kk, kr, km = name + "_ang", name + "_kf", name + "_r", name + "_m"
    for i in range(half):
        inv = float(np.float32(THETA ** (-float(i) / half)))
        P.op("dve", "tensor_scalar", ang[:, :, i], posf[:], inv, None, op0=ALU.mult, reads=[kp], writes=[ka])
    for which, dst, shift in (("s", sin, 0.0), ("c", cos, math.pi / 2)):
        kd = name + ("_sin" if which == "s" else "_cos")
        P.op("dve", "tensor_scalar", kf[:], ang[:], shift, 1.0 / TWO_PI, op0=ALU.add, op1=ALU.mult, reads=[ka], writes=[kk])
        P.op("dve", "tensor_copy", ki[:], kf[:], reads=[kk], writes=[name + "_ki"])
        P.op("dve", "tensor_copy", kf[:], ki[:], reads=[name + "_ki"], writes=[kk])
        P.op("dve", "scalar_tensor_tensor", out=r[:], in0=kf[:], scalar=-C1, in1=ang[:], op0=ALU.mult, op1=ALU.add, reads=[kk, ka], writes=[kr])
        P.op("dve", "scalar_tensor_tensor", out=r[:], in0=kf[:], scalar=-C2, in1=r[:], op0=ALU.mult, op1=ALU.add, reads=[kk, kr], writes=[kr])
        if shift != 0.0:
            P.op("dve", "tensor_scalar", r[:], r[:], shift, None, op0=ALU.add, reads=[kr], writes=[kr])
        P.op("dve", "tensor_scalar", m[:], r[:], math.pi, -TWO_PI, op0=ALU.is_gt, op1=ALU.mult, reads=[kr], writes=[km])
        P.op("dve", "tensor_tensor", out=r[:], in0=r[:], in1=m[:], op=ALU.add, reads=[kr, km], writes=[kr])
        P.op("dve", "tensor_scalar", m[:], r[:], -math.pi, TWO_PI, op0=ALU.is_lt, op1=ALU.mult, reads=[kr], writes=[km])
        P.op("dve", "tensor_tensor", out=r[:], in0=r[:], in1=m[:], op=ALU.add, reads=[kr, km], writes=[kr])
        P.op("dve", "tensor_scalar", r[:], r[:], -3.14159, 3.14159, op0=ALU.max, op1=ALU.min, reads=[kr], writes=[kr])
        P.op("act", "activation", dst[:], r[:], AF.Sin, reads=[kr], writes=[kd])
    return cos, sin


def load_weight_bf16(P, w_dram, ncol, name, stg=None, stg_keys=None, kchunks=8, engines=None):
    wb = P.sb(name, [128, kchunks, ncol], BF16)
    wv = w_dram.rearrange("(kc p) n -> p kc n", p=128)
    n0 = 0
    while n0 < ncol:
        n1 = min(n0 + 512, ncol)
        P.dma("pool", wb[:, :, n0:n1], wv[:, :, n0:n1], writes=[(name, n0)])
        n0 = n1
    return wb


def wkeys(name, n0, n1):
    return [(name, c) for c in range((n0 // 512) * 512, n1, 512)]


def emit_proj(P, C, layer, io, NT):
    nc = P.nc
    NCOL = 3368 if layer == 0 else 3088
    NG = NT // 4
    ident, identf, eps, one = C["ident"], C["identf"], C["eps"], C["one"]
    pfx = "p%d_" % layer

    wb = load_weight_bf16(P, io["w_in"], NCOL, pfx + "w")
    lnb = P.sb(pfx + "lnb", [128, D], F32)
    P.dma("sp", lnb[:], io["ln"].partition_broadcast(128), writes=[pfx + "lnb"])
    G = P.sb(pfx + "G", [128, 32, 64], F32)
    G4 = P.sb(pfx + "G4", [128, 4, 64], F32)
    gl = io["gains"]
    for i, g in enumerate(gl):
        P.dma("sp", G4[:, i, :], g.partition_broadcast(128), writes=[(pfx + "G4", i)])
    per = 32 // len(gl)
    for i in range(len(gl)):
        P.op("dve", "tensor_copy", G[:, i * per:(i + 1) * per, :], G4[:, i, :].unsqueeze(1).to_broadcast([128, per, 64]), reads=[(pfx + "G4", i)], writes=[(pfx + "G", i)])
    Gkeys = [(pfx + "G", i) for i in range(len(gl))]
    posf = P.sb(pfx + "posf", [128, NT], F32)
    P.dma("sp", posf[:], io["pos"], writes=[pfx + "posf"])
    if layer == 0:
        cos8, sin8 = rope_tables(P, posf, pfx + "posf", NT, 8, pfx + "r8")
        cos4, sin4 = rope_tables(P, posf, pfx + "posf", NT, 4, pfx + "r4")
        gik = P.sb(pfx + "gik", [128, 32], F32)
        P.dma("sp", gik[:], io["g_ik"].partition_broadcast(128), writes=[pfx + "gik"])
        iw_all = P.sb(pfx + "iw_all", [128, NT, 8], F32)
    else:
        bfb = P.sb(pfx + "bfb", [128, 16], F32)
        P.dma("sp", bfb[:], io["b_f"].partition_broadcast(128), writes=[pfx + "bfb"])
        lfT = P.sb(pfx + "lfT", [16, NT * 128], F32)

    xt = [P.sb(pfx + "x%d" % i, [128, D], F32) for i in range(2)]
    junk = P.sb(pfx + "junk", [128, D], BF16)
    hb = [P.sb(pfx + "h%d" % i, [128, D], BF16) for i in range(2)]
    hT = [P.sb(pfx + "hT%d" % i, [128, 8, 128], BF16) for i in range(2)]
    pj = [P.sb(pfx + "pj%d" % i, [128, NCOL], F32) for i in range(2)]
    st_l = [P.sb(pfx + "st%d" % i, [128, 8], F32) for i in range(2)]
    tmpA_l = [P.sb(pfx + "tmpA%d" % i, [128, 32, 64], F32) for i in range(2)]
    hst_l = [P.sb(pfx + "hst%d" % i, [128, 3, 32], F32) for i in range(2)]
    rt_l = [P.sb(pfx + "rt%d" % i, [128, 4, 32, 8], F32) for i in range(2)]
    nr = [P.sb(pfx + "nr%d" % i, [128, 32, 64], BF16) for i in range(2)]
    vt = [P.sb(pfx + "vt%d" % i, [128, 1024], BF16) for i in range(2)]
    qkT_g = P.sb(pfx + "qkTg", [64, 32, 512], BF16)
    tp = [P.ps(pfx + "tp%d" % i, [128, 8, 128], BF16) for i in range(2)]
    mm = [P.ps(pfx + "mm%d" % i, [128, 512], F32) for i in range(3)]
    if layer == 0:
        tI = P.sb(pfx + "tI", [128, 9, 32], F32)
        rti_l = [P.sb(pfx + "rti%d" % i, [128, 4, 9, 4], F32) for i in range(2)]
        ihi = P.sb(pfx + "ihi", [128, 9, 32], BF16)
        ilo = P.sb(pfx + "ilo", [128, 9, 32], BF16)
        i3 = [P.sb(pfx + "i3_%d" % i, [128, 9, 96], BF16) for i in range(2)]
        iqT_g = P.sb(pfx + "iqTg", [96, 8, 512], BF16)
        ikT_g = P.sb(pfx + "ikTg", [96, 512], BF16)
    else:
        lft = P.sb(pfx + "lft", [128, 4, 16], F32)
        tpf = P.ps(pfx + "tpf", [16, 128], F32)

    def K(n, i=None):
        return (pfx + n) if i is None else (pfx + n, i)

    tpc = 0

    def tile_body(t):
        nonlocal tpc
        b = t % 2
        g, tl = t // 4, t % 4
        st, tmpA, hst, rt = st_l[b], tmpA_l[b], hst_l[b], rt_l[b]

        def KK(n, i=None, b=b):
            return K(n + "_b%d" % b, i)
        x = xt[b]
        kx = K("x", b)
        P.op("act", "activation", junk[:], x[:], AF.Square, accum_out=st[:, 0:1], reads=[kx], writes=[K("junk"), KK("st", 0)])
        P.op("act", "activation", st[:, 1:2], st[:, 0:1], AF.Sqrt, bias=eps[:], scale=1.0 / D, reads=[KK("st", 0), "c_eps"], writes=[KK("st", 1)])
        P.op("dve", "reciprocal", st[:, 2:3], st[:, 1:2], reads=[KK("st", 1)], writes=[KK("st", 2)])
        h = hb[b]
        P.op("dve", "scalar_tensor_tensor", out=h[:], in0=x[:], scalar=st[:, 2:3], in1=lnb[:], op0=ALU.mult, op1=ALU.mult, reads=[kx, KK("st", 2), K("lnb")], writes=[K("h", b)])
        if t + 2 < NT:
            P.dma("sp", x[:], io["x"][(t + 2) * 128:(t + 3) * 128, :], writes=[kx])
        tpp = tp[tpc % 2]
        ktp = K("tp", tpc % 2)
        tpc += 1
        for kc in range(8):
            P.op("pe", "transpose", tpp[:, kc, :], h[:, kc * 128:(kc + 1) * 128], ident[:], reads=[K("h", b), "c_ident"], writes=[ktp])
        hTt = hT[b]
        P.op("act", "copy", hTt[:], tpp[:], reads=[ktp], writes=[K("hT", b)])
        yield "A"
        pjt = pj[b]
        n0 = 0
        gi = 0
        while n0 < NCOL:
            n1 = min(n0 + 512, NCOL)
            m = mm[gi % 3]
            km = K("mm", gi % 3)
            for kc in range(8):
                P.op("pe", "matmul", m[:, 0:n1 - n0], lhsT=hTt[:, kc, :], rhs=wb[:, kc, n0:n1], start=(kc == 0), stop=(kc == 7), reads=[K("hT", b)] + wkeys(pfx + "w", n0, n1), writes=[km])
            if gi % 2 == 0:
                P.op("act", "copy", pjt[:, n0:n1], m[:, 0:n1 - n0], reads=[km], writes=[K("pj", b)])
            else:
                P.op("dve", "tensor_copy", pjt[:, n0:n1], m[:, 0:n1 - n0], reads=[km], writes=[K("pj", b)])
            n0 = n1
            gi += 1
            yield "B"
        yield "Bend"
        kpj = K("pj", b)
        nrt = nr[b]
        knr = K("nr", b)

        def normrope(src, H, h0, rope_half, cos, sin, Gsl, Gk):
            ta = tmpA[:, h0:h0 + H, :]
            kta = KK("tmpA", h0)
            P.op("act", "activation", ta, src, AF.Square, reads=[kpj], writes=[kta])
            yield "C"
            P.op("dve", "tensor_reduce", out=hst[:, 0, h0:h0 + H], in_=ta, axis=AX.X, op=ALU.add, reads=[kta], writes=[KK("hst", h0)])
            yield "C"
            P.op("act", "activation", hst[:, 1, h0:h0 + H], hst[:, 0, h0:h0 + H], AF.Sqrt, bias=eps[:], scale=1.0 / 64, reads=[KK("hst", h0), "c_eps"], writes=[KK("hst1", h0)])
            yield "C"
            P.op("dve", "reciprocal", hst[:, 2, h0:h0 + H], hst[:, 1, h0:h0 + H], reads=[KK("hst1", h0)], writes=[KK("hst2", h0)])
            yield "C"
            P.op("dve", "tensor_tensor", out=ta, in0=src, in1=hst[:, 2, h0:h0 + H].unsqueeze(2).to_broadcast([128, H, 64]), op=ALU.mult, reads=[kpj, KK("hst2", h0)], writes=[kta])
            yield "C"
            P.op("dve", "tensor_tensor", out=ta, in0=ta, in1=Gsl, op=ALU.mult, reads=[kta] + Gk, writes=[kta])
            yield "C"
            if rope_half:
                hf = rope_half
                x1 = tmpA[:, h0:h0 + H, 0:hf]
                x2 = tmpA[:, h0:h0 + H, hf:2 * hf]
                cb = cos[:, t, :].unsqueeze(1).to_broadcast([128, H, hf])
                sb_ = sin[:, t, :].unsqueeze(1).to_broadcast([128, H, hf])
                r = [rt[:, i, h0:h0 + H, :] for i in range(4)]
                krt = KK("rt", h0)
                P.op("dve", "tensor_tensor", out=r[0], in0=x1, in1=cb, op=ALU.mult, reads=[kta], writes=[(krt, 0)])
                yield "C"
                P.op("dve", "tensor_tensor", out=r[1], in0=x2, in1=sb_, op=ALU.mult, reads=[kta], writes=[(krt, 1)])
                yield "C"
                P.op("dve", "tensor_tensor", out=r[2], in0=x2, in1=cb, op=ALU.mult, reads=[kta], writes=[(krt, 2)])
                yield "C"
                P.op("dve", "tensor_tensor", out=r[3], in0=x1, in1=sb_, op=ALU.mult, reads=[kta], writes=[(krt, 3)])
                yield "C"
                P.op("dve", "tensor_tensor", out=nrt[:, h0:h0 + H, 0:hf], in0=r[0], in1=r[1], op=ALU.subtract, reads=[(krt, 0), (krt, 1)], writes=[(knr, h0, 0)])
                yield "C"
                P.op("dve", "tensor_tensor", out=nrt[:, h0:h0 + H, hf:2 * hf], in0=r[2], in1=r[3], op=ALU.add, reads=[(krt, 2), (krt, 3)], writes=[(knr, h0, 1)])
                yield "C"
                P.op("act", "copy", nrt[:, h0:h0 + H, 2 * hf:64], tmpA[:, h0:h0 + H, 2 * hf:64], reads=[kta], writes=[(knr, h0, 2)])
                yield "C"
                return [(knr, h0, 0), (knr, h0, 1), (knr, h0, 2)]
            else:
                P.op("act", "copy", nrt[:, h0:h0 + H, :], ta, reads=[kta], writes=[(knr, h0, 2)])
                yield "C"
                return [(knr, h0, 2)]

        if layer == 0:
            k1 = yield from normrope(pjt[:, 0:1024].rearrange("p (h d) -> p h d", d=64), 16, 0, 8, cos8, sin8, G[:, 0:16, :], Gkeys[0:2])
            k2 = yield from normrope(pjt[:, 1832:2856].rearrange("p (h d) -> p h d", d=64), 16, 16, 8, cos8, sin8, G[:, 16:32, :], Gkeys[2:4])
            nrkeys = {0: k1, 16: k2}
        else:
            k1 = yield from normrope(pjt[:, 0:1024].rearrange("p (h d) -> p h d", d=64), 16, 0, 0, None, None, G[:, 0:16, :], Gkeys[0:1])
            k2 = yield from normrope(pjt[:, 1024:2048].rearrange("p (h d) -> p h d", d=64), 16, 16, 0, None, None, G[:, 16:32, :], Gkeys[1:2])
            nrkeys = {0: k1, 16: k2}
        for q8 in range(4):
            tpp = tp[tpc % 2]
            ktp = K("tp", tpc % 2)
            tpc += 1
            for j in range(8):
                hh = q8 * 8 + j
                P.op("pe", "transpose", tpp[0:64, j, :], nrt[:, hh, :], ident[:], reads=nrkeys[(hh // 16) * 16] + ["c_ident"], writes=[ktp])
                yield "C"
            eng = "act" if q8 % 2 == 0 else "dve"
            if eng == "act":
                P.op("act", "copy", qkT_g[:, q8 * 8:(q8 + 1) * 8, tl * 128:(tl + 1) * 128], tpp[0:64, :, :], reads=[ktp], writes=[K("qkTg", q8)])
                yield "C"
            else:
                P.op("dve", "tensor_copy", qkT_g[:, q8 * 8:(q8 + 1) * 8, tl * 128:(tl + 1) * 128], tpp[0:64, :, :], reads=[ktp], writes=[K("qkTg", q8)])
                yield "C"
        vtt = vt[b]
        if layer == 0:
            P.op("dve", "tensor_copy", vtt[:, 0:512], pjt[:, 1024:1536], reads=[kpj], writes=[K("vt", b)])
            yield "C"
            P.op("dve", "tensor_copy", vtt[:, 512:1024], pjt[:, 2856:3368], reads=[kpj], writes=[K("vt", b)])
            yield "C"
        else:
            P.op("dve", "tensor_copy", vtt[:], pjt[:, 2048:3072], reads=[kpj], writes=[K("vt", b)])
            yield "C"
        gkeys = io.setdefault("grp_keys", {}).setdefault(g, [])
        if layer == 0:
            P.dma("act", io["va_p"][g][:, :, tl, :], vtt[:, 0:512].rearrange("p (h d) -> p h d", d=64), reads=[K("vt", b)], writes=[("dram_va", t)])
            yield "C"
            P.dma("act", io["vb_p"][g][:, :, tl, :], vtt[:, 512:1024].rearrange("p (h d) -> p h d", d=128), reads=[K("vt", b)], writes=[("dram_vb", t)])
            yield "C"
            gkeys += [("dram_va", t), ("dram_vb", t)]
        else:
            P.dma("act", io["vc_p"][g][:, :, tl, :], vtt[:, :].rearrange("p (h d) -> p h d", d=64), reads=[K("vt", b)], writes=[("dram_vc", t)])
            yield "C"
            gkeys += [("dram_vc", t)]
        if layer == 0:
            P.op("dve", "tensor_copy", tI[:, 0:8, :], pjt[:, 1536:1792].rearrange("p (h d) -> p h d", d=32), reads=[kpj], writes=[K("tI", 0)])
            yield "C"
            P.op("act", "activation", junk[:, 0:32], pjt[:, 1792:1824], AF.Square, accum_out=st[:, 3:4], reads=[kpj], writes=[K("junk"), KK("st", 3)])
            yield "C"
            P.op("act", "activation", st[:, 4:5], st[:, 3:4], AF.Sqrt, bias=eps[:], scale=1.0 / 32, reads=[KK("st", 3), "c_eps"], writes=[KK("st", 4)])
            yield "C"
            P.op("dve", "reciprocal", st[:, 5:6], st[:, 4:5], reads=[KK("st", 4)], writes=[KK("st", 5)])
            yield "C"
            P.op("dve", "scalar_tensor_tensor", out=tI[:, 8, :], in0=pjt[:, 1792:1824], scalar=st[:, 5:6], in1=gik[:], op0=ALU.mult, op1=ALU.mult, reads=[kpj, KK("st", 5), K("gik")], writes=[K("tI", 1)])
            yield "C"
            kti = [K("tI", 0), K("tI", 1)]
            x1 = tI[:, :, 0:4]
            x2 = tI[:, :, 4:8]
            cb = cos4[:, t, :].unsqueeze(1).to_broadcast([128, 9, 4])
            sb_ = sin4[:, t, :].unsqueeze(1).to_broadcast([128, 9, 4])
            rr = [rti_l[b][:, i, :, :] for i in range(4)]
            kr = KK("rti")
            P.op("dve", "tensor_tensor", out=rr[0], in0=x1, in1=cb, op=ALU.mult, reads=kti, writes=[(kr, 0)])
            yield "C"
            P.op("dve", "tensor_tensor", out=rr[1], in0=x2, in1=sb_, op=ALU.mult, reads=kti, writes=[(kr, 1)])
            yield "C"
            P.op("dve", "tensor_tensor", out=rr[2], in0=x2, in1=cb, op=ALU.mult, reads=kti, writes=[(kr, 2)])
            yield "C"
            P.op("dve", "tensor_tensor", out=rr[3], in0=x1, in1=sb_, op=ALU.mult, reads=kti, writes=[(kr, 3)])
            yield "C"
            P.op("dve", "tensor_tensor", out=tI[:, :, 0:4], in0=rr[0], in1=rr[1], op=ALU.subtract, reads=[(kr, 0), (kr, 1)], writes=kti)
            yield "C"
            P.op("dve", "tensor_tensor", out=tI[:, :, 4:8], in0=rr[2], in1=rr[3], op=ALU.add, reads=[(kr, 2), (kr, 3)], writes=kti)
            yield "C"
            P.op("dve", "tensor_copy", ihi[:], tI[:], reads=kti, writes=[K("ihi")])
            yield "C"
            P.op("dve", "tensor_tensor", out=ilo[:], in0=tI[:], in1=ihi[:], op=ALU.subtract, reads=kti + [K("ihi")], writes=[K("ilo")])
            yield "C"
            i3t = i3[b]
            ki3 = K("i3", b)
            P.op("dve", "tensor_copy", i3t[:, 0:8, 0:32], ihi[:, 0:8, :], reads=[K("ihi")], writes=[(ki3, 0)])
            yield "C"
            P.op("dve", "tensor_copy", i3t[:, 0:8, 32:64], ihi[:, 0:8, :], reads=[K("ihi")], writes=[(ki3, 1)])
            yield "C"
            P.op("dve", "tensor_copy", i3t[:, 0:8, 64:96], ilo[:, 0:8, :], reads=[K("ilo")], writes=[(ki3, 2)])
            yield "C"
            P.op("dve", "tensor_copy", i3t[:, 8, 0:32], ihi[:, 8, :], reads=[K("ihi")], writes=[(ki3, 3)])
            yield "C"
            P.op("dve", "tensor_copy", i3t[:, 8, 32:64], ilo[:, 8, :], reads=[K("ilo")], writes=[(ki3, 4)])
            yield "C"
            P.op("dve", "tensor_copy", i3t[:, 8, 64:96], ihi[:, 8, :], reads=[K("ihi")], writes=[(ki3, 5)])
            yield "C"
            i3keys = [(ki3, i) for i in range(6)]
            tpp = tp[tpc % 2]
            ktp = K("tp", tpc % 2)
            tpc += 1
            for j in range(8):
                P.op("pe", "transpose", tpp[0:96, j, :], i3t[:, j, :], ident[:], reads=i3keys + ["c_ident"], writes=[ktp])
                yield "C"
            P.op("act", "copy", iqT_g[:, :, tl * 128:(tl + 1) * 128], tpp[0:96, :, :], reads=[ktp], writes=[K("iqTg")])
            yield "C"
            tpp = tp[tpc % 2]
            ktp = K("tp", tpc % 2)
            tpc += 1
            P.op("pe", "transpose", tpp[0:96, 0, :], i3t[:, 8, :], ident[:], reads=i3keys + ["c_ident"], writes=[ktp])
            yield "C"
            P.op("dve", "tensor_copy", ikT_g[:, tl * 128:(tl + 1) * 128], tpp[0:96, 0, :], reads=[ktp], writes=[K("ikTg")])
            yield "C"
            P.op("dve", "tensor_scalar", iw_all[:, t, :], pjt[:, 1824:1832], 1.0 / 16.0, None, op0=ALU.mult, reads=[kpj], writes=[K("iw_all", t)])
            yield "C"
        else:
            z, a_, e_, m_ = lft[:, 0, :], lft[:, 1, :], lft[:, 2, :], lft[:, 3, :]
            P.op("dve", "tensor_tensor", out=z, in0=pjt[:, 3072:3088], in1=bfb[:], op=ALU.add, reads=[kpj, K("bfb")], writes=[K("lft", 0)])
            yield "C"
            P.op("act", "activation", a_, z, AF.Abs, reads=[K("lft", 0)], writes=[K("lft", 1)])
            yield "C"
            P.op("act", "activation", e_, a_, AF.Exp, scale=-1.0, reads=[K("lft", 1)], writes=[K("lft", 2)])
            yield "C"
            P.op("act", "activation", e_, e_, AF.Ln, bias=one[:], scale=1.0, reads=[K("lft", 2), "c_one"], writes=[K("lft", 2)])
            yield "C"
            P.op("dve", "tensor_scalar", m_, z, 0.0, None, op0=ALU.min, reads=[K("lft", 0)], writes=[K("lft", 3)])
            yield "C"
            P.op("dve", "tensor_tensor", out=m_, in0=m_, in1=e_, op=ALU.subtract, reads=[K("lft", 3), K("lft", 2)], writes=[K("lft", 3)])
            yield "C"
            P.op("pe", "transpose", tpf[:, :], m_, identf[:], reads=[K("lft", 3), "c_identf"], writes=[K("tpf")])
            yield "C"
            P.op("dve", "tensor_copy", lfT[:, t * 128:(t + 1) * 128], tpf[:, :], reads=[K("tpf")], writes=[K("lfT", t)])
            yield "C"
        if layer == 1 and t == NT - 1:
            P.dma("sp", io["logfT"], lfT[:], reads=[K("lfT", t_) for t_ in range(NT)], writes=["dram_logfT"])
            yield "C"
            if "on_lf" in io:
                io["on_lf"](["dram_logfT"])
        if tl == 3:
            for q8 in range(4):
                if layer == 0:
                    dst = (io["qT"], None, io["qT"], None)[q8]
                    h0 = (0, 0, 8, 8)[q8]
                else:
                    dst = (io["qT"], io["qT"], None, None)[q8]
                    h0 = (0, 8, 0, 8)[q8]
                if dst is io["qT"]:
                    P.dma("act", dst[:, h0:h0 + 8, g * 512:(g + 1) * 512], qkT_g[:, q8 * 8:(q8 + 1) * 8, :], reads=[K("qkTg", q8)], writes=[("dram_qkT", g, q8)])
                    yield "C"
                else:
                    P.dma("act", io["kT_p"][g][:, h0:h0 + 8, :], qkT_g[:, q8 * 8:(q8 + 1) * 8, :], reads=[K("qkTg", q8)], writes=[("dram_qkT", g, q8)])
                    gkeys.append(("dram_qkT", g, q8))
                    yield "C"
            if layer == 0:
                P.dma("act", io["iqT"][:, :, g * 512:(g + 1) * 512], iqT_g[:], reads=[K("iqTg")], writes=[("dram_iqT", g)])
                yield "C"
                P.dma("act", io["ikT_p"][g], ikT_g[:], reads=[K("ikTg")], writes=[("dram_ikT", g)])
                gkeys.append(("dram_ikT", g))
                yield "C"
            if "on_group" in io:
                io["on_group"](g, list(gkeys))

    gens = [tile_body(t) for t in range(NT)]

    def adv(gen, stops):
        while True:
            try:
                tag = next(gen)
            except StopIteration:
                return None
            if tag in stops:
                return tag

    for t_ in range(min(2, NT)):
        P.dma("sp", xt[t_ % 2][:], io["x"][t_ * 128:(t_ + 1) * 128, :], writes=[K("x", t_ % 2)])
    adv(gens[0], ("Bend",))
    for t in range(NT):
        cur = gens[t]
        if t + 1 < NT:
            nxt = gens[t + 1]
            adv(nxt, ("A",))
            alive = True
            while True:
                tag = adv(nxt, ("B", "Bend"))
                for _ in range(7):
                    if alive and adv(cur, ("C",)) is None:
                        alive = False
                if tag != "B":
                    break
        adv(cur, ())
    if layer == 0:
        P.dma("sp", io["iw"].rearrange("(t p) h -> p t h", p=128), iw_all[:], reads=[K("iw_all", t) for t in range(NT)], writes=["dram_iw"])


TOPK = 256
NITER = 20
BLO = -64.0
BW = 128.0
NEG = -1.0e30
MASK_DT = FP8
SAT = {"saturate": False}


def emit_attn(P, C, layer, io, NG=4, LA=2):
    ident, eps = C["ident"], C["eps"]
    pfx = "a%d_" % layer
    GK = io.get("gk", {})
    L0 = (layer == 0)
    NT = NG * 4
    Kd = 70

    def K(n, i=None):
        return (pfx + n) if i is None else (pfx + n, i)

    n_s = 3
    ps_s = [P.ps(pfx + "pss%d" % i, [128, 512], F32) for i in range(n_s)]
    n_os = 1 if L0 else 2
    ps_o_raw = [[P.ps(pfx + "pso%d_%d" % (s, b), [128, 512], F32) for b in range(2)] for s in range(n_os)]
    def oview_psum(os_, b, vd):
        return ps_o_raw[os_][b][:, 0:2 * (vd + 1)].rearrange("p (a b) -> p a b", b=vd + 1)
    n_x = 8 - n_s - 2 * n_os
    ps_x = [P.ps(pfx + "psx%d" % i, [128, 512], F32) for i in range(n_x)]
    xc = [0]

    def next_x():
        i = xc[0] % n_x
        xc[0] += 1
        return ps_x[i], K("psx", i)

    wout = load_weight_bf16(P, io["w_out"], 1024, pfx + "wo")
    if L0:
        score = P.sb(pfx + "score", [128, 8192], F32)
    kcol_i = P.sb(pfx + "kcol_i", [128, 64], I32)
    kcol = P.sb(pfx + "kcol", [128, 64], F32)
    qrow = P.sb(pfx + "qrow", [128, NT * 128], F32)
    P.dma("sp", qrow[:], io["qkey_row"].partition_broadcast(128), writes=[K("qrow")])
    if L0:
        qrow_b = P.sb(pfx + "qrow_b", [128, NT * 128], BF16)
        P.op("pool", "tensor_copy", qrow_b[:], qrow[:], reads=[K("qrow")], writes=[K("qrow_b")])
    if L0:
        P.op("pool", "iota", kcol_i[0:64, :], pattern=[[2, 64]], base=0, channel_multiplier=0, writes=[K("kcol_i")])
        P.op("pool", "iota", kcol_i[64:128, :], pattern=[[2, 64]], base=1, channel_multiplier=0, writes=[K("kcol_i")])
    else:
        P.op("pool", "iota", kcol_i[:, :], pattern=[[128, 64]], base=0, channel_multiplier=1, writes=[K("kcol_i")])
    P.op("dve", "tensor_copy", kcol[:], kcol_i[:], reads=[K("kcol_i")], writes=[K("kcol")])

    if L0:
        ikT = P.sb(pfx + "ikT", [96, 8192], BF16)
        for i_ in range(4):
            P.dma("sp", ikT[:, :].rearrange("d (i j q) -> d i j q", i=4, j=4, q=512)[:, i_, :, :], io["ikT_g"][i_], reads=GK.get(("ikT", i_), []), writes=[K("ikT", i_)])
        iw = P.sb(pfx + "iw", [128, NT, 8], F32)
        P.dma("sp", iw[:], io["iw"], writes=[K("iw")])
        qcrel = P.sb(pfx + "qcrel", [128, NT], F32)
        P.dma("sp", qcrel[:], io["qchk_col"], writes=[K("qcrel")])
        for i in range(NG):
            if i > 0:
                P.op("dve", "tensor_scalar", qcrel[:, 4 * i:4 * i + 4], qcrel[:, 4 * i:4 * i + 4], -32.0 * i, None, op0=ALU.add, reads=[K("qcrel")], writes=[K("qcrel")])
        relk_i = score[:, 4096:6144].bitcast(I32).rearrange("p (a b) -> p a b", b=64)
        relk = P.sb(pfx + "relk", [128, 2048], BF16)
        P.op("pool", "iota", relk_i, pattern=[[1, 32], [0, 64]], base=0, channel_multiplier=0, writes=[K("relk_i"), K("score")])
        P.op("dve", "tensor_copy", relk[:], relk_i.rearrange("p a b -> p (a b)"), reads=[K("relk_i"), K("score")], writes=[K("relk")])
        maskT = [P.sb(pfx + "maskT0", [128, 16 * (NG - 1 if NG > 1 else 1), 512], MASK_DT), P.sb(pfx + "maskT1", [128, 16 * NG, 512], MASK_DT)]
        mkb = [P.sb(pfx + "mk%d" % i, [128, 2048], BF16) for i in range(2)]
        rbuf = [P.sb(pfx + "r%d" % i, [128, 512], F32) for i in range(2)]
        bs = P.sb(pfx + "bs", [128, 8], F32)
        junkA = P.sb(pfx + "junkA", [128, 2048], FP8)
        iqb = [P.sb(pfx + "iqb%d" % i_, [96, 8, 128], BF16) for i_ in range(2)]
        ident2 = P.sb(pfx + "ident2", [128, 128], BF16)
        P.op("dve", "tensor_scalar", ident2[:], ident[:], 2.0, None, op0=ALU.mult, reads=["c_ident"], writes=[K("ident2")])
        L4 = P.sb(pfx + "L4", [128, 4, 64], F32)
        for i, nm in enumerate(["lq1", "lk1", "lq2", "lk2"]):
            P.dma("sp", L4[:, i, :], io[nm].partition_broadcast(128), writes=[K("L4", i)])
        lm = P.sb(pfx + "lm", [128, 8], F32)
        lj = P.sb(pfx + "lj", [128, 64], F32)
        for j in range(2):
            P.op("dve", "tensor_tensor", out=lj[:], in0=L4[:, 2 * j, :], in1=L4[:, 2 * j + 1, :], op=ALU.mult, reads=[K("L4", 2 * j), K("L4", 2 * j + 1)], writes=[K("lj")])
            P.op("dve", "tensor_reduce", out=lm[:, j:j + 1], in_=lj[:], axis=AX.X, op=ALU.add, reads=[K("lj")], writes=[K("lm", j)])
            P.op("act", "activation", lm[:, 2 + j:3 + j], lm[:, j:j + 1], AF.Exp, reads=[K("lm", j)], writes=[K("lm", 2 + j)])
        lam_init = 0.8 - 0.6 * math.exp(-0.3 * layer)
        P.op("dve", "tensor_tensor", out=lm[:, 4:5], in0=lm[:, 3:4], in1=lm[:, 2:3], op=ALU.subtract, reads=[K("lm", 2), K("lm", 3)], writes=[K("lm", 4)])
        P.op("dve", "tensor_scalar", lm[:, 5:6], lm[:, 4:5], -lam_init, None, op0=ALU.add, reads=[K("lm", 4)], writes=[K("neglam")])
        gsub = P.sb(pfx + "gsub", [128, 128], F32)
        P.dma("sp", gsub[:], io["subln"].partition_broadcast(128), writes=[K("gsub")])
        P.op("dve", "tensor_scalar", gsub[:], gsub[:], 1.0 - lam_init, None, op0=ALU.mult, reads=[K("gsub")], writes=[K("gsub")])
        o1 = P.sb(pfx + "o1", [128, 4, 128], F32)
        o2 = P.sb(pfx + "o2", [128, 4, 128], F32)
        sst = P.sb(pfx + "sst", [128, 3, 4], F32)

    NRING = 3
    Kc = [P.sb(pfx + "Kc%d" % i, [Kd, 2048], BF16) for i in range(NRING)]
    vdmax = 128 if L0 else 64
    Vc = [P.sb(pfx + "Vc%d" % i, [128, 16, vdmax + 1], BF16) for i in range(NRING)]
    for i in range(NRING):
        P.op("pool", "memset", Vc[i][:, :, 0:1], 1.0, writes=[K("Vone", i)])
        if L0:
            P.op("pool", "memset", Kc[i][64:70, :], 0.0, writes=[K("Kc6", i)])
    if L0:
        QTj = [P.sb(pfx + "QTj%d" % i_, [70, 512], BF16) for i_ in range(3)]
        for i_ in range(3):
            P.op("pool", "memset", QTj[i_][64:70, :], 0.0, writes=[K("QTj6", i_)])
    else:
        QTg = P.sb(pfx + "QTg", [Kd, 16, 512], BF16)
    NPT = 4
    PT = [P.sb(pfx + "pt%d" % i, [128, 512], BF16) for i in range(NPT)]
    ao = P.sb(pfx + "ao", [128, 4, 1024], BF16)
    aT = P.sb(pfx + "aT", [128, 8, 128], BF16)
    if L0:
        xt = [qrow[:, 0:1024], qrow[:, 1024:2048]]
    else:
        xt = [P.sb(pfx + "x%d" % i, [128, 1024], F32) for i in range(2)]
    rc = P.sb(pfx + "rc", [128, 4], F32)
    if L0:
        ocp = P.sb(pfx + "ocp", [128, 2, 258], F32)
    if not L0:
        negm = P.sb(pfx + "negm", [128, 16, 512], BF16)


    def prepass_steps(i):
        steps = []
        nch = i + 1
        nkg = 4 * nch
        q0 = i * 512
        mT = maskT[i % 2]
        mb = i % 2

        def s_load(qt):
            t = 4 * i + qt
            P.dma("pool", iqb[t % 2][:], io["iqT"][:, :, q0 + qt * 128:q0 + (qt + 1) * 128], writes=[K("iqb", t % 2)])

        def s_score(qt, kg, h):
            t = 4 * i + qt
            sk = K("score", kg)
            px, kpx = next_x()
            P.op("pe", "matmul", px[:, :], lhsT=iqb[t % 2][:, h, :], rhs=ikT[:, kg * 512:(kg + 1) * 512], start=True, stop=True,
                 reads=[K("iqb", t % 2), K("ikT", kg // 4)], writes=[kpx])
            r = rbuf[(kg * 8 + h) % 2]
            kr = K("r", (kg * 8 + h) % 2)
            P.op("act", "activation", r[:], px[:, :], AF.Relu, reads=[kr], writes=[kpx, kr])
            sc = score[:, kg * 512:(kg + 1) * 512]
            if h == 0:
                P.op("dve", "tensor_scalar", sc, r[:], iw[:, t, 0:1], None, op0=ALU.mult, reads=[kr, K("iw")], writes=[sk] + ([K("score")] if first_score[0] else []))
                first_score[0] = False
            else:
                P.op("dve", "scalar_tensor_tensor", out=sc, in0=r[:], scalar=iw[:, t, h:h + 1], in1=sc, op0=ALU.mult, op1=ALU.add, reads=[kr, K("iw"), sk], writes=[sk])

        def s_pen(qt, kg):
            t = 4 * i + qt
            sk = K("score", kg)
            a = kg - 4 * i
            r = rbuf[0]
            P.op("dve", "tensor_scalar", r[:], relk[:, a * 512:(a + 1) * 512], qcrel[:, t:t + 1], NEG, op0=ALU.is_gt, op1=ALU.mult,
                 reads=[K("relk"), K("qcrel")], writes=[K("r", 0)])
            P.op("dve", "tensor_tensor", out=score[:, kg * 512:(kg + 1) * 512], in0=score[:, kg * 512:(kg + 1) * 512], in1=r[:], op=ALU.add,
                 reads=[K("r", 0), sk], writes=[sk])

        skeys = [K("score", kg) for kg in range(nkg)]

        def s_binit():
            P.op("dve", "memset", bs[:, 1:2], BLO + BW / 2, writes=[K("bs", 1)])

        n_act = nch // 2
        n_d = nch - n_act
        BIG = 1.0e5

        def s_bis(it):
            w = BW / (2.0 ** (it + 1))
            if n_act:
                P.op("dve", "tensor_scalar", bs[:, 5:6], bs[:, 1:2], BIG, None, op0=ALU.mult, reads=[K("bs", 1)], writes=[K("bs", 5)])
            for c in range(n_d):
                P.op("dve", "tensor_scalar", mkb[0][:], score[:, c * 2048:(c + 1) * 2048], bs[:, 1:2], (bs[:, 2:3] if c > 0 else None),
                     op0=ALU.is_ge, op1=ALU.add, accum_out=bs[:, 2:3],
                     reads=skeys[4 * c:4 * c + 4] + [K("bs", 1)] + ([K("bs", 2)] if c > 0 else []), writes=[K("mk", 0), K("bs", 2)])
            for a_ in range(n_act):
                c = n_d + a_
                P.op("act", "activation", junkA[:], score[:, c * 2048:(c + 1) * 2048], AF.Tanh, bias=bs[:, 5:6], scale=-BIG, accum_out=bs[:, 6 + a_:7 + a_],
                     reads=skeys[4 * c:4 * c + 4] + [K("bs", 5)], writes=[K("junkA"), K("bs", 6 + a_)], saturate=False)
            thr = float(TOPK) - 0.5
            if n_act:
                P.op("dve", "scalar_tensor_tensor", out=bs[:, 2:3], in0=bs[:, 2:3], scalar=2.0, in1=bs[:, 6:7], op0=ALU.mult, op1=ALU.subtract,
                     reads=[K("bs", 2), K("bs", 6)], writes=[K("bs", 2)])
                if n_act == 2:
                    P.op("dve", "tensor_tensor", out=bs[:, 2:3], in0=bs[:, 2:3], in1=bs[:, 7:8], op=ALU.subtract, reads=[K("bs", 2), K("bs", 7)], writes=[K("bs", 2)])
                thr = 2.0 * thr - n_act * 2048.0
            P.op("dve", "tensor_scalar", bs[:, 3:4], bs[:, 2:3], thr, w, op0=ALU.is_ge, op1=ALU.mult, reads=[K("bs", 2)], writes=[K("bs", 3)])
            P.op("dve", "scalar_tensor_tensor", out=bs[:, 1:2], in0=bs[:, 3:4], scalar=-w / 2.0, in1=bs[:, 1:2], op0=ALU.add, op1=ALU.add,
                 reads=[K("bs", 3), K("bs", 1)], writes=[K("bs", 1)])

        def s_bfin():
            P.op("dve", "tensor_scalar", bs[:, 0:1], bs[:, 1:2], -BW / (2.0 ** (NITER + 1)), None, op0=ALU.add, reads=[K("bs", 1)], writes=[K("bs", 0)])

        def s_mask(qt, c):
            mk = mkb[c % 2]
            kmk = K("mk", c % 2)
            P.op("dve", "tensor_scalar", mk[:], score[:, c * 2048:(c + 1) * 2048], bs[:, 0:1], -240.0, op0=ALU.is_lt, op1=ALU.mult,
                 reads=skeys[4 * c:4 * c + 4] + [K("bs", 0)], writes=[kmk])

        def s_tr(qt, c, hf):
            mk = mkb[c % 2]
            kmk = K("mk", c % 2)
            px, kpx = next_x()
            pxb = px[:, :].bitcast(BF16).rearrange("p (a b) -> p a b", b=128)
            for j in range(8):
                kt = hf * 8 + j
                P.op("pe", "transpose", pxb[:, j, :], mk[:, kt * 128:(kt + 1) * 128], ident[:], reads=[kmk, "c_ident"], writes=[kpx])
            dst = mT[:, c * 16 + hf * 8:c * 16 + hf * 8 + 8, qt * 128:(qt + 1) * 128]
            if hf == 0:
                P.op("act", "copy", dst, pxb[:, 0:8, :], writes=[kpx, K("maskT", (mb, c * 2 + hf, qt))], **SAT)
            else:
                P.op("act", "copy", dst, pxb[:, 0:8, :], writes=[kpx, K("maskT", (mb, c * 2 + hf, qt))], **SAT)

        for qt in range(4):
            steps.append(lambda qt=qt: s_load(qt))
            for kg in range(nkg):
                for h in range(8):
                    steps.append(lambda qt=qt, kg=kg, h=h: s_score(qt, kg, h))
                if kg >= 4 * i:
                    steps.append(lambda qt=qt, kg=kg: s_pen(qt, kg))
            steps.append(s_binit)
            for it in range(NITER):
                steps.append(lambda it=it: s_bis(it))
            steps.append(s_bfin)
            for c in range(nch):
                steps.append(lambda qt=qt, c=c: s_mask(qt, c))
                for hf in range(2):
                    steps.append(lambda qt=qt, c=c, hf=hf: s_tr(qt, c, hf))
        return steps

    slot_base = [0]
    order = ([1, 2, 3, 0] if (L0 and NG == 4) else list(range(NG)))
    first_score = [True]
    for oi, i in enumerate(order):
        nch = i + 1
        q0 = i * 512
        if not L0:
            P.dma("sp", QTg[0:64, :, :], io["qT"][:, :, q0:q0 + 512], writes=[K("QTg")])
            P.dma("sp", QTg[64:70, :, :], io["caug_q"][:, :, q0:q0 + 512], writes=[K("QTg6")])
        qkeys = [] if L0 else [K("QTg"), K("QTg6")]
        if not L0:
            for kt in range(16):
                ktg = i * 16 + kt
                P.op("dve", "tensor_scalar", negm[:, kt, :], qrow[:, q0:q0 + 512], kcol[:, ktg:ktg + 1], -30000.0, op0=ALU.is_lt, op1=ALU.mult,
                     reads=[K("qrow"), K("kcol")], writes=[K("negm", kt)])

        if L0 and oi == 0:
            for st_ in prepass_steps(i):
                st_()
        nxt_steps = prepass_steps(order[oi + 1]) if (L0 and oi + 1 < NG) else []

        if L0:
            jobs = [dict(mode="dsa", vd=64, qh=h, kh=h, vname="va_g", vh=h) for h in range(8)]
            jobs += [dict(mode="chunk", vd=128, qh=8 + j, kh=8 + j, vname="vb_g", vh=j // 2, pair=j) for j in range(8)]
        else:
            jobs = [dict(mode="causal", vd=64, qh=h, kh=h, vname="vc_g", vh=h) for h in range(16)]
        steps = [(ji, c) for ji in range(len(jobs)) for c in range(nch)]
        units = [(ji, c, kt) for (ji, c) in steps for kt in range(16)]

        def slot_of(si):
            return (slot_base[0] + si) % NRING

        def emit_load2(si):
            ji, c = steps[si]
            jb = jobs[ji]
            sl = slot_of(si)
            if L0 and c == 0:
                P.dma("sp", QTj[ji % 3][0:64, :], io["qT"][:, jb["qh"], q0:q0 + 512], writes=[K("QTj", ji % 3)])
            P.dma("sp", Kc[sl][0:64, :].rearrange("d (r q) -> d r q", r=4), io["kT_g"][c][:, :, jb["kh"], :], reads=GK.get(("kT", c), []), writes=[K("Kc", sl)])
            if not L0:
                P.dma("sp", Kc[sl][64:70, :], io["caug_k"][jb["kh"], :, c * 2048:(c + 1) * 2048], writes=[K("Kc6", sl)])
            vd = jb["vd"]
            for r in range(4):
                P.dma("sp", Vc[sl][:, 4 * r:4 * r + 4, 1:1 + vd], io[jb["vname"]][c][jb["vh"]][:, r, :, :], reads=GK.get((jb["vname"], c), []), writes=[K("Vc", sl)])

        def ksl(si):
            sl = slot_of(si)
            return [K("Kc", sl), K("Kc6", sl)]

        def emit_qk(u):
            ji, c, kt = units[u]
            jb = jobs[ji]
            si = ji * nch + c
            sl = slot_of(si)
            ps = ps_s[u % n_s]
            kps = K("pss", u % n_s)
            addm = (jb["mode"] == "causal" and c == i) or jb["mode"] == "dsa"
            P.op("pe", "matmul", ps[:, :], lhsT=Kc[sl][0:Kd, kt * 128:(kt + 1) * 128], rhs=(QTj[ji % 3][:, :] if L0 else QTg[0:Kd, jb["qh"], :]), start=True, stop=(not addm),
                 reads=ksl(si) + qkeys + ([K("QTj", ji % 3), K("QTj6", ji % 3)] if L0 else []), writes=[kps])
            if jb["mode"] == "dsa":
                ktg_ = c * 16 + kt
                P.op("pe", "matmul", ps[:, :], lhsT=ident2[:, :], rhs=maskT[i % 2][:, ktg_, :], start=False, stop=True,
                     reads=[K("ident2")] + [K("maskT", (i % 2, ktg_ // 8, qt)) for qt in range(4)], writes=[kps])
            elif addm:
                P.op("pe", "matmul", ps[:, :], lhsT=ident[:, :], rhs=negm[:, kt, :], start=False, stop=True,
                     reads=["c_ident", K("negm", kt)], writes=[kps])
            pt = PT[u % NPT]
            kpt = K("pt", u % NPT)
            P.op("act", "activation", pt[:], ps[:, :], AF.Exp, scale=0.125, reads=[kps], writes=[kpt])
            ktg = c * 16 + kt
            if c == i and jb["mode"] == "chunk":
                P.op("dve", "scalar_tensor_tensor", out=pt[:], in0=qrow_b[:, q0:q0 + 512], scalar=kcol[:, ktg:ktg + 1], in1=pt[:], op0=ALU.is_ge, op1=ALU.mult,
                     reads=[kpt, K("qrow_b"), K("kcol")], writes=[kpt])

        def emit_pv(u):
            ji, c, kt = units[u]
            jb = jobs[ji]
            si = ji * nch + c
            sl = slot_of(si)
            vd = jb["vd"]
            os_ = ji % n_os
            pt = PT[u % NPT]
            kpt = K("pt", u % NPT)
            first = (c == 0 and kt == 0)
            last = (c == nch - 1 and kt == 15)
            for s in range(4):
                ob = oview_psum(os_, s // 2, vd)
                P.op("pe", "matmul", ob[:, s % 2, 0:vd + 1], lhsT=pt[:, s * 128:(s + 1) * 128], rhs=Vc[sl][:, kt, 0:vd + 1],
                     start=(first and s % 2 == 0), stop=last, skip_group_check=True,
                     reads=[kpt, K("Vc", sl), K("Vone", sl)], writes=[K("pso", (os_, s // 2))])
            if last:
                emit_norm(ji)

        def emit_norm(ji):
            jb = jobs[ji]
            vd = jb["vd"]
            os_ = ji % n_os
            okeys = [K("pso", (os_, 0)), K("pso", (os_, 1))]
            if L0:
                P.op("act", "copy", ocp[:, 0, 0:2 * (vd + 1)], ps_o_raw[os_][0][:, 0:2 * (vd + 1)], writes=[okeys[0], K("ocp", 0)])
                P.op("dve", "tensor_copy", ocp[:, 1, 0:2 * (vd + 1)], ps_o_raw[os_][1][:, 0:2 * (vd + 1)], writes=[okeys[1], K("ocp", 1)])
                okeys = [K("ocp", 0), K("ocp", 1)]

                def oview(os__, b_, vd_):
                    return ocp[:, b_, 0:2 * (vd_ + 1)].rearrange("p (a b) -> p a b", b=vd_ + 1)
            else:
                oview = oview_psum
            for s in range(4):
                ob = oview(os_, s // 2, vd)
                P.op("dve", "reciprocal", rc[:, s:s + 1], ob[:, s % 2, 0:1], writes=[okeys[s // 2], K("rc", s)])
            if jb["mode"] != "chunk":
                h = jb["qh"]
                for s in range(4):
                    ob = oview(os_, s // 2, vd)
                    if s // 2 == 0:
                        P.op("act", "activation", ao[:, s, h * 64:(h + 1) * 64], ob[:, s % 2, 1:1 + vd], AF.Copy, scale=rc[:, s:s + 1],
                             reads=[K("rc", s)], writes=[okeys[s // 2], K("ao", (s, h))])
                    else:
                        P.op("dve", "tensor_scalar", ao[:, s, h * 64:(h + 1) * 64], ob[:, s % 2, 1:1 + vd], rc[:, s:s + 1], None, op0=ALU.mult,
                             reads=[K("rc", s)], writes=[okeys[s // 2], K("ao", (s, h))])
            else:
                j = jb["pair"]
                hb, m = j // 2, j % 2
                dst = o1 if m == 0 else o2
                for s in range(4):
                    ob = oview(os_, s // 2, vd)
                    P.op("dve", "tensor_scalar", dst[:, s, :], ob[:, s % 2, 1:1 + vd], rc[:, s:s + 1], None, op0=ALU.mult,
                         reads=[K("rc", s)], writes=[okeys[s // 2], K("o12", (m, s))])
                if m == 1:
                    o12 = [K("o12", (mm_, s)) for mm_ in range(2) for s in range(4)]
                    P.op("dve", "scalar_tensor_tensor", out=o1[:], in0=o2[:], scalar=lm[:, 5:6], in1=o1[:], op0=ALU.mult, op1=ALU.add,
                         reads=o12 + [K("neglam")], writes=o12)
                    P.op("dve", "tensor_tensor", out=o2[:], in0=o1[:], in1=o1[:], op=ALU.mult, reads=o12, writes=o12)
                    P.op("dve", "tensor_reduce", out=sst[:, 0, :], in_=o2[:], axis=AX.X, op=ALU.add, reads=o12, writes=[K("sst", 0)])
                    P.op("act", "activation", sst[:, 1, :], sst[:, 0, :], AF.Sqrt, bias=eps[:], scale=1.0 / 128, reads=[K("sst", 0), "c_eps"], writes=[K("sst", 1)])
                    P.op("dve", "reciprocal", sst[:, 2, :], sst[:, 1, :], reads=[K("sst", 1)], writes=[K("sst", 2)])
                    P.op("dve", "tensor_tensor", out=o1[:], in0=o1[:], in1=sst[:, 2, :].unsqueeze(2).to_broadcast([128, 4, 128]), op=ALU.mult,
                         reads=o12 + [K("sst", 2)], writes=o12)
                    P.op("dve", "tensor_tensor", out=ao[:, :, 512 + hb * 128:512 + (hb + 1) * 128], in0=o1[:], in1=gsub[:].unsqueeze(1).to_broadcast([128, 4, 128]), op=ALU.mult,
                         reads=o12 + [K("gsub")], writes=[K("ao", (s, 8 + 2 * hb + e)) for s in range(4) for e in range(2)])

        nst = len(steps)
        nxt_done = [0]
        emit_load2(0)
        if nst > 1:
            emit_load2(1)
        nu = len(units)
        for idx in range(nu + LA):
            if idx < nu:
                emit_qk(idx)
            if idx - LA >= 0:
                emit_pv(idx - LA)
            if idx < nu:
                ji, c, kt = units[idx]
                si = ji * nch + c
                if kt == LA and si + 2 < nst:
                    emit_load2(si + 2)
                want = ((idx + 1) * len(nxt_steps)) // nu
                while nxt_done[0] < want:
                    nxt_steps[nxt_done[0]]()
                    nxt_done[0] += 1
        slot_base[0] = (slot_base[0] + nst) % NRING
        while nxt_done[0] < len(nxt_steps):
            nxt_steps[nxt_done[0]]()
            nxt_done[0] += 1

        aokeys = lambda s: [K("ao", (s, h)) for h in range(16)]
        for s in range(4):
            t = 4 * i + s
            px, kpx = next_x()
            pxb = px[:, :].bitcast(BF16).rearrange("p (a b) -> p a b", b=128)
            for kc in range(8):
                P.op("pe", "transpose", pxb[:, kc, :], ao[:, s, kc * 128:(kc + 1) * 128], ident[:], reads=aokeys(s) + ["c_ident"], writes=[kpx])
            P.op("act", "copy", aT[:], pxb[:, 0:8, :], reads=[kpx], writes=[K("aT")])
            x = xt[t % 2]
            kx = K("x", t % 2)
            P.dma("sp", x[:], io["x"][t * 128:(t + 1) * 128, :], writes=[kx] + ([K("qrow")] if L0 else []))
            for n in range(2):
                px, kpx = next_x()
                for kc in range(8):
                    P.op("pe", "matmul", px[:, :], lhsT=aT[:, kc, :], rhs=wout[:, kc, n * 512:(n + 1) * 512], start=(kc == 0), stop=(kc == 7),
                         reads=[K("aT")] + wkeys(pfx + "wo", n * 512, (n + 1) * 512), writes=[kpx])
                P.op("dve", "tensor_tensor", out=x[:, n * 512:(n + 1) * 512], in0=x[:, n * 512:(n + 1) * 512], in1=px[:, :], op=ALU.add, reads=[kx, kpx], writes=[kx])
            P.dma("pool", io["out"][t * 128:(t + 1) * 128, :], x[:], reads=[kx], writes=[("dram_out", t)])
            if s == 3:
                P.dma("sp", io["tails"][2 * i:2 * i + 2, :], x[126:128, :], reads=[kx], writes=[("dram_tails", i)])


DFF = 2816
NFC = 44
NGC = 22


def emit_ffn(P, C, io, NG=4, pfx="f_"):
    ident, eps = C["ident"], C["eps"]

    def K(n, i=None):
        return (pfx + n) if i is None else (pfx + n, i)

    ps_u = [P.ps(pfx + "psu%d" % i, [128, 512], F32) for i in range(3)]
    ps_h = [P.ps(pfx + "psh%d" % i, [128, 512], F32) for i in range(2)]
    ps_d = [P.ps(pfx + "psd%d" % i, [128, 512], F32) for i in range(2)]
    ps_t = P.ps(pfx + "pst", [128, 8, 128], BF16)

    xg = P.sb(pfx + "xg", [128, 4, 1024], F32)
    wup = load_weight_bf16(P, io["w_up"], 2 * DFF, pfx + "wup")
    wdn = P.sb(pfx + "wdn", [128, NGC, 1024], BF16)
    wdv = io["w_down"].rearrange("(c p) n -> p c n", p=128)
    for c0 in range(0, NGC, 4):
        c1 = min(c0 + 4, NGC)
        P.dma("pool", wdn[:, c0:c1, :], wdv[:, c0:c1, :], writes=[K("wdn", c0)])
    wdn_keys = [K("wdn", c0) for c0 in range(0, NGC, 4)]
    lnb = P.sb(pfx + "lnb", [128, 1024], F32)
    P.dma("sp", lnb[:], io["ln"].partition_broadcast(128), writes=[K("lnb")])
    cw = P.sb(pfx + "cw", [128, 4, NFC], F32)
    P.dma("sp", cw[:], io["cw"], writes=[K("cw")])

    xh = P.sb(pfx + "xh", [2, 1024], F32)
    e1h = P.sb(pfx + "e1h", [2, 4], F32)
    P.dma("sp", e1h[:], io["e1h"][0:2, :], writes=[K("e1h")])
    st = P.sb(pfx + "st", [128, 4], F32)
    hb = [P.sb(pfx + "h%d" % i, [128, 1024], BF16) for i in range(2)]
    hT = P.sb(pfx + "hT", [128, 8, 514], BF16)
    U = [P.sb(pfx + "U%d" % i, [128, 514], F32) for i in range(2)]
    cv = [P.sb(pfx + "cv%d" % i, [128, 512], F32) for i in range(3)]
    sg = [P.sb(pfx + "sg%d" % i, [128, 512], F32) for i in range(2)]
    actT = P.sb(pfx + "actT", [128, NGC, 512], BF16)
    cand = actT[0:2, 0:4, :].rearrange("p a b -> p (a b)").bitcast(F32)

    for g in range(NG):
        ck = [K("cand")] + [K("actT", fc) for fc in range(4)]
        for jp in range(4):
            if jp == 0 and g == 0:
                P.op("pool", "memset", cand, 0.0, writes=ck)
            else:
                r_src = (jp - 1) % 4
                g_src = g - (1 if jp == 0 else 0)
                P.dma("sp", cand, io["tails_g"][r_src, 2 * g_src:2 * g_src + 2, :], reads=io.get("gk_tails", []), writes=ck)
            if jp == 0:
                P.op("dve", "tensor_scalar", xh[:], cand, e1h[:, 0:1], None, op0=ALU.mult, reads=ck + [K("e1h")], writes=[K("xh")])
            else:
                P.op("dve", "scalar_tensor_tensor", out=xh[:], in0=cand, scalar=e1h[:, jp:jp + 1], in1=xh[:], op0=ALU.mult, op1=ALU.add,
                     reads=ck + [K("e1h"), K("xh")], writes=[K("xh")])
        P.dma("sp", xg[:], io["x"][g * 512:(g + 1) * 512, :].rearrange("(t p) d -> p t d", p=128), writes=[K("xg")])
        for tl in range(-1, 4):
            npart = 2 if tl < 0 else 128
            src = xh[:, :] if tl < 0 else xg[:, tl, :]
            ksrc = K("xh") if tl < 0 else K("xg")
            b = (tl + 1) % 2
            P.op("act", "activation", hb[b][0:npart, :], src, AF.Square, accum_out=st[0:npart, 0:1], reads=[ksrc], writes=[K("h", b), K("st", 0)])
            P.op("act", "activation", st[0:npart, 1:2], st[0:npart, 0:1], AF.Sqrt, bias=eps[0:npart, :], scale=1.0 / 1024, reads=[K("st", 0), "c_eps"], writes=[K("st", 1)])
            P.op("dve", "reciprocal", st[0:npart, 2:3], st[0:npart, 1:2], reads=[K("st", 1)], writes=[K("st", 2)])
            h = hb[b]
            P.op("dve", "scalar_tensor_tensor", out=h[0:npart, :], in0=src, scalar=st[0:npart, 2:3], in1=lnb[0:npart, :], op0=ALU.mult, op1=ALU.mult,
                 reads=[ksrc, K("st", 2), K("lnb")], writes=[K("h", b)])
            if tl < 0:
                for kc in range(8):
                    P.op("pe", "transpose", ps_t[:, kc, 0:2], h[0:2, kc * 128:(kc + 1) * 128], ident[0:2, 0:2], reads=[K("h", b), "c_ident"], writes=[K("pst")])
                P.op("act", "copy", hT[:, :, 0:2], ps_t[:, :, 0:2], writes=[K("pst"), K("hT", -1)])
            else:
                for kc in range(8):
                    P.op("pe", "transpose", ps_t[:, kc, :], h[:, kc * 128:(kc + 1) * 128], ident[:], reads=[K("h", b), "c_ident"], writes=[K("pst")])
                P.op("act", "copy", hT[:, :, 2 + tl * 128:2 + (tl + 1) * 128], ps_t[:, :, :], writes=[K("pst"), K("hT", tl)])
        hTkeys = [K("hT", tl) for tl in range(-1, 4)]

        def up_chunk(fc, n):
            pu = ps_u[n % 3]
            kpu = K("psu", n % 3)
            ph = ps_h[n % 2]
            kph = K("psh", n % 2)
            for kc in range(8):
                P.op("pe", "matmul", pu[:, :], lhsT=wup[:, kc, fc * 128:(fc + 1) * 128], rhs=hT[:, kc, 2:514], start=(kc == 0), stop=(kc == 7),
                     reads=hTkeys + wkeys(pfx + "wup", fc * 128, (fc + 1) * 128), writes=[kpu])
            for kc in range(8):
                P.op("pe", "matmul", ph[:, 0:2], lhsT=wup[:, kc, fc * 128:(fc + 1) * 128], rhs=hT[:, kc, 0:2], start=(kc == 0), stop=(kc == 7),
                     reads=hTkeys + wkeys(pfx + "wup", fc * 128, (fc + 1) * 128), writes=[kph])
            u = U[n % 2]
            ku = K("U", n % 2)
            c = cv[n % 3]
            kc_ = K("cv", n % 3)
            P.op("act", "copy", u[:, 2:514], pu[:, :], writes=[kpu, (ku, 1)])
            P.op("act", "activation", c[:], pu[:, :], AF.Identity, bias=cw[:, 3, fc:fc + 1], scale=cw[:, 2, fc:fc + 1], reads=[K("cw")], writes=[kpu, kc_])
            P.op("act", "copy", u[:, 0:2], ph[:, 0:2], writes=[kph, (ku, 0)])
            P.op("dve", "scalar_tensor_tensor", out=c[:], in0=u[:, 1:513], scalar=cw[:, 1, fc:fc + 1], in1=c[:], op0=ALU.mult, op1=ALU.add, reads=[(ku, 0), (ku, 1), K("cw"), kc_], writes=[kc_])
            P.op("dve", "scalar_tensor_tensor", out=c[:], in0=u[:, 0:512], scalar=cw[:, 0, fc:fc + 1], in1=c[:], op0=ALU.mult, op1=ALU.add, reads=[(ku, 0), (ku, 1), K("cw"), kc_], writes=[kc_])
            return c, kc_

        n = 0
        for fc in range(NGC):
            cg, kcg = up_chunk(fc, n)
            n += 1
            cvl, kcv = up_chunk(fc + NGC, n)
            n += 1
            s = sg[fc % 2]
            ks = K("sg", fc % 2)
            P.op("act", "activation", s[:], cg[:], AF.Silu, reads=[kcg], writes=[ks])
            P.op("dve", "tensor_tensor", out=actT[:, fc, :], in0=s[:], in1=cvl[:], op=ALU.mult, reads=[ks, kcv], writes=[K("actT", fc)])
        akeys = [K("actT", fc) for fc in range(NGC)]

        for tl in range(4):
            t = g * 4 + tl
            for nh in range(2):
                pd = ps_d[(tl * 2 + nh) % 2]
                kpd = K("psd", (tl * 2 + nh) % 2)
                for fc in range(NGC):
                    P.op("pe", "matmul", pd[:, :], lhsT=actT[:, fc, tl * 128:(tl + 1) * 128], rhs=wdn[:, fc, nh * 512:(nh + 1) * 512], start=(fc == 0), stop=(fc == NGC - 1),
                         reads=akeys + wdn_keys, writes=[kpd])
                P.op("dve", "tensor_tensor", out=xg[:, tl, nh * 512:(nh + 1) * 512], in0=xg[:, tl, nh * 512:(nh + 1) * 512], in1=pd[:, :], op=ALU.add,
                     reads=[K("xg")], writes=[kpd, K("xo", (tl, nh))])
            P.dma("act", io["out"][t * 128:(t + 1) * 128, :], xg[:, tl, :], reads=[K("xo", (tl, 0)), K("xo", (tl, 1)), K("xg")], writes=[("dram_out", t)])


def emit_cumsum(P, io):
    pfx = "cs_"

    def K(n, i=None):
        return (pfx + n) if i is None else (pfx + n, i)
    lfb = P.sb(pfx + "lfb", [16, 8192], F32)
    for j in range(4):
        P.dma("sp", lfb[:, :].rearrange("h (i j q) -> h j i q", i=4, j=4, q=512)[:, j, :, :],
              io["logf_g"][:, j, :].rearrange("h (i q) -> h i q", q=512), reads=io.get("gk_lf", []), writes=[K("lfb")])
    lfo = P.sb(pfx + "lfo", [16, 2048], F32)
    P.dma("sp", lfo[:], io["logf_own"], writes=[K("lfo")])
    e1h = P.sb(pfx + "e1h", [16, 4], F32)
    P.dma("sp", e1h[:], io["e1h"][0:16, :], writes=[K("e1h")])
    ones = P.sb(pfx + "ones", [16, 512], F32)
    P.op("dve", "memset", ones[:], 1.0, writes=[K("ones")])
    c = P.sb(pfx + "c", [16, 8192], F32)
    cq = P.sb(pfx + "cq", [16, 2048], F32)
    Pc = P.sb(pfx + "Pc", [16, 17], F32)
    P.op("dve", "memset", Pc[:, 0:1], 0.0, writes=[K("Pc", 0)])
    for g in range(16):
        P.op("dve", "tensor_tensor_scan", c[:, g * 512:(g + 1) * 512], ones[:], lfb[:, g * 512:(g + 1) * 512], Pc[:, g:g + 1], op0=ALU.mult, op1=ALU.add,
             reads=[K("ones"), K("lfb"), K("Pc", g)], writes=[K("c", g)])
        P.op("dve", "tensor_copy", Pc[:, g + 1:g + 2], c[:, (g + 1) * 512 - 1:(g + 1) * 512], reads=[K("c", g)], writes=[K("Pc", g + 1)])
    ini = P.sb(pfx + "ini", [16, 4], F32)
    for i in range(4):
        P.op("dve", "tensor_scalar", ini[:, i:i + 1], Pc[:, 4 * i:4 * i + 1], e1h[:, 0:1], None, op0=ALU.mult, reads=[K("Pc", 4 * i), K("e1h")], writes=[K("ini", i)])
        for j in range(1, 4):
            P.op("dve", "scalar_tensor_tensor", out=ini[:, i:i + 1], in0=Pc[:, 4 * i + j:4 * i + j + 1], scalar=e1h[:, j:j + 1], in1=ini[:, i:i + 1], op0=ALU.mult, op1=ALU.add,
                 reads=[K("Pc", 4 * i + j), K("e1h"), K("ini", i)], writes=[K("ini", i)])
        P.op("dve", "tensor_tensor_scan", cq[:, i * 512:(i + 1) * 512], ones[:], lfo[:, i * 512:(i + 1) * 512], ini[:, i:i + 1], op0=ALU.mult, op1=ALU.add,
             reads=[K("ones"), K("lfo"), K("ini", i)], writes=[K("cq", i)])
    c8 = P.sb(pfx + "c8", [16, 2048], F32)
    r = P.sb(pfx + "r", [16, 2048], F32)
    o = [P.sb(pfx + "o%d" % i, [16, 6, 2048], BF16) for i in range(2)]
    for n in range(5):
        b = n % 2
        src = cq[:, :] if n == 4 else c[:, n * 2048:(n + 1) * 2048]
        ksrc = [K("cq", i) for i in range(4)] if n == 4 else [K("c", g) for g in range(4 * n, 4 * n + 4)]
        ot = o[b]
        ko = K("o", b)
        P.op("dve", "tensor_scalar", c8[:], src, 8.0, None, op0=ALU.mult, reads=ksrc, writes=[K("c8")])
        P.op("dve", "tensor_copy", ot[:, 0, :], c8[:], reads=[K("c8")], writes=[(ko, 0)])
        P.op("dve", "tensor_tensor", out=r[:], in0=c8[:], in1=ot[:, 0, :], op=ALU.subtract, reads=[K("c8"), (ko, 0)], writes=[K("r")])
        P.op("dve", "tensor_copy", ot[:, 1, :], r[:], reads=[K("r")], writes=[(ko, 1)])
        P.op("dve", "tensor_tensor", out=r[:], in0=r[:], in1=ot[:, 1, :], op=ALU.subtract, reads=[K("r"), (ko, 1)], writes=[K("r")])
        P.op("dve", "tensor_copy", ot[:, 2, :], r[:], reads=[K("r")], writes=[(ko, 2)])
        hk = [(ko, 0), (ko, 1), (ko, 2)]
        if n == 4:
            P.op("dve", "memset", ot[:, 3:6, :], 1.0, writes=[(ko, 3)])
            P.dma("sp", io["caug_q"].rearrange("r h q -> h r q"), ot[:], reads=hk + [(ko, 3)], writes=["dram_cq"] + hk + [(ko, 3)])
        else:
            P.op("dve", "tensor_scalar", ot[:, 3:6, :], ot[:, 0:3, :], -1.0, None, op0=ALU.mult, reads=hk, writes=[(ko, 3)])
            P.op("dve", "memset", ot[:, 0:3, :], 1.0, reads=[(ko, 3)], writes=hk)
            P.dma("sp", io["caug_k"][:, :, n * 2048:(n + 1) * 2048], ot[:], reads=hk + [(ko, 3)], writes=[("dram_ck", n)] + hk + [(ko, 3)])


S_ = 8192
NCORES = 8
GROUPS = [[0, 1, 2, 3], [4, 5, 6, 7]]


def _toks(j, NG=4):
    return np.concatenate([np.arange(512) + 512 * (4 * i + j) for i in range(NG)])


def build_fused():
    nc = bass.Bass("TRN2", target_bir_lowering=False)
    NTOK = 2048

    def din(name, shape, dt=F32):
        return nc.dram_tensor(name, list(shape), dt, kind="ExternalInput").ap()

    def scr(name, shape, dt):
        return nc.dram_tensor(name, list(shape), dt)

    E = {}
    for name, shape in [("x", [NTOK, 1024]), ("pos", [128, 16]), ("qchk_row", [NTOK]), ("qpos_row", [NTOK]), ("qchk_col", [128, 16]), ("e1h", [16, 4]),
                        ("ln_mix", [2, 1024]), ("ln_ffn", [2, 1024]), ("ev_w_in", [1024, 3368]), ("ev_w_out", [1024, 1024]),
                        ("ev_a_qnorm", [64]), ("ev_a_knorm", [64]), ("ev_idx_knorm", [32]), ("ev_b_qnorm", [64]), ("ev_b_knorm", [64]),
                        ("ev_lam_q1", [64]), ("ev_lam_k1", [64]), ("ev_lam_q2", [64]), ("ev_lam_k2", [64]), ("ev_b_subln", [128]),
                        ("od_w_in", [1024, 3088]), ("od_b_f", [16]), ("od_w_out", [1024, 1024]), ("od_c_qnorm", [64]), ("od_c_knorm", [64]),
                        ("ffn_up", [2, 1024, 5632]), ("cw", [2, 128, 4, 44]), ("ffn_down", [2, 2816, 1024])]:
        E[name] = din(name, shape)
    out = nc.dram_tensor("out", [NTOK, 1024], F32, kind="ExternalOutput").ap()

    T = {}
    specs = [("qT0", [1024, NTOK], BF16), ("iqT", [768, NTOK], BF16), ("iw", [NTOK, 8], F32),
             ("x1", [NTOK, 1024], F32), ("tl1", [8, 1024], F32), ("tl1_g", [32, 1024], F32), ("x2", [NTOK, 1024], F32),
             ("qT1", [1024, NTOK], BF16), ("lf", [16, NTOK], F32), ("lf_g", [64, NTOK], F32),
             ("cak", [96, S_], BF16), ("caq", [96, NTOK], BF16),
             ("x3", [NTOK, 1024], F32), ("tl3", [8, 1024], F32), ("tl3_g", [32, 1024], F32)]
    for g in range(4):
        specs += [("kT0_%d" % g, [1024, 512], BF16), ("kT0_%d_g" % g, [4096, 512], BF16), ("kT1_%d" % g, [1024, 512], BF16), ("kT1_%d_g" % g, [4096, 512], BF16),
                  ("vc_%d" % g, [2048, 256], BF16), ("vc_%d_g" % g, [8192, 256], BF16),
                  ("va_%d" % g, [1024, 256], BF16), ("va_%d_g" % g, [4096, 256], BF16), ("vb_%d" % g, [512, 512], BF16), ("vb_%d_g" % g, [2048, 512], BF16),
                  ("ikT_%d" % g, [96, 512], BF16), ("ikT_%d_g" % g, [384, 512], BF16)]
    for name, shape, dt in specs:
        T[name] = scr("s_" + name, shape, dt)

    def A(name):
        return T[name].ap()

    with ExitStack() as es:
        P = Prog(nc, es)
        C = build_consts(P)
        with P.phase():
            io = {"x": E["x"], "pos": E["pos"], "ln": E["ln_mix"][0], "w_in": E["ev_w_in"],
                  "gains": [E["ev_a_qnorm"], E["ev_a_knorm"], E["ev_b_qnorm"], E["ev_b_knorm"]], "g_ik": E["ev_idx_knorm"],
                  "qT": A("qT0").rearrange("(d h) q -> d h q", h=16), "kT_p": [A("kT0_%d" % g).rearrange("(h d) q -> d h q", h=16) for g in range(4)],
                  "iqT": A("iqT").rearrange("(d h) q -> d h q", h=8), "ikT_p": [A("ikT_%d" % g) for g in range(4)], "iw": A("iw"),
                  "va_p": [A("va_%d" % g).rearrange("(h p) (t d) -> p h t d", p=128, d=64) for g in range(4)],
                  "vb_p": [A("vb_%d" % g).rearrange("(h p) (t d) -> p h t d", p=128, d=128) for g in range(4)]}

            def on_group0(g, keys):
                for nm in ["ikT_%d" % g, "kT0_%d" % g, "va_%d" % g, "vb_%d" % g]:
                    P.coll(T[nm], T[nm + "_g"], GROUPS, reads=keys, writes=[("g", nm)])
            io["on_group"] = on_group0
            emit_proj(P, C, 0, io, 16)
        gk0 = {}
        for g in range(4):
            gk0[("ikT", g)] = [("g", "ikT_%d" % g)]
            gk0[("kT", g)] = [("g", "kT0_%d" % g)]
            gk0[("va_g", g)] = [("g", "va_%d" % g)]
            gk0[("vb_g", g)] = [("g", "vb_%d" % g)]
        with P.phase():
            io = {"qT": A("qT0").rearrange("(d h) q -> d h q", h=16), "kT_g": [A("kT0_%d_g" % g).rearrange("(r h d) q -> d r h q", r=4, h=16) for g in range(4)],
                  "ikT_g": [A("ikT_%d_g" % g).rearrange("(j d) q -> d j q", j=4) for g in range(4)], "iqT": A("iqT").rearrange("(d h) q -> d h q", h=8),
                  "iw": A("iw").rearrange("(t p) h -> p t h", p=128), "qkey_row": E["qchk_row"], "qchk_col": E["qchk_col"],
                  "va_g": [A("va_%d_g" % g).rearrange("(r h p) (t d) -> h p r t d", r=4, p=128, d=64) for g in range(4)],
                  "vb_g": [A("vb_%d_g" % g).rearrange("(r h p) (t d) -> h p r t d", r=4, p=128, d=128) for g in range(4)],
                  "lq1": E["ev_lam_q1"], "lk1": E["ev_lam_k1"], "lq2": E["ev_lam_q2"], "lk2": E["ev_lam_k2"], "subln": E["ev_b_subln"],
                  "x": E["x"], "w_out": E["ev_w_out"], "out": A("x1"), "tails": A("tl1"), "gk": gk0}
            emit_attn(P, C, 0, io, NG=4)
        P.coll(T["tl1"], T["tl1_g"], GROUPS, writes=[("g", "tl1")])
        with P.phase():
            io = {"x": A("x1"), "tails_g": A("tl1_g").rearrange("(r g) d -> r g d", r=4), "e1h": E["e1h"], "ln": E["ln_ffn"][0], "w_up": E["ffn_up"][0],
                  "cw": E["cw"][0], "w_down": E["ffn_down"][0], "out": A("x2"), "gk_tails": [("g", "tl1")]}
            emit_ffn(P, C, io, NG=4, pfx="f0_")
        with P.phase():
            io = {"x": A("x2"), "pos": E["pos"], "ln": E["ln_mix"][1], "w_in": E["od_w_in"], "gains": [E["od_c_qnorm"], E["od_c_knorm"]], "b_f": E["od_b_f"],
                  "qT": A("qT1").rearrange("(d h) q -> d h q", h=16), "kT_p": [A("kT1_%d" % g).rearrange("(h d) q -> d h q", h=16) for g in range(4)],
                  "vc_p": [A("vc_%d" % g).rearrange("(h p) (t d) -> p h t d", p=128, d=64) for g in range(4)], "logfT": A("lf")}

            def on_group1(g, keys):
                for nm in ["kT1_%d" % g, "vc_%d" % g]:
                    P.coll(T[nm], T[nm + "_g"], GROUPS, reads=keys, writes=[("g", nm)])
            io["on_group"] = on_group1
            io["on_lf"] = lambda keys: P.coll(T["lf"], T["lf_g"], GROUPS, reads=keys, writes=[("g", "lf")])
            emit_proj(P, C, 1, io, 16)
        gk1 = {}
        for g in range(4):
            gk1[("kT", g)] = [("g", "kT1_%d" % g)]
            gk1[("vc_g", g)] = [("g", "vc_%d" % g)]
        with P.phase():
            io = {"logf_g": A("lf_g").rearrange("(j h) q -> h j q", j=4), "logf_own": A("lf"), "e1h": E["e1h"],
                  "caug_k": A("cak").rearrange("(h r) q -> h r q", r=6), "caug_q": A("caq").rearrange("(r h) q -> r h q", h=16), "gk_lf": [("g", "lf")]}
            emit_cumsum(P, io)
        with P.phase():
            io = {"qT": A("qT1").rearrange("(d h) q -> d h q", h=16), "kT_g": [A("kT1_%d_g" % g).rearrange("(r h d) q -> d r h q", r=4, h=16) for g in range(4)],
                  "qkey_row": E["qpos_row"], "vc_g": [A("vc_%d_g" % g).rearrange("(r h p) (t d) -> h p r t d", r=4, p=128, d=64) for g in range(4)],
                  "caug_k": A("cak").rearrange("(h r) q -> h r q", r=6), "caug_q": A("caq").rearrange("(r h) q -> r h q", h=16),
                  "x": A("x2"), "w_out": E["od_w_out"], "out": A("x3"), "tails": A("tl3"), "gk": gk1}
            emit_attn(P, C, 1, io, NG=4)
        P.coll(T["tl3"], T["tl3_g"], GROUPS, writes=[("g", "tl3")])
        with P.phase():
            io = {"x": A("x3"), "tails_g": A("tl3_g").rearrange("(r g) d -> r g d", r=4), "e1h": E["e1h"], "ln": E["ln_ffn"][1], "w_up": E["ffn_up"][1],
                  "cw": E["cw"][1], "w_down": E["ffn_down"][1], "out": out, "gk_tails": [("g", "tl3")]}
            emit_ffn(P, C, io, NG=4, pfx="f1_")
        P.finish()
        P.emit()
    return nc


def kernel(**inp):
    inp = {k: np.asarray(v) for k, v in inp.items()}
    x = np.ascontiguousarray(inp["x"], dtype=np.float32)
    nc = build_fused()
    cw = np.stack([np.concatenate([inp["ffn_conv"][l], inp["ffn_conv_b"][l][None]], 0).reshape(4, 44, 128).transpose(2, 0, 1) for l in range(2)], 0)
    shared = {"ln_mix": inp["ln_mix"], "ln_ffn": inp["ln_ffn"], "ev_w_in": inp["ev_w_in"][0], "ev_w_out": inp["ev_w_out"][0],
              "od_w_in": inp["od_w_in"][0], "od_b_f": inp["od_b_f"][0], "od_w_out": inp["od_w_out"][0],
              "ffn_up": inp["ffn_up"], "cw": np.ascontiguousarray(cw, dtype=np.float32), "ffn_down": inp["ffn_down"]}
    for k in ["ev_a_qnorm", "ev_a_knorm", "ev_idx_knorm", "ev_b_qnorm", "ev_b_knorm", "ev_lam_q1", "ev_lam_k1", "ev_lam_q2", "ev_lam_k2", "ev_b_subln",
              "od_c_qnorm", "od_c_knorm"]:
        shared[k] = inp[k][0]
    shared = {k: np.ascontiguousarray(v, dtype=np.float32) for k, v in shared.items()}
    maps = []
    for c in range(NCORES):
        b, j = c // 4, c % 4
        toks = _toks(j)
        m = dict(shared)
        e1h = np.zeros((16, 4), np.float32)
        e1h[:, j] = 1.0
        m.update(x=np.ascontiguousarray(x[b, toks]), pos=np.ascontiguousarray(toks.astype(np.float32).reshape(16, 128).T),
                 qchk_row=(toks // 64).astype(np.float32), qpos_row=toks.astype(np.float32),
                 qchk_col=np.ascontiguousarray((toks // 64).astype(np.float32).reshape(16, 128).T), e1h=e1h)
        maps.append(m)
    res = run_bass_kernel_spmd(nc, maps, core_ids=list(range(NCORES)))
    out = np.empty_like(x)
    for c in range(NCORES):
        out[c // 4, _toks(c % 4)] = np.asarray(res.results[c]["out"])
    return out.astype(np.float32)
```
